# Optimizing a Trainium2 kernel written in Bass

```python
import jax, jax.numpy as jnp
from jax import lax
import numpy as np

D_MODEL = 1024
BATCH = 1
SEQ = 16384
DEPTH = 1
DEC_BATCH = 32
DEC_SEQ = 4
PAST_LEN = 16384
PAGE_SIZE = 128

N_HEADS = 8
HEAD_DIM = 64
ATTN_WIDTH = N_HEADS * HEAD_DIM
GMLP_GROUPS = 8
GMLP_WIDTH = 512
GMLP_GROUP_DIM = GMLP_WIDTH // GMLP_GROUPS
CHUNK = 128
DILATIONS = ((128, 1), (512, 4), (2048, 16))
WINDOW_MAX = 2048
BLOCK = 128
ROT_DIM = HEAD_DIM // 4
ROPE_THETA = 500000.0
D_FF = 4 * D_MODEL
PLE_DIM = 256
IN_WIDTH = 3 * ATTN_WIDTH + 2 * GMLP_WIDTH
EPS = 1e-6
NEG = -1e30

kernel_name = "hymba_dilated_gmlp_decoder_step"


def rmsnorm(x, g):
    xf = x.astype(jnp.float32)
    y = xf * lax.rsqrt(jnp.mean(xf * xf, axis=-1, keepdims=True) + EPS)
    return (y * g.astype(jnp.float32)).astype(x.dtype)


def layernorm(x, g, b):
    xf = x.astype(jnp.float32)
    mu = jnp.mean(xf, axis=-1, keepdims=True)
    var = jnp.mean(jnp.square(xf - mu), axis=-1, keepdims=True)
    y = (xf - mu) * lax.rsqrt(var + EPS)
    return (y * g.astype(jnp.float32) + b.astype(jnp.float32)).astype(x.dtype)


def rope_partial(x, pos):
    half = ROT_DIM // 2
    inv_freq = ROPE_THETA ** (-jnp.arange(0, ROT_DIM, 2, dtype=jnp.float32) / ROT_DIM)
    ang = pos.astype(jnp.float32)[:, None] * inv_freq[None, :]
    cos = jnp.cos(ang)[:, None, :].astype(x.dtype)
    sin = jnp.sin(ang)[:, None, :].astype(x.dtype)
    x1, x2, rest = x[..., :half], x[..., half:ROT_DIM], x[..., ROT_DIM:]
    return jnp.concatenate([x1 * cos - x2 * sin, x2 * cos + x1 * sin, rest], axis=-1)


def project(h, pos, norm_g, w_in):
    B, T, _ = h.shape
    z = rmsnorm(h, norm_g) @ w_in
    q, k, v, u, vc = jnp.split(z, np.cumsum([ATTN_WIDTH] * 3 + [GMLP_WIDTH]).tolist(), axis=-1)
    q = rope_partial(q.reshape(B, T, N_HEADS, HEAD_DIM), pos)
    k = rope_partial(k.reshape(B, T, N_HEADS, HEAD_DIM), pos)
    v = v.reshape(B, T, N_HEADS, HEAD_DIM)
    return q, k, v, jax.nn.gelu(u), jax.nn.gelu(vc)


def softmax_stats(s):
    m = jnp.max(s, axis=-1, keepdims=True)
    e = jnp.exp(s - m)
    den = jnp.sum(e, axis=-1, keepdims=True)
    return e / den, (m + jnp.log(den))[..., 0]


def dilated_attn_prompt(q, k, v, window, dil):
    B, S, H, D = q.shape
    span = window // dil
    n = S // dil
    nb = -(-n // BLOCK)
    pad = nb * BLOCK - n

    def sub(a):
        return a.reshape(B, n, dil, H, D).transpose(0, 2, 1, 3, 4)

    qs = jnp.pad(sub(q), ((0, 0), (0, 0), (0, pad), (0, 0), (0, 0))).reshape(B, dil, nb, BLOCK, H, D)

    def kwin(a):
        a = jnp.pad(sub(a), ((0, 0), (0, 0), (BLOCK, pad), (0, 0), (0, 0))).reshape(B, dil, nb + 1, BLOCK, H, D)
        return jnp.concatenate([a[:, :, :-1], a[:, :, 1:]], axis=3)

    ks, vs = kwin(k), kwin(v)
    s = jnp.einsum('brnqhd,brnkhd->brnhqk', qs, ks, preferred_element_type=jnp.float32) * (HEAD_DIM ** -0.5)
    qi = jnp.arange(BLOCK)[:, None]
    kj = jnp.arange(2 * BLOCK)[None, :]
    diff = qi + BLOCK - kj
    band = (diff >= 0) & (diff <= span)
    not_before_start = (jnp.arange(nb)[:, None, None] > 0) | (kj[None] >= BLOCK)
    mask = band[None] & not_before_start
    s = jnp.where(mask[:, None], s, NEG)
    p, lse = softmax_stats(s)
    o = jnp.einsum('brnhqk,brnkhd->brnqhd', p, vs.astype(jnp.float32))
    o = o.reshape(B, dil, nb * BLOCK, H, D)[:, :, :n].transpose(0, 2, 1, 3, 4).reshape(B, S, H, D)
    lse = lse.transpose(0, 1, 2, 4, 3).reshape(B, dil, nb * BLOCK, H)[:, :, :n]
    lse = lse.transpose(0, 2, 1, 3).reshape(B, S, H)
    return o, lse


def dilated_attn_sample(q, k_all, v_all, window, dil, lbuf):
    T = q.shape[1]
    span = window // dil
    idx = lbuf + jnp.arange(T)[:, None] - dil * jnp.arange(span + 1)[None, :]
    valid = idx >= 0
    idx = jnp.maximum(idx, 0)
    kg = k_all[:, idx]
    vg = v_all[:, idx]
    s = jnp.einsum('bthd,btkhd->bthk', q, kg, preferred_element_type=jnp.float32) * (HEAD_DIM ** -0.5)
    s = jnp.where(valid[None, :, None, :], s, NEG)
    p, lse = softmax_stats(s)
    o = jnp.einsum('bthk,btkhd->bthd', p, vg.astype(jnp.float32))
    return o, lse


def combine_dilations(outs, lses):
    w = jax.nn.softmax(jnp.stack(lses, axis=0), axis=0)
    return jnp.sum(w[..., None] * jnp.stack(outs, axis=0), axis=0)


def gmlp_gate(u, vc, ln_g, ln_b, w_s, b_s):
    B, T, _ = u.shape
    vn = layernorm(vc, ln_g, ln_b)
    L = min(T, CHUNK)
    n = T // L
    tril = jnp.tril(jnp.ones((L, L), dtype=bool))
    w = jnp.where(tril[None], w_s[:, :L, :L], 0).astype(vn.dtype)
    vg = vn.reshape(B, n, L, GMLP_GROUPS, GMLP_GROUP_DIM)
    mixed = jnp.einsum('gij,bnjgc->bnigc', w, vg) + b_s[:, :L].T[:, :, None]
    return u * mixed.reshape(B, T, GMLP_WIDTH), vn


def finish(h, attn, gated, p, w_out, norm2_g, w_up, w_down, gate_norm_g, w_gate, w_ple):
    B, T, _ = h.shape
    mix = jnp.concatenate([attn.reshape(B, T, ATTN_WIDTH).astype(h.dtype), gated], axis=-1)
    h = h + mix @ w_out
    f = jnp.square(jax.nn.relu(rmsnorm(h, norm2_g) @ w_up)) @ w_down
    h = h + f
    gate = jax.nn.sigmoid(rmsnorm(h, gate_norm_g) @ w_gate)
    return h + gate * (p @ w_ple)


def setup_inputs(seed: int = 0) -> dict:
    key = jax.random.key(seed)
    ks = jax.random.split(key, 24)
    f32 = jnp.float32
    nrm = lambda k, shape, scale: jax.random.normal(k, shape, f32) * scale
    wbuf = min(WINDOW_MAX, PAST_LEN)
    return {
        "x_prompt": nrm(ks[0], (BATCH, SEQ, D_MODEL), 1.0),
        "x_sample": nrm(ks[1], (DEC_BATCH, DEC_SEQ, D_MODEL), 1.0),
        "cache_k": nrm(ks[2], (DEPTH, DEC_BATCH, wbuf, N_HEADS, HEAD_DIM), 1.0),
        "cache_v": nrm(ks[3], (DEPTH, DEC_BATCH, wbuf, N_HEADS, HEAD_DIM), 1.0),
        "p_prompt": nrm(ks[4], (DEPTH, BATCH, SEQ, PLE_DIM), 1.0),
        "p_sample": nrm(ks[5], (DEPTH, DEC_BATCH, DEC_SEQ, PLE_DIM), 1.0),
        "norm1_g": 1.0 + nrm(ks[6], (DEPTH, D_MODEL), 0.02),
        "w_in": nrm(ks[7], (DEPTH, D_MODEL, IN_WIDTH), D_MODEL ** -0.5),
        "ln_v_g": 1.0 + nrm(ks[8], (DEPTH, GMLP_WIDTH), 0.02),
        "ln_v_b": nrm(ks[9], (DEPTH, GMLP_WIDTH), 0.02),
        "w_spatial": nrm(ks[10], (DEPTH, GMLP_GROUPS, CHUNK, CHUNK), CHUNK ** -0.5),
        "b_spatial": nrm(ks[11], (DEPTH, GMLP_GROUPS, CHUNK), 0.02),
        "w_out": nrm(ks[12], (DEPTH, ATTN_WIDTH + GMLP_WIDTH, D_MODEL), (ATTN_WIDTH + GMLP_WIDTH) ** -0.5),
        "norm2_g": 1.0 + nrm(ks[13], (DEPTH, D_MODEL), 0.02),
        "w_up": nrm(ks[14], (DEPTH, D_MODEL, D_FF), D_MODEL ** -0.5),
        "w_down": nrm(ks[15], (DEPTH, D_FF, D_MODEL), D_FF ** -0.5),
        "gate_norm_g": 1.0 + nrm(ks[16], (DEPTH, D_MODEL), 0.02),
        "w_gate": nrm(ks[17], (DEPTH, D_MODEL, D_MODEL), D_MODEL ** -0.5),
        "w_ple": nrm(ks[18], (DEPTH, PLE_DIM, D_MODEL), PLE_DIM ** -0.5),
        "final_g": 1.0 + nrm(ks[19], (D_MODEL,), 0.02),
    }


def reference(x_prompt, x_sample, cache_k, cache_v, p_prompt, p_sample,
              norm1_g, w_in, ln_v_g, ln_v_b, w_spatial, b_spatial, w_out,
              norm2_g, w_up, w_down, gate_norm_g, w_gate, w_ple, final_g):
    S = x_prompt.shape[1]
    T = x_sample.shape[1]
    lbuf = cache_k.shape[2]
    past = PAST_LEN
    pos_prompt = jnp.arange(S, dtype=jnp.float32)
    pos_sample = past + jnp.arange(T, dtype=jnp.float32)
    keep = min(WINDOW_MAX, S)
    hp, hs = x_prompt, x_sample
    nk_p, nv_p, nk_s, nv_s, nvc_s = [], [], [], [], []
    for i in range(DEPTH):
        q, k, v, u, vc = project(hp, pos_prompt, norm1_g[i], w_in[i])
        outs, lses = zip(*[dilated_attn_prompt(q, k, v, w, d) for (w, d) in DILATIONS])
        attn = combine_dilations(outs, lses)
        gated, _ = gmlp_gate(u, vc, ln_v_g[i], ln_v_b[i], w_spatial[i], b_spatial[i])
        hp = finish(hp, attn, gated, p_prompt[i], w_out[i], norm2_g[i], w_up[i], w_down[i],
                    gate_norm_g[i], w_gate[i], w_ple[i])
        nk_p.append(k[:, S - keep:])
        nv_p.append(v[:, S - keep:])
        q, k, v, u, vc = project(hs, pos_sample, norm1_g[i], w_in[i])
        k_all = jnp.concatenate([cache_k[i].astype(k.dtype), k], axis=1)
        v_all = jnp.concatenate([cache_v[i].astype(v.dtype), v], axis=1)
        outs, lses = zip(*[dilated_attn_sample(q, k_all, v_all, w, d, lbuf) for (w, d) in DILATIONS])
        attn = combine_dilations(outs, lses)
        gated, vn = gmlp_gate(u, vc, ln_v_g[i], ln_v_b[i], w_spatial[i], b_spatial[i])
        hs = finish(hs, attn, gated, p_sample[i], w_out[i], norm2_g[i], w_up[i], w_down[i],
                    gate_norm_g[i], w_gate[i], w_ple[i])
        nk_s.append(k)
        nv_s.append(v)
        nvc_s.append(vn)
    y_prompt = rmsnorm(hp, final_g)
    y_sample = rmsnorm(hs, final_g)
    return (y_prompt, y_sample, jnp.stack(nk_p), jnp.stack(nv_p),
            jnp.stack(nk_s), jnp.stack(nv_s), jnp.stack(nvc_s))
```

```python
import numpy as np
from contextlib import ExitStack
import concourse.bass as bass
import concourse.mybir as mybir
from concourse.bass_utils import run_bass_kernel_spmd

F32 = mybir.dt.float32
BF16 = mybir.dt.bfloat16
AF = mybir.ActivationFunctionType
ALU = mybir.AluOpType
AX = mybir.AxisListType

ENGS = ["pe", "act", "dve", "pool", "sp"]
NCORES = 8
D = 1024
SH = 2048
NT = 16
EPS = 1e-6
LN3 = float(np.log(3.0))


import os
KSTOP = int(os.environ.get("KSTOP", "99"))


class _Stop(Exception):
    pass


class Buf:
    __slots__ = ("w", "r")

    def __init__(self):
        self.w = None
        self.r = []


class Sched:
    def __init__(self, nc, stack, n_dma_sems=24):
        self.nc = nc
        self.ops = {e: [] for e in ENGS}
        self.sem = {}
        self.cnt = {e: 0 for e in ENGS}
        for e in ENGS:
            self.sem[e] = stack.enter_context(nc.semaphore("s_" + e))
        self.dsems = {}
        self.dpos = {}
        for q, n in (("sp", n_dma_sems), ("act", 8), ("dve", 8), ("pool", 8)):
            self.dsems[q] = [[stack.enter_context(nc.semaphore(f"d_{q}{i}")), 0] for i in range(n)]
            self.dpos[q] = 0
        self.known = {}
        self.cap = None

    def capture(self, fn, *args):
        lst = []
        self.cap = lst
        fn(*args)
        self.cap = None
        return lst

    def replay(self, lists):
        lists = [l for l in lists if l]
        pos = [0] * len(lists)
        while True:
            best, bf = -1, 2.0
            for i, l in enumerate(lists):
                if pos[i] < len(l):
                    f = pos[i] / len(l)
                    if f < bf:
                        best, bf = i, f
            if best < 0:
                break
            kind, a, kw = lists[best][pos[best]]
            pos[best] += 1
            if kind == "op":
                self.op(*a, **kw)
            else:
                self.dma(*a, **kw)

    def _wait(self, eng, tok):
        if tok is None:
            return
        key, semobj, val, prod = tok
        if prod == eng and eng == "pe":
            return
        kk = (eng, key)
        if self.known.get(kk, 0) >= val:
            return
        self.known[kk] = val
        self.ops[eng].append(lambda e, s=semobj, v=val: e.wait_ge(s, v))

    def _deps(self, eng, reads, writes):
        for b in reads:
            self._wait(eng, b.w)
        for b in writes:
            self._wait(eng, b.w)
            for t in b.r:
                self._wait(eng, t)

    def _commit(self, tok, reads, writes):
        for b in reads:
            b.r.append(tok)
            if len(b.r) > 24:
                b.r = b.r[-24:]
        for b in writes:
            b.w = tok
            b.r = []

    def op(self, eng, fn, reads=(), writes=(), sig=True):
        if self.cap is not None:
            self.cap.append(("op", (eng, fn), dict(reads=reads, writes=writes, sig=sig)))
            return None
        self._deps(eng, reads, writes)
        if sig:
            self.cnt[eng] += 1
            v = self.cnt[eng]
            s = self.sem[eng]
            self.ops[eng].append(lambda e, f=fn, s=s: f(e).then_inc(s, 1))
            tok = (eng, s, v, eng)
            self._commit(tok, reads, writes)
            return tok
        self.ops[eng].append(lambda e, f=fn: f(e))
        return None

    def dma(self, out, in_, reads=(), writes=(), q="sp"):
        if self.cap is not None:
            self.cap.append(("dma", (out, in_), dict(reads=reads, writes=writes, q=q)))
            return None
        self._deps(q, reads, writes)
        pool = self.dsems[q]
        i = self.dpos[q]
        self.dpos[q] = (i + 1) % len(pool)
        ent = pool[i]
        key = f"d_{q}{i}"
        if ent[1] > 0:
            self._wait(q, (key, ent[0], ent[1], None))
        ent[1] += 16
        s, v = ent[0], ent[1]
        self.ops[q].append(lambda e, o=out, i_=in_, s=s: e.dma_start(out=o, in_=i_).then_inc(s, 16))
        tok = (key, s, v, None)
        self._commit(tok, reads, writes)
        return tok

    def barrier(self):
        toks = []
        for e in ENGS:
            if self.cnt[e] > 0:
                toks.append((e, self.sem[e], self.cnt[e], e))
        for q, pool in self.dsems.items():
            for i, ent in enumerate(pool):
                if ent[1] > 0:
                    toks.append((f"d_{q}{i}", ent[0], ent[1], None))
        for e in ENGS:
            for t in toks:
                if t[3] == e:
                    continue
                self._wait(e, t)

    def emit(self):
        nc = self.nc
        with nc.Block() as block:
            @block.tensor
            def _(e):
                for f in self.ops["pe"]:
                    f(e)

            @block.scalar
            def _(e):
                for f in self.ops["act"]:
                    f(e)

            @block.vector
            def _(e):
                for f in self.ops["dve"]:
                    f(e)

            @block.gpsimd
            def _(e):
                for f in self.ops["pool"]:
                    f(e)

            @block.sync
            def _(e):
                for f in self.ops["sp"]:
                    f(e)


class Ring:
    def __init__(self, tiles):
        self.tiles = tiles
        self.bufs = [Buf() for _ in tiles]
        self.i = 0

    def next(self):
        j = self.i % len(self.tiles)
        self.i += 1
        return self.tiles[j], self.bufs[j]


def build_program():
    nc = bass.Bass("TRN2", target_bir_lowering=False)

    def din(name, shape, dt=F32):
        return nc.dram_tensor(name, list(shape), dt, kind="ExternalInput").ap()

    def dout(name, shape, dt=F32):
        return nc.dram_tensor(name, list(shape), dt, kind="ExternalOutput").ap()

    def dscr(name, shape, dt):
        return nc.dram_tensor(name, list(shape), dt).ap()

    xall = din("xall", [33, 128, D])
    pall = din("pall", [17, 128, 256])
    cs = din("cs", [33, 128, 16])
    ck = din("ck", [4, 2048, 512])
    cv = din("cv", [4, 2048, 512])
    w_in = din("w_in", [D, 2560])
    w_out = din("w_out", [D, D])
    w_up = din("w_up", [D, 4096])
    w_down = din("w_down", [4096, D])
    w_gate = din("w_gate", [D, D])
    w_ple = din("w_ple", [256, D])
    gains = din("gains", [128, 24])
    rowv = din("rowv", [3, 1024])
    wsT = din("wsT", [2, 128, 1024])
    trilm = din("trilm", [2, 128, 128])
    bsp = din("bsp", [2, 128, 8])
    ident_in = din("ident", [128, 128])
    amask = din("amask", [2, 128, 512])
    onesp = din("onesp", [128, 256])
    diagm = din("diagm", [8, 512])
    selc = din("selc", [8, 256])

    y_o = dout("y", [17, 128, D])
    nk_o = dout("nk", [17, 128, 512])
    nv_o = dout("nv", [17, 128, 512])
    nvc_o = dout("nvc", [128, 512])

    vsc = dscr("vsc", [4096, 1024], BF16)
    qsc = dscr("qsc", [16, 512], F32)
    wo_s = dscr("wo_s", [128, 8, 1024], BF16)
    wg_s = dscr("wg_s", [128, 8, 1024], BF16)
    wp_s = dscr("wp_s", [128, 2, 1024], BF16)
    wd_s = dscr("wd_s", [128, 32, 1024], BF16)
    wu_s = dscr("wu_s", [8, 128, 8, 512], BF16)

    with ExitStack() as st:
        S = Sched(nc, st)

        def sbt(stack, name, shape, dt):
            return stack.enter_context(nc.sbuf_tensor("sb_" + name, list(shape), dt))

        def pst(stack, name, shape, dt=F32):
            return stack.enter_context(nc.psum_tensor("ps_" + name, list(shape), dt))

        try:
            ident_f = sbt(st, "ident_f", [128, 128], F32)
            ident = sbt(st, "ident", [128, 128], BF16)
            gn = sbt(st, "gn", [128, 24], F32)
            mixT = sbt(st, "mixT", [128, 8, SH], BF16)
            mixTs = sbt(st, "mixTs", [128, 8, 128], BF16)
            b_c = Buf()
            S.dma(ident_f[:], ident_in, writes=[b_c])
            S.dma(gn[:], gains, writes=[b_c])
            S.op("dve", lambda e: e.tensor_copy(out=ident[:], in_=ident_f[:]), reads=[b_c], writes=[b_c])
            S.op("pool", lambda e: e.memset(mixTs[:], 0.0), writes=[b_c])

            with ExitStack() as sa:
                winb = sbt(sa, "winb", [128, 8, 2560], BF16)
                b_win = Buf()
                for kc in range(8):
                    S.dma(winb[:, kc, :], w_in[kc * 128:(kc + 1) * 128, :], writes=[b_win], q="pool")
                b_wsc = Buf()
                wq = []
                for j in range(4):
                    wq.append((wo_s[:, 2 * j:2 * j + 2, :], w_out[j * 256:(j + 1) * 256, :].rearrange("(c p) n -> p c n", p=128)))
                for kc in range(8):
                    for hf in range(2):
                        wq.append((wu_s[4 * hf:4 * hf + 4, :, kc, :].rearrange("u p n -> p u n"),
                                   w_up[kc * 128:(kc + 1) * 128, hf * 2048:(hf + 1) * 2048].rearrange("p (u n) -> p u n", u=4)))
                for j in range(16):
                    wq.append((wd_s[:, 2 * j:2 * j + 2, :], w_down[j * 256:(j + 1) * 256, :].rearrange("(c p) n -> p c n", p=128)))
                for j in range(4):
                    wq.append((wg_s[:, 2 * j:2 * j + 2, :], w_gate[j * 256:(j + 1) * 256, :].rearrange("(c p) n -> p c n", p=128)))
                wq.append((wp_s[:, :, :], w_ple.rearrange("(c p) n -> p c n", p=128)))

                def wq_issue(n):
                    for _ in range(n):
                        if wq:
                            o_, i_ = wq.pop(0)
                            S.dma(o_, i_, writes=[b_wsc], q="pool")

                kT = sbt(sa, "kT", [128, 4, 4096], BF16)
                qT = sbt(sa, "qT", [128, 4, SH], BF16)
                with ExitStack() as s1:
                    xr = Ring([sbt(s1, f"xt{i}", [128, D], F32) for i in range(3)])
                    csr = Ring([sbt(s1, f"cst{i}", [128, 16], F32) for i in range(6)])
                    xb_r = Ring([sbt(s1, f"xb{i}", [128, D], BF16) for i in range(2)])
                    xT_r = Ring([sbt(s1, f"xT{i}", [128, 8, 128], BF16) for i in range(2)])
                    st_r = Ring([sbt(s1, f"st1_{i}", [128, 8], F32) for i in range(4)])
                    st2 = sbt(s1, "st2", [128, 8], F32); b_st2 = Buf()
                    bnst = sbt(s1, "bnst", [128, 6], F32)
                    bnag = sbt(s1, "bnag", [128, 2], F32)
                    qkf_r = Ring([sbt(s1, f"qkf{i}", [128, 1024], F32) for i in range(2)])
                    vf_r = Ring([sbt(s1, f"vf{i}", [128, 512], F32) for i in range(2)])
                    vpad_r = Ring([sbt(s1, f"vpad{i}", [128, 8, 128], BF16) for i in range(1)])
                    rtmp = sbt(s1, "rtmp", [128, 4, 16, 8], F32); b_rtmp = Buf()
                    qkb = sbt(s1, "qkb", [128, 1024], BF16); b_qkb = Buf()
                    ub_r = Ring([sbt(s1, f"ub{i}", [128, 512], BF16) for i in range(2)])
                    vcf_r = Ring([sbt(s1, f"vcf{i}", [128, 512], F32) for i in range(2)])
                    vnf = sbt(s1, "vnf", [128, 512], F32); b_vnf = Buf()
                    vnb = sbt(s1, "vnb", [128, 512], BF16); b_vnb = Buf()
                    gtmp = None; b_gtmp = None
                    gb = sbt(s1, "gb", [128, 512], BF16); b_gb = Buf()
                    lng = sbt(s1, "lng", [128, 1024], F32)
                    wsf = qkf_r.tiles[0]; b_wsf = qkf_r.bufs[0]
                    wsb = sbt(s1, "wsb", [128, 2, 1024], BF16)
                    trm = sbt(s1, "trm", [128, 2, 128], F32)
                    bsb = sbt(s1, "bsb", [128, 2, 8], F32)
                    kt_r = Ring([sbt(s1, f"skt{i}", [128, 512], F32) for i in range(4)])
                    vt_r = Ring([sbt(s1, f"svt{i}", [128, 512], F32) for i in range(4)])
                    qbc_r = Ring([sbt(s1, f"qbc{i}", [128, 512], F32) for i in range(1)])
                    prod = sbt(s1, "prod", [128, 512], F32); b_prod = Buf(); gtmp = prod; b_gtmp = b_prod
                    ssc_r = Ring([sbt(s1, f"ssc{i}", [128, 32], F32) for i in range(2)])
                    pex_r = Ring([sbt(s1, f"pex{i}", [128, 32], F32) for i in range(2)])
                    onesc = sbt(s1, "onesc", [128, 1], F32)
                    rden = sbt(s1, "rden", [8, 1], F32); b_rden = Buf()
                    msk8 = sbt(s1, "msk8", [8, 512], F32); b_msk8 = Buf()
                    dgm = sbt(s1, "dgm", [8, 512], F32)
                    sel = sbt(s1, "sel", [8, 256], F32)
                    attb = sbt(s1, "attb", [128, 512], BF16); b_attb = Buf()
                    pz = Ring([pst(s1, f"pz{i}", [128, 512]) for i in range(2)])
                    pT = pst(s1, "pT", [128, 8, 128], BF16); b_pT = Buf()
                    pTx = pst(s1, "pTx", [128, 8, 128], BF16); b_pTx = Buf()
                    pmix = pst(s1, "pmix", [128, 512]); b_pmix = Buf()
                    pacc = pst(s1, "pacc", [8, 512]); b_pacc = Buf()
                    pden = pst(s1, "pden", [8, 512]); b_pden = Buf()
                    prow = pst(s1, "prow", [16, 512]); b_prow = Buf()

                    b_k1 = Buf()
                    S.dma(lng[:], rowv[1:2, :].partition_broadcast(128), writes=[b_k1])
                    S.dma(trm[:], trilm.rearrange("t p n -> p t n"), writes=[b_k1])
                    S.dma(bsb[:], bsp.rearrange("t p n -> p t n"), writes=[b_k1])
                    S.dma(dgm[:], diagm, writes=[b_k1])
                    S.dma(sel[:], selc, writes=[b_k1])
                    for t2 in range(2):
                        S.dma(wsf[:], wsT[t2], writes=[b_wsf])
                        S.op("dve", lambda e, t2=t2: e.tensor_tensor(
                            out=wsb[:, t2, :].rearrange("p (g i) -> p g i", g=8),
                            in0=wsf[:].rearrange("p (g i) -> p g i", g=8),
                            in1=trm[:, t2, :].unsqueeze(1).broadcast_to([128, 8, 128]), op=ALU.mult),
                            reads=[b_k1, b_wsf], writes=[b_k1])
                    S.op("pool", lambda e: e.memset(onesc[:], 1.0), writes=[b_k1])
                    for (sc_, bsc_) in zip(ssc_r.tiles, ssc_r.bufs):
                        S.op("pool", lambda e, sc_=sc_: e.memset(sc_[:], 0.0), writes=[bsc_])
                    S.op("pool", lambda e: e.memset(attb[:], 0.0), writes=[b_attb])
                    for (vp, bvp) in zip(vpad_r.tiles, vpad_r.bufs):
                        S.op("pool", lambda e, vp=vp: e.memset(vp[:], 0.0), writes=[bvp])

                    b_nk = Buf(); b_nv = Buf(); b_qsc = Buf(); b_vsc = Buf(); b_out = Buf()

                    order = [32] + list(range(32))
                    loads = {}

                    def issue_load(t):
                        xt, bx = xr.next()
                        ct, bct = csr.next()
                        S.dma(xt[:], xall[t], writes=[bx])
                        S.dma(ct[:], cs[t], writes=[bct])
                        loads[t] = (xt, bx, ct, bct)

                    ctxs = {}

                    f1ctx = {}

                    def f1a(t, part):
                        if part == "act":
                            xt, bx, ct, bct = loads.pop(t)
                            xb, b_xb = xb_r.next()
                            xT, b_xT = xT_r.next()
                            st1, b_st1 = st_r.next()
                            S.op("act", lambda e: e.activation(out=qkb[:], in_=xt[:], func=AF.Square, accum_out=st1[:, 0:1]),
                                 reads=[bx], writes=[b_qkb, b_st1])
                            S.op("act", lambda e: e.activation(out=st1[:, 1:2], in_=st1[:, 0:1], func=AF.Sqrt, scale=1.0 / D, bias=EPS),
                                 reads=[b_st1], writes=[b_st1])
                            f1ctx[t] = (xt, bx, ct, bct, xb, b_xb, xT, b_xT, st1, b_st1)
                        else:
                            st1, b_st1 = f1ctx[t][8], f1ctx[t][9]
                            S.op("dve", lambda e: e.reciprocal(out=st1[:, 2:3], in_=st1[:, 1:2]), reads=[b_st1], writes=[b_st1])

                    def f1b(t, part):
                        xt, bx, ct, bct, xb, b_xb, xT, b_xT, st1, b_st1 = f1ctx[t]
                        if part == "cast":
                            S.op("dve", lambda e: e.tensor_copy(out=xb[:], in_=xt[:]), reads=[bx], writes=[b_xb])
                            return
                        for c in range(8):
                            S.op("pe", lambda e, c=c: e.transpose(out=pTx[:, c, :], in_=xb[:, c * 128:(c + 1) * 128], identity=ident[:]),
                                 reads=[b_xb, b_c], writes=[b_pTx], sig=(c == 7))
                        for c in range(8):
                            S.op("dve", lambda e, c=c: e.tensor_scalar(out=xT[:, c, :], in0=pTx[:, c, :], scalar1=gn[:, c:c + 1], scalar2=None, op0=ALU.mult),
                                 reads=[b_pTx, b_c], writes=[b_xT], sig=(c == 7))

                    def f2(t):
                        xt, bx, ct, bct, xb, b_xb, xT, b_xT, st1, b_st1 = f1ctx.pop(t)
                        halo = t < 16
                        rstd = st1[:, 2:3]

                        def proj(col0):
                            pzt, bpz = pz.next()
                            for c in range(8):
                                S.op("pe", lambda e, c=c: e.matmul(pzt[:], lhsT=xT[:, c, :], rhs=winb[:, c, col0:col0 + 512],
                                                                   start=(c == 0), stop=(c == 7)),
                                     reads=[b_xT, b_win], writes=[bpz], sig=(c == 7))
                            return pzt, bpz

                        qkf, bqk = qkf_r.next()
                        vf, bvf = vf_r.next()
                        ub, b_ub = ub_r.next()
                        vcf, b_vcf = vcf_r.next()
                        if not halo:
                            pq, bpq = proj(0)
                            S.op("act", lambda e: e.activation(out=qkf[:, 0:512], in_=pq[:], func=AF.Copy, scale=rstd),
                                 reads=[bpq, b_st1], writes=[bqk])
                        pk, bpk = proj(512)
                        S.op("act", lambda e: e.activation(out=qkf[:, 512:1024], in_=pk[:], func=AF.Copy, scale=rstd),
                             reads=[bpk, b_st1], writes=[bqk])
                        pv, bpv = proj(1024)
                        S.op("act", lambda e: e.activation(out=vf[:], in_=pv[:], func=AF.Copy, scale=rstd),
                             reads=[bpv, b_st1], writes=[bvf])
                        if not halo:
                            pu, bpu = proj(1536)
                            S.op("act", lambda e: e.activation(out=ub[:], in_=pu[:], func=AF.Gelu, scale=rstd),
                                 reads=[bpu, b_st1], writes=[b_ub])
                            pc, bpc = proj(2048)
                            S.op("act", lambda e: e.activation(out=vcf[:], in_=pc[:], func=AF.Gelu, scale=rstd),
                                 reads=[bpc, b_st1], writes=[b_vcf])
                        ctxs[t] = (ct, bct, qkf, bqk, vf, bvf, ub, b_ub, vcf, b_vcf)

                    def lna(t, part):
                        if t < 16:
                            return
                        vcf, b_vcf = ctxs[t][8], ctxs[t][9]
                        if part == "stats":
                            S.op("dve", lambda e: e.bn_stats(out=bnst[:], in_=vcf[:]), reads=[b_vcf], writes=[b_st2])
                            S.op("dve", lambda e: e.bn_aggr(out=bnag[:], in_=bnst[:]), reads=[b_st2], writes=[b_st2])
                            S.op("act", lambda e: e.activation(out=st2[:, 3:4], in_=bnag[:, 1:2], func=AF.Sqrt, scale=1.0, bias=EPS),
                                 reads=[b_st2], writes=[b_st2])
                        else:
                            S.op("dve", lambda e: e.reciprocal(out=st2[:, 4:5], in_=st2[:, 3:4]), reads=[b_st2], writes=[b_st2])

                    bctx = {}

                    def tileA_back(t, part):
                        ct, bct, qkf, bqk, vf, bvf, ub, b_ub, vcf, b_vcf = ctxs[t]
                        halo = t < 16
                        samp = t == 32
                        h0 = 8 if halo else 0
                        nh = 16 - h0
                        if part == "a":
                            v3 = qkf[:].rearrange("p (h d) -> p h d", d=64)
                            x1 = v3[:, h0:16, 0:8]
                            x2 = v3[:, h0:16, 8:16]
                            cosb = ct[:, 0:8].unsqueeze(1).broadcast_to([128, nh, 8])
                            sinb = ct[:, 8:16].unsqueeze(1).broadcast_to([128, nh, 8])
                            tm = [rtmp[:, i, h0:16, :] for i in range(4)]
                            S.op("dve", lambda e: e.tensor_tensor(out=tm[0], in0=x1, in1=cosb, op=ALU.mult), reads=[bqk, bct], writes=[b_rtmp])
                            S.op("dve", lambda e: e.tensor_tensor(out=tm[1], in0=x2, in1=sinb, op=ALU.mult), reads=[bqk, bct], writes=[b_rtmp])
                            S.op("dve", lambda e: e.tensor_tensor(out=tm[2], in0=x2, in1=cosb, op=ALU.mult), reads=[bqk, bct], writes=[b_rtmp])
                            S.op("dve", lambda e: e.tensor_tensor(out=tm[3], in0=x1, in1=sinb, op=ALU.mult), reads=[bqk, bct], writes=[b_rtmp])
                            S.op("dve", lambda e: e.tensor_tensor(out=x1, in0=tm[0], in1=tm[1], op=ALU.subtract), reads=[b_rtmp], writes=[bqk])
                            S.op("dve", lambda e: e.tensor_tensor(out=x2, in0=tm[2], in1=tm[3], op=ALU.add), reads=[b_rtmp], writes=[bqk])
                            if not halo:
                                ot = t - 16
                                S.dma(nk_o[ot], qkf[:, 512:1024], reads=[bqk], writes=[b_nk])
                                S.dma(nv_o[ot], vf[:], reads=[bvf], writes=[b_nv])
                            if samp:
                                S.dma(qsc, qkf[0:16, 0:512], reads=[bqk], writes=[b_qsc])
                            else:
                                vp, bvp = vpad_r.next()
                                vp4 = vp[:].rearrange("p (c hh) n -> p c hh n", hh=2)
                                vf4 = vf[:].rearrange("p (c hh d) -> p c hh d", hh=2, d=64)
                                for hh in range(2):
                                    S.op("pool", lambda e, hh=hh: e.tensor_copy(out=vp4[:, :, hh, hh * 64:(hh + 1) * 64], in_=vf4[:, :, hh, :]),
                                         reads=[bvf], writes=[bvp])
                                S.dma(vsc[t * 128:(t + 1) * 128, :], vp[:].rearrange("p h n -> p (h n)"), reads=[bvp], writes=[b_vsc])
                                S.op("pool" if halo else "dve", lambda e: e.tensor_copy(out=qkb[:, h0 * 64:1024], in_=qkf[:, h0 * 64:1024]),
                                     reads=[bqk], writes=[b_qkb])
                            return
                        if part == "b":
                            if samp:
                                return
                            j0 = 4 if halo else 0
                            for j in range(j0, 8):
                                S.op("pe", lambda e, j=j: e.transpose(out=pT[:, j, :], in_=qkb[:, j * 128:(j + 1) * 128], identity=ident[:]),
                                     reads=[b_qkb, b_c], writes=[b_pT], sig=(j == 7))
                            S.op("dve", lambda e: e.tensor_copy(out=kT[:, :, t * 128:(t + 1) * 128], in_=pT[:, 4:8, :]), reads=[b_pT], writes=[b_out])
                            if not halo:
                                S.op("dve", lambda e: e.tensor_copy(out=qT[:, :, (t - 16) * 128:(t - 15) * 128], in_=pT[:, 0:4, :]),
                                     reads=[b_pT], writes=[b_out])
                            return
                        if halo:
                            return
                        ws_i = 1 if samp else 0
                        if part == "c":
                            S.op("dve", lambda e: e.tensor_scalar(out=vnf[:], in0=vcf[:], scalar1=bnag[:, 0:1], scalar2=st2[:, 4:5],
                                                                 op0=ALU.subtract, op1=ALU.mult), reads=[b_vcf, b_st2], writes=[b_vnf])
                            S.op("pool", lambda e: e.tensor_tensor(out=vnf[:], in0=vnf[:], in1=lng[:, 0:512], op=ALU.mult), reads=[b_vnf, b_k1], writes=[b_vnf])
                            if samp:
                                S.op("pool", lambda e: e.tensor_tensor(out=vnf[:], in0=vnf[:], in1=lng[:, 512:1024], op=ALU.add), reads=[b_vnf, b_k1], writes=[b_vnf])
                                S.op("pool", lambda e: e.tensor_copy(out=vnb[:], in_=vnf[:]), reads=[b_vnf], writes=[b_vnb])
                                S.dma(nvc_o, vnf[:], reads=[b_vnf], writes=[b_out])
                            else:
                                S.op("pool", lambda e: e.tensor_tensor(out=vnb[:], in0=vnf[:], in1=lng[:, 512:1024], op=ALU.add), reads=[b_vnf, b_k1], writes=[b_vnb])
                            return
                        if part == "d":
                            for g in range(8):
                                S.op("pe", lambda e, g=g: e.matmul(pmix[:, g * 64:(g + 1) * 64], lhsT=wsb[:, ws_i, g * 128:(g + 1) * 128],
                                                                   rhs=vnb[:, g * 64:(g + 1) * 64], start=True, stop=True),
                                     reads=[b_vnb, b_k1], writes=[b_pmix], sig=(g == 7))
                            S.op("dve", lambda e: e.tensor_tensor(out=gtmp[:].rearrange("p (g c) -> p g c", g=8),
                                                                 in0=pmix[:].rearrange("p (g c) -> p g c", g=8),
                                                                 in1=bsb[:, ws_i, :].unsqueeze(2).broadcast_to([128, 8, 64]), op=ALU.add),
                                 reads=[b_pmix, b_k1], writes=[b_gtmp])
                            S.op("dve", lambda e: e.tensor_tensor(out=gb[:], in0=gtmp[:], in1=ub[:], op=ALU.mult), reads=[b_gtmp, b_ub], writes=[b_gb])
                            return
                        if part == "e":
                            for j in range(4):
                                S.op("pe", lambda e, j=j: e.transpose(out=pT[:, j, :], in_=gb[:, j * 128:(j + 1) * 128], identity=ident[:]),
                                     reads=[b_gb, b_c], writes=[b_pT], sig=(j == 3))
                            if samp:
                                S.op("dve", lambda e: e.tensor_copy(out=mixTs[:, 4:8, :], in_=pT[:, 0:4, :]), reads=[b_pT], writes=[b_out])
                            else:
                                S.op("dve", lambda e: e.tensor_copy(out=mixT[:, 4:8, (t - 16) * 128:(t - 15) * 128], in_=pT[:, 0:4, :]),
                                     reads=[b_pT], writes=[b_out])

                    state = {"first": True}

                    sctx = {}

                    def sample_unit(bt, part):
                        b, tt = bt // 4, bt % 4
                        specs = []
                        for g, d in enumerate((1, 4, 16, 0)):
                            specs.append((g, d, 1 if d == 0 else 128))
                        if part == "kdma":
                            qb, bqb = qbc_r.next()
                            S.dma(qb[:], qsc[bt:bt + 1, :].partition_broadcast(128), reads=[b_qsc], writes=[bqb])
                            kts = []
                            for (g, d, np_) in specs:
                                ktile, bkt = kt_r.next()
                                if d == 0:
                                    S.dma(ktile[0:1, :], nk_o[16, bt:bt + 1, :], reads=[b_nk], writes=[bkt])
                                else:
                                    r0 = 2048 + tt - 128 * d
                                    if d == 1 and tt > 0:
                                        nc_ = 128 - tt
                                        S.dma(ktile[0:nc_, :], ck[b, r0:2048, :], writes=[bkt])
                                        S.dma(ktile[nc_:128, :], nk_o[16, 4 * b:4 * b + tt, :], reads=[b_nk], writes=[bkt])
                                    else:
                                        S.dma(ktile[:], ck[b, r0:r0 + 127 * d + 1:d, :], writes=[bkt])
                                kts.append((ktile, bkt))
                            sctx[bt] = dict(qb=qb, bqb=bqb, kts=kts)
                            return
                        c_ = sctx[bt]
                        if part == "vdma":
                            vts = []
                            for (g, d, np_) in specs:
                                vtile, bvt = vt_r.next()
                                if d == 0:
                                    S.dma(vtile[0:1, :], nv_o[16, bt:bt + 1, :], reads=[b_nv], writes=[bvt])
                                else:
                                    r0 = 2048 + tt - 128 * d
                                    if d == 1 and tt > 0:
                                        nc_ = 128 - tt
                                        S.dma(vtile[0:nc_, :], cv[b, r0:2048, :], writes=[bvt])
                                        S.dma(vtile[nc_:128, :], nv_o[16, 4 * b:4 * b + tt, :], reads=[b_nv], writes=[bvt])
                                    else:
                                        S.dma(vtile[:], cv[b, r0:r0 + 127 * d + 1:d, :], writes=[bvt])
                                vts.append((vtile, bvt))
                            c_["vts"] = vts
                            return
                        if part == "s1":
                            qb, bqb = c_["qb"], c_["bqb"]
                            sc, bsc = ssc_r.next()
                            pe_, bpe = pex_r.next()

                            def score(g, np_, ktile, bkt):
                                S.op("dve", lambda e: e.tensor_tensor(out=prod[0:np_, :], in0=ktile[0:np_, :], in1=qb[0:np_, :], op=ALU.mult),
                                     reads=[bkt, bqb], writes=[b_prod])
                                S.op("dve", lambda e: e.tensor_reduce(out=sc[0:np_, g * 8:(g + 1) * 8], in_=prod[0:np_, :].rearrange("p (h d) -> p h d", h=8),
                                                                     axis=AX.X, op=ALU.add), reads=[b_prod], writes=[bsc])
                            for (g, d, np_), (ktile, bkt) in zip(specs, c_["kts"]):
                                score(g, np_, ktile, bkt)
                            S.op("act", lambda e: e.activation(out=pe_[:], in_=sc[:], func=AF.Exp, scale=0.125), reads=[bsc], writes=[bpe])
                            S.op("dve", lambda e: e.tensor_scalar(out=pe_[0:1, 24:32], in0=pe_[0:1, 24:32], scalar1=3.0, scalar2=None, op0=ALU.mult),
                                 reads=[bpe], writes=[bpe])
                            c_["pe"], c_["bpe"] = pe_, bpe
                            return
                        pe_, bpe = c_["pe"], c_["bpe"]

                        def pv(g, np_, vtile, bvt):
                            S.op("pe", lambda e: e.matmul(pacc[:], lhsT=pe_[0:np_, g * 8:(g + 1) * 8], rhs=vtile[0:np_, :], start=(g == 0), stop=(g == 3)),
                                 reads=[bpe, bvt], writes=[b_pacc])
                            S.op("pe", lambda e: e.matmul(pden[:, 0:1], lhsT=pe_[0:np_, g * 8:(g + 1) * 8], rhs=onesc[0:np_, :], start=(g == 0), stop=(g == 3)),
                                 reads=[bpe, b_k1], writes=[b_pden])
                        for (g, d, np_), (vtile, bvt) in zip(specs, c_["vts"]):
                            pv(g, np_, vtile, bvt)
                        S.op("dve", lambda e: e.reciprocal(out=rden[:], in_=pden[:, 0:1]), reads=[b_pden], writes=[b_rden])
                        S.op("dve", lambda e: e.scalar_tensor_tensor(out=msk8[:], in0=pacc[:], scalar=rden[:], in1=dgm[:],
                                                                    op0=ALU.mult, op1=ALU.mult),
                             reads=[b_pacc, b_rden, b_k1], writes=[b_msk8])
                        S.op("pe", lambda e: e.matmul(prow[:], lhsT=sel[:, bt * 16:(bt + 1) * 16], rhs=msk8[:],
                                                      start=(bt == 0), stop=(bt == 15)),
                             reads=[b_msk8, b_k1], writes=[b_prow])
                        sctx.pop(bt)

                    def sample_finish():
                        S.op("dve", lambda e: e.tensor_copy(out=attb[0:16, :], in_=prow[:]), reads=[b_prow], writes=[b_attb])
                        for j in range(4):
                            S.op("pe", lambda e, j=j: e.transpose(out=pT[:, j, :], in_=attb[:, j * 128:(j + 1) * 128], identity=ident[:]),
                                 reads=[b_attb, b_c], writes=[b_pT], sig=(j == 3))
                        S.op("act", lambda e: e.activation(out=mixTs[:, 0:4, :], in_=pT[:, 0:4, :], func=AF.Copy), reads=[b_pT], writes=[b_out])

                    NO = len(order)
                    for k in range(3):
                        issue_load(order[k])
                    for k in range(3):
                        f1a(order[k], "act")
                    f1a(order[0], "recip"); f1a(order[1], "recip")
                    f1b(order[0], "cast"); f1b(order[0], "T"); f2(order[0])
                    f1b(order[1], "cast"); f1b(order[1], "T")
                    lna(order[0], "stats")
                    f1b(order[2], "cast")
                    issue_load(order[3])
                    for i, t in enumerate(order):
                        if i + 4 < NO:
                            issue_load(order[i + 4])
                        if 1 <= i <= 16:
                            sample_unit(i - 1, "kdma")
                        if i + 2 < NO:
                            f1b(order[i + 2], "T")
                        if i >= 1:
                            tileA_back(order[i - 1], "e")
                            ctxs.pop(order[i - 1])
                        if i + 1 < NO:
                            f2(order[i + 1])
                        if i + 2 < NO:
                            f1a(order[i + 2], "recip")
                        tileA_back(t, "a")
                        lna(t, "recip")
                        tileA_back(t, "c")
                        tileA_back(t, "b")
                        tileA_back(t, "d")
                        if 2 <= i <= 17:
                            sample_unit(i - 2, "s2")
                        if 1 <= i <= 16:
                            sample_unit(i - 1, "vdma")
                        wq_issue(2)
                        if i + 3 < NO:
                            f1a(order[i + 3], "act")
                            f1b(order[i + 3], "cast")
                        if 1 <= i <= 16:
                            sample_unit(i - 1, "s1")
                        if i + 1 < NO:
                            lna(order[i + 1], "stats")
                        if i == 17:
                            sample_finish()
                    tileA_back(order[-1], "e")
                    ctxs.pop(order[-1])
                    wq_issue(100)
                    S.barrier()
                    if KSTOP <= 1:
                        S.emit(); raise _Stop(nc)

                with ExitStack() as s2:
                    mk_f = sbt(s2, "mk_f", [128, 2, 512], F32)
                    mk = sbt(s2, "mk", [128, 2, 512], BF16)
                    onp_f = sbt(s2, "onp_f", [128, 256], F32)
                    onp = sbt(s2, "onp", [128, 2, 128], BF16)
                    accs = [(sbt(s2, f"accN{i}", [128, SH], F32), sbt(s2, f"accD{i}", [128, SH], F32), Buf(), Buf()) for i in range(2)]
                    PT_r = Ring([sbt(s2, f"PT{i}", [128, 512], BF16) for i in range(4)])
                    V_r = Ring([sbt(s2, f"Vt{i}", [128, 2, 2, 128], BF16) for i in range(6)])
                    ST_r = Ring([pst(s2, f"ST{i}", [128, 2, 512]) for i in range(3)])
                    NUM_r = Ring([pst(s2, f"NUM{i}", [128, 512]) for i in range(1)])
                    DEN_r = Ring([pst(s2, f"DEN{i}", [128, 512]) for i in range(1)])
                    b_k2 = Buf()
                    S.dma(mk_f[:], amask.rearrange("t p n -> p t n"), writes=[b_k2])
                    S.dma(onp_f[:], onesp, writes=[b_k2])
                    S.op("dve", lambda e: e.tensor_copy(out=mk[:], in_=mk_f[:]), reads=[b_k2], writes=[b_k2])
                    S.op("dve", lambda e: e.tensor_copy(out=onp[:].rearrange("p h n -> p (h n)"), in_=onp_f[:]), reads=[b_k2], writes=[b_k2])

                    def blocks_of(d, G):
                        if d == 1:
                            return [128 * (4 * G + j) for j in range(4)]
                        if d == 4:
                            return [512 * G + r for r in range(4)]
                        return [4 * G + r for r in range(4)]

                    def acc_view(acc, d, G):
                        if d == 1:
                            return acc[:, 512 * G:512 * (G + 1)].rearrange("p (j i) -> p j i", j=4)
                        if d == 4:
                            return acc[:, 512 * G:512 * (G + 1)].rearrange("p (i r) -> p r i", r=4)
                        return acc[:].rearrange("p (i r) -> p r i", r=16)[:, 4 * G:4 * G + 4, :]

                    DILS = tuple(int(v) for v in os.environ.get("KDILS", "1,4,16").split(","))
                    blks = []
                    for c in range(4):
                        for di, d in enumerate(DILS):
                            for G in range(4):
                                for j, q0 in enumerate(blocks_of(d, G)):
                                    blks.append(dict(c=c, di=di, d=d, G=G, j=j, q0=q0))
                    nblk = len(blks)

                    def att_vload(B):
                        d, c = B["d"], B["c"]
                        kc0 = 2048 + B["q0"]
                        kp0 = kc0 - 128 * d
                        Vt, bV = V_r.next()
                        for kt, k0 in enumerate((kp0, kc0)):
                            S.dma(Vt[:, kt, :, :].rearrange("p h n -> p (h n)"),
                                  vsc[k0:k0 + 127 * d + 1:d, 256 * c:256 * (c + 1)], writes=[bV])
                        B["Vt"], B["bV"] = Vt, bV

                    def att_front(B, idx):
                        d, c, q0 = B["d"], B["c"], B["q0"]
                        kc0 = 2048 + q0
                        kp0 = kc0 - 128 * d
                        STp, bS = ST_r.next()
                        n = 0
                        for hh in range(2):
                            for kt, k0 in enumerate((kp0, kc0)):
                                n += 1
                                S.op("pe", lambda e, hh=hh, kt=kt, k0=k0: e.matmul(
                                    STp[:, hh, kt * 128:(kt + 1) * 128],
                                    lhsT=kT[hh * 64:(hh + 1) * 64, c, k0:k0 + 127 * d + 1:d],
                                    rhs=qT[hh * 64:(hh + 1) * 64, c, q0:q0 + 127 * d + 1:d], start=True, stop=True),
                                    writes=[bS], sig=(n == 4))
                        PT, bP = PT_r.next()
                        S.op("act", lambda e: e.activation(out=PT[:].rearrange("p (h x) -> p h x", h=2), in_=STp[:, :, 0:256], func=AF.Exp, scale=0.125),
                             reads=[bS], writes=[bP])
                        mi = 1 if kp0 < 2048 else 0
                        S.op("dve" if idx % 4 != 3 else "pool", lambda e: e.tensor_tensor(out=PT[:], in0=PT[:], in1=mk[:, mi, :], op=ALU.mult),
                             reads=[bP, b_k2], writes=[bP])
                        B["PT"], B["bP"] = PT, bP

                    cur = {}

                    def att_back(B):
                        c, di, d, G, j = B["c"], B["di"], B["d"], B["G"], B["j"]
                        PT, bP, Vt, bV = B["PT"], B["bP"], B["Vt"], B["bV"]
                        if j == 0:
                            cur["NUM"], cur["bN"] = NUM_r.next()
                            cur["DEN"], cur["bD"] = DEN_r.next()
                        NUM, bN, DEN, bD = cur["NUM"], cur["bN"], cur["DEN"], cur["bD"]
                        n = 0
                        for hh in range(2):
                            for kt in range(2):
                                n += 1
                                S.op("pe", lambda e, hh=hh, kt=kt, n=n: e.matmul(
                                    NUM[:, j * 128:(j + 1) * 128], lhsT=Vt[:, kt, hh, :],
                                    rhs=PT[:, (hh * 2 + kt) * 128:(hh * 2 + kt + 1) * 128], start=(n == 1), stop=(n == 4)),
                                    reads=[bP, bV], writes=[bN], sig=False)
                        n = 0
                        for hh in range(2):
                            for kt in range(2):
                                n += 1
                                S.op("pe", lambda e, hh=hh, kt=kt, n=n: e.matmul(
                                    DEN[:, j * 128:(j + 1) * 128], lhsT=onp[:, hh, :],
                                    rhs=PT[:, (hh * 2 + kt) * 128:(hh * 2 + kt + 1) * 128], start=(n == 1), stop=(n == 4)),
                                    reads=[bP, bV, b_k2], writes=[bN, bD], sig=(n == 4))
                        if j != 3:
                            return
                        aN, aD, baN, baD = accs[c % 2]
                        nv_ = acc_view(aN, d, G)
                        dv_ = acc_view(aD, d, G)
                        N3 = NUM[:].rearrange("p (j i) -> p j i", j=4)
                        D3 = DEN[:].rearrange("p (j i) -> p j i", j=4)
                        if di == 0:
                            S.op("act", lambda e: e.activation(out=nv_, in_=N3, func=AF.Copy), reads=[bN], writes=[baN])
                            S.op("dve", lambda e: e.tensor_copy(out=dv_, in_=D3), reads=[bD], writes=[baD])
                        else:
                            S.op("dve", lambda e: e.tensor_tensor(out=nv_, in0=N3, in1=nv_, op=ALU.add), reads=[bN, baN], writes=[baN])
                            S.op("dve", lambda e: e.tensor_tensor(out=dv_, in0=D3, in1=dv_, op=ALU.add), reads=[bD, baD], writes=[baD])
                        if di == len(DILS) - 1 and G == 3:
                            for G2 in range(4):
                                att_final(c, G2)

                    def att_final(c, G):
                        aN, aD, baN, baD = accs[c % 2]
                        sl = slice(512 * G, 512 * (G + 1))
                        S.op("dve", lambda e: e.reciprocal(out=aD[:, sl], in_=aD[:, sl]), reads=[baD], writes=[baD])
                        S.op("pool", lambda e: e.tensor_tensor(out=mixT[:, c, sl], in0=aN[:, sl], in1=aD[:, sl], op=ALU.mult),
                             reads=[baN, baD], writes=[baN, baD])

                    for k in range(min(4, nblk)):
                        att_vload(blks[k])
                    att_front(blks[0], 0)
                    if nblk > 1:
                        att_front(blks[1], 1)
                    for idx in range(nblk):
                        if idx + 4 < nblk:
                            att_vload(blks[idx + 4])
                        if idx + 2 < nblk:
                            att_front(blks[idx + 2], idx + 2)
                        att_back(blks[idx])
                    S.barrier()
                    if KSTOP <= 2:
                        S.emit(); raise _Stop(nc)

            with ExitStack() as s3:
                wdn = sbt(s3, "wdn", [128, 32, 1024], BF16)
                wo = sbt(s3, "wo", [128, 8, 1024], BF16)
                wg = sbt(s3, "wg", [128, 8, 1024], BF16)
                wp = sbt(s3, "wp", [128, 2, 1024], BF16)
                fgb = sbt(s3, "fgb", [128, 1024], F32)
                wu_r = Ring([sbt(s3, f"wu{i}", [128, 8, 512], BF16) for i in range(2)])
                xh_r = Ring([sbt(s3, f"xh{i}", [128, D], F32) for i in range(4)])
                pt_r = Ring([sbt(s3, f"ptl{i}", [128, 256], F32) for i in range(4)])
                hb = sbt(s3, "hb", [128, D], BF16); b_hb = Buf()
                junk2 = hb; b_junk2 = b_hb
                hT = sbt(s3, "hT", [128, 8, 256], BF16); b_hT = Buf()
                h2T = sbt(s3, "h2T", [128, 8, 128], BF16); b_h2T = Buf()
                actT = sbt(s3, "actT", [128, 32, 256], BF16); b_actT = Buf()
                rl_r = Ring([sbt(s3, f"rl{i}", [128, 256], F32) for i in range(2)])
                gate_r = Ring([sbt(s3, f"gate{i}", [128, 512], F32) for i in range(1)])
                pb16 = sbt(s3, "pb16", [128, 256], BF16); b_pb16 = Buf()
                pTp = sbt(s3, "pTp", [128, 2, 128], BF16); b_pTp = Buf()
                stt_all = [sbt(s3, f"stt{i}", [128, 2, 16], F32) for i in range(2)]; cur_stt = [stt_all[0]]
                pbk = Ring([pst(s3, f"pb{i}", [128, 512]) for i in range(3)])
                ps1 = Ring([pst(s3, "ps1", [128, 512])])
                pT3 = pst(s3, "pT3", [128, 8, 128], BF16); b_pT3 = Buf()
                hb2 = sbt(s3, "hb2", [128, D], BF16); b_hb2 = Buf()
                pa_r = Ring([pst(s3, f"pa{i}", [128, 512]) for i in range(2)])
                pT2 = pst(s3, "pT2", [128, 8, 128], BF16); b_pT2 = Buf()
                b_w = Buf(); b_yo = Buf()
                b_wo = Buf(); b_wg = Buf(); b_wp = Buf(); b_wd = Buf()
                S.dma(wo[:], wo_s, reads=[b_wsc], writes=[b_wo])
                S.dma(fgb[:], rowv[0:1, :].partition_broadcast(128), writes=[b_w])
                for j in range(4):
                    S.dma(wdn[:, 8 * j:8 * j + 8, :], wd_s[:, 8 * j:8 * j + 8, :], reads=[b_wsc], writes=[b_wd])
                S.dma(wg[:], wg_s, reads=[b_wsc], writes=[b_wg])
                S.dma(wp[:], wp_s, reads=[b_wsc], writes=[b_wp])

                def stats_and_T(b_st, xh, bxh, si, col, dstT, b_dstT, ncol_off, pTb=None, b_pTb=None, hbb=None, b_hbb=None, gcol0=0):
                    stt = cur_stt[0]
                    pTb = pT2 if pTb is None else pTb
                    b_pTb = b_pT2 if b_pTb is None else b_pTb
                    hbb = hb if hbb is None else hbb
                    b_hbb = b_hb if b_hbb is None else b_hbb
                    S.op("act", lambda e: e.activation(out=hbb[:], in_=xh[:], func=AF.Square, accum_out=stt[:, si, col:col + 1]),
                         reads=[bxh], writes=[b_hbb, b_st])
                    S.op("act", lambda e: e.activation(out=stt[:, si, col + 1:col + 2], in_=stt[:, si, col:col + 1], func=AF.Sqrt,
                                                       scale=1.0 / D, bias=EPS), reads=[b_st], writes=[b_st])
                    S.op("dve", lambda e: e.reciprocal(out=stt[:, si, col + 2:col + 3], in_=stt[:, si, col + 1:col + 2]), reads=[b_st], writes=[b_st])
                    if dstT is None:
                        return
                    S.op("act", lambda e: e.activation(out=hbb[:], in_=xh[:], func=AF.Copy), reads=[bxh], writes=[b_hbb])
                    for c in range(8):
                        S.op("pe", lambda e, c=c: e.transpose(out=pTb[:, c, :], in_=hbb[:, c * 128:(c + 1) * 128], identity=ident[:]),
                             reads=[b_hbb, b_c], writes=[b_pTb], sig=(c == 7))
                    for c in range(8):
                        S.op("dve", lambda e, c=c: e.tensor_scalar(out=dstT[:, c, ncol_off:ncol_off + 128], in0=pTb[:, c, :],
                                                                  scalar1=gn[:, gcol0 + c:gcol0 + c + 1], scalar2=None, op0=ALU.mult),
                             reads=[b_pTb, b_c], writes=[b_dstT], sig=(c == 7))

                def mm_group(out_ap, bout, pairs, reads):
                    n = len(pairs)
                    for i, (l, r) in enumerate(pairs):
                        S.op("pe", lambda e, l=l, r=r, i=i: e.matmul(out_ap, lhsT=l, rhs=r, start=(i == 0), stop=(i == n - 1)),
                             reads=reads, writes=[bout], sig=(i == n - 1))

                def stage1(b_st, xh, bxh, mc, si):
                    stt = cur_stt[0]
                    for hf in range(2):
                        pb_, bpb = ps1.next()
                        mm_group(pb_[:], bpb, [(mc[:, fc, :], wo[:, fc, hf * 512:(hf + 1) * 512]) for fc in range(8)], [b_wo])
                        xs = xh[:, hf * 512:(hf + 1) * 512]
                        S.op("dve", lambda e, pb_=pb_, xs=xs: e.tensor_tensor(out=xs, in0=pb_[:], in1=xs, op=ALU.add),
                             reads=[bpb, bxh], writes=[bxh])
                    stats_and_T(b_st, xh, bxh, si, 0, hT, b_hT, si * 128, pT3, b_pT3, hb2, b_hb2, gcol0=8)
                    S.op("dve", lambda e: e.tensor_tensor(out=stt[:, si, 3:4], in0=stt[:, si, 2:3], in1=stt[:, si, 2:3], op=ALU.mult),
                         reads=[b_st], writes=[b_st])

                wu_pending = []

                wu_b2 = {}

                def wu_load(u):
                    wu, bwu = wu_r.next()
                    bwu2 = wu_b2.setdefault(id(bwu), Buf())
                    S.dma(wu[:, 0:4, :], wu_s[u][:, 0:4, :], reads=[b_wsc], writes=[bwu], q="sp")
                    S.dma(wu[:, 4:8, :], wu_s[u][:, 4:8, :], reads=[b_wsc], writes=[bwu2], q="act")
                    wu_pending.append((wu, bwu, bwu2))

                def stage2_unit(u, T):
                    wu, bwu, bwu2 = wu_pending.pop(0)
                    for f4 in range(4):
                        ffc = 4 * u + f4
                        pa, bpa = pa_r.next()
                        mm_group(pa[:, 0:T], bpa, [(wu[:, kc, f4 * 128:(f4 + 1) * 128], hT[:, kc, 0:T]) for kc in range(8)], [bwu, bwu2, b_hT])
                        rl, brl = rl_r.next()
                        S.op("act", lambda e, rl=rl, pa=pa: e.activation(out=rl[:, 0:T], in_=pa[:, 0:T], func=AF.Relu), reads=[bpa], writes=[brl])
                        S.op("pool" if ffc % 2 else "dve",
                             lambda e, rl=rl, ffc=ffc: e.tensor_tensor(out=actT[:, ffc, 0:T], in0=rl[:, 0:T], in1=rl[:, 0:T], op=ALU.mult),
                             reads=[brl], writes=[b_actT])

                def stage34(b_st, xh, bxh, ptl, bpt, si, s):
                    stt = cur_stt[0]
                    for hf in range(2):
                        pb_, bpb = pbk.next()
                        mm_group(pb_[:], bpb, [(actT[:, ffc, si * 128:(si + 1) * 128], wdn[:, ffc, hf * 512:(hf + 1) * 512]) for ffc in range(32)],
                                 [b_actT, b_wd])
                        xs = xh[:, hf * 512:(hf + 1) * 512]
                        S.op("dve", lambda e, pb_=pb_, xs=xs: e.scalar_tensor_tensor(out=xs, in0=pb_[:], scalar=stt[:, si, 3:4], in1=xs,
                                                                                    op0=ALU.mult, op1=ALU.add),
                             reads=[bpb, bxh, b_st], writes=[bxh])
                    stats_and_T(b_st, xh, bxh, si, 4, h2T, b_h2T, 0, gcol0=16)
                    S.op("pool", lambda e: e.tensor_copy(out=pb16[:], in_=ptl[:]), reads=[bpt], writes=[b_pb16])
                    for c in range(2):
                        S.op("pe", lambda e, c=c: e.transpose(out=pT2[:, c, :], in_=pb16[:, c * 128:(c + 1) * 128], identity=ident[:]),
                             reads=[b_pb16, b_c], writes=[b_pT2], sig=(c == 1))
                    S.op("dve", lambda e: e.tensor_copy(out=pTp[:], in_=pT2[:, 0:2, :]), reads=[b_pT2], writes=[b_pTp])
                    for hf in range(2):
                        pg, bpg = pbk.next()
                        mm_group(pg[:], bpg, [(h2T[:, kc, :], wg[:, kc, hf * 512:(hf + 1) * 512]) for kc in range(8)], [b_h2T, b_wg])
                        gt, bgt = gate_r.next()
                        S.op("act", lambda e, gt=gt, pg=pg: e.activation(out=gt[:], in_=pg[:], func=AF.Sigmoid, scale=stt[:, si, 6:7]),
                             reads=[bpg, b_st], writes=[bgt])
                        pp, bpp = pbk.next()
                        mm_group(pp[:], bpp, [(pTp[:, kc, :], wp[:, kc, hf * 512:(hf + 1) * 512]) for kc in range(2)], [b_pTp, b_wp])
                        xs = xh[:, hf * 512:(hf + 1) * 512]
                        S.op("dve", lambda e, gt=gt, pp=pp: e.tensor_tensor(out=gt[:], in0=pp[:], in1=gt[:], op=ALU.mult), reads=[bpp, bgt], writes=[bgt])
                        S.op("dve", lambda e, gt=gt, xs=xs: e.tensor_tensor(out=xs, in0=gt[:], in1=xs, op=ALU.add), reads=[bgt, bxh], writes=[bxh])
                    stats_and_T(b_st, xh, bxh, si, 8, None, None, 0)
                    S.op("dve", lambda e: e.scalar_tensor_tensor(out=xh[:], in0=xh[:], scalar=stt[:, si, 10:11], in1=fgb[:], op0=ALU.mult, op1=ALU.mult),
                         reads=[bxh, b_st, b_w], writes=[bxh])
                    S.dma(y_o[s], xh[:], reads=[bxh], writes=[b_yo])

                passes = [([2 * i, 2 * i + 1], False) for i in range(8)] + [([16], True)]
                pinfo = {}

                def pass_loads(p):
                    subs, samp = passes[p]
                    tiles = []
                    for si, s in enumerate(subs):
                        xh, bxh = xh_r.next()
                        ptl, bpt = pt_r.next()
                        S.dma(xh[:], xall[32 if samp else 16 + s], writes=[bxh])
                        S.dma(ptl[:], pall[s], writes=[bpt])
                        tiles.append((xh, bxh, ptl, bpt))
                    pinfo[p] = (tiles, Buf(), stt_all[p % 2])

                def pass_stage1(p):
                    subs, samp = passes[p]
                    tiles, b_st, sttp = pinfo[p]
                    cur_stt[0] = sttp
                    for si, s in enumerate(subs):
                        xh, bxh, ptl, bpt = tiles[si]
                        mc = mixTs[:, :, :] if samp else mixT[:, :, s * 128:(s + 1) * 128]
                        stage1(b_st, xh, bxh, mc, si)

                def pass_stage34(p):
                    subs, samp = passes[p]
                    tiles, b_st, sttp = pinfo[p]
                    cur_stt[0] = sttp
                    for si, s in enumerate(subs):
                        xh, bxh, ptl, bpt = tiles[si]
                        stage34(b_st, xh, bxh, ptl, bpt, si, s)

                pass_loads(0)
                wu_load(0); wu_load(1)
                pass_stage1(0)
                for p in range(len(passes)):
                    subs, samp = passes[p]
                    T = 128 * len(subs)
                    if p + 1 < len(passes):
                        pass_loads(p + 1)
                    for u in range(8):
                        stage2_unit(u, T)
                        g_next = p * 8 + u + 2
                        if g_next < 8 * len(passes):
                            wu_load(g_next % 8)
                    streams = [S.capture(pass_stage34, p)]
                    if p + 1 < len(passes):
                        streams.append(S.capture(pass_stage1, p + 1))
                    S.replay(streams)
                S.barrier()
        except ZeroDivisionError:
            pass
        S.emit()
    return nc


_PROGRAM = None


def _rope_tables(pos):
    pos = np.asarray(pos, dtype=np.float32)
    inv = (np.float32(500000.0) ** (-np.arange(0, 16, 2, dtype=np.float32) / np.float32(16))).astype(np.float32)
    ang = (pos[:, None] * inv[None, :]).astype(np.float32)
    return np.concatenate([np.cos(ang).astype(np.float32), np.sin(ang).astype(np.float32)], axis=1)


def kernel(x_prompt, x_sample, cache_k, cache_v, p_prompt, p_sample,
           norm1_g, w_in, ln_v_g, ln_v_b, w_spatial, b_spatial, w_out,
           norm2_g, w_up, w_down, gate_norm_g, w_gate, w_ple, final_g):
    global _PROGRAM
    f32 = np.float32
    x_prompt = np.asarray(x_prompt, f32); x_sample = np.asarray(x_sample, f32)
    cache_k = np.asarray(cache_k, f32); cache_v = np.asarray(cache_v, f32)
    p_prompt = np.asarray(p_prompt, f32); p_sample = np.asarray(p_sample, f32)
    xp = x_prompt[0]
    pp = p_prompt[0, 0]
    def gl(g):
        return np.ascontiguousarray(np.asarray(g, f32).reshape(8, 128).T)
    gains = np.concatenate([gl(norm1_g[0]), gl(norm2_g[0]), gl(gate_norm_g[0])], axis=1)
    rowv = np.zeros((3, 1024), f32)
    rowv[0] = np.asarray(final_g, f32)
    rowv[1, :512] = np.asarray(ln_v_g[0], f32)
    rowv[1, 512:] = np.asarray(ln_v_b[0], f32)
    ws = np.asarray(w_spatial[0], f32)
    bs = np.asarray(b_spatial[0], f32)
    wsT = np.zeros((2, 128, 8, 128), f32)
    wsT[0] = np.transpose(ws, (2, 0, 1))
    trilm = np.zeros((2, 128, 128), f32)
    trilm[0] = np.triu(np.ones((128, 128), f32))
    bsp = np.zeros((2, 128, 8), f32)
    bsp[0] = bs.T
    for b in range(4):
        for j in range(4):
            for i in range(4):
                wsT[1, 4 * b + j, :, 4 * b + i] = ws[:, i, j]
                if j <= i:
                    trilm[1, 4 * b + j, 4 * b + i] = 1.0
        bsp[1, 4 * b:4 * b + 4, :] = bs[:, :4].T
    wsT = wsT.reshape(2, 128, 1024)
    ident = np.eye(128, dtype=f32)
    ik = np.arange(128)[:, None]; iq = np.arange(128)[None, :]
    m_prev = (ik >= iq).astype(f32); m_cur = (ik <= iq).astype(f32)
    mN = np.concatenate([m_prev, m_cur, m_prev, m_cur], axis=1)
    mH = np.concatenate([np.zeros_like(m_prev), m_cur, np.zeros_like(m_prev), m_cur], axis=1)
    onesp = np.zeros((128, 2, 128), f32)
    onesp[:, 0, :64] = 1.0; onesp[:, 1, 64:] = 1.0
    onesp = onesp.reshape(128, 256)
    diagm = np.zeros((8, 8, 64), f32)
    for h in range(8):
        diagm[h, h, :] = 1.0
    diagm = diagm.reshape(8, 512)
    selc = np.zeros((8, 16, 16), f32)
    for bt in range(16):
        selc[:, bt, bt] = 1.0
    selc = selc.reshape(8, 256)
    shared = dict(w_in=np.ascontiguousarray(w_in[0], f32), w_out=np.ascontiguousarray(w_out[0], f32),
                  w_up=np.ascontiguousarray(w_up[0], f32), w_down=np.ascontiguousarray(w_down[0], f32),
                  w_gate=np.ascontiguousarray(w_gate[0], f32), w_ple=np.ascontiguousarray(w_ple[0], f32),
                  gains=gains, rowv=rowv, wsT=wsT, trilm=trilm, bsp=bsp, ident=ident,
                  onesp=onesp, diagm=diagm, selc=selc)
    in_maps = []
    for c in range(NCORES):
        xall = np.zeros((33, 128, D), f32)
        if c > 0:
            xall[0:16] = xp[(c - 1) * SH:c * SH].reshape(16, 128, D)
        xall[16:32] = xp[c * SH:(c + 1) * SH].reshape(16, 128, D)
        xall[32, :16] = x_sample[4 * c:4 * c + 4].reshape(16, D)
        pall = np.zeros((17, 128, 256), f32)
        pall[:16] = pp[c * SH:(c + 1) * SH].reshape(16, 128, 256)
        pall[16, :16] = p_sample[0, 4 * c:4 * c + 4].reshape(16, 256)
        pos = np.zeros((33, 128), f32)
        pos[:32] = ((c - 1) * SH + np.arange(2 * SH)).reshape(32, 128)
        pos[32, :16] = 16384 + np.tile(np.arange(4), 4)
        cs = _rope_tables(pos.reshape(-1)).reshape(33, 128, 16)
        m = dict(shared)
        m.update(xall=xall, pall=pall, cs=cs,
                 ck=np.ascontiguousarray(cache_k[0, 4 * c:4 * c + 4].reshape(4, 2048, 512)),
                 cv=np.ascontiguousarray(cache_v[0, 4 * c:4 * c + 4].reshape(4, 2048, 512)),
                 amask=np.stack([mN, mN if c > 0 else mH], axis=0))
        in_maps.append(m)
    if _PROGRAM is None:
        try:
            _PROGRAM = build_program()
        except _Stop as e_:
            _PROGRAM = e_.args[0]
    res = run_bass_kernel_spmd(_PROGRAM, in_maps, core_ids=list(range(NCORES)))
    R = res.results
    y_prompt = np.concatenate([R[c]["y"][:16].reshape(SH, D) for c in range(NCORES)], 0)[None]
    y_sample = np.concatenate([R[c]["y"][16, :16].reshape(4, 4, D) for c in range(NCORES)], 0)
    nkp = R[7]["nk"][:16].reshape(1, 1, SH, 8, 64)
    nvp = R[7]["nv"][:16].reshape(1, 1, SH, 8, 64)
    nks = np.concatenate([R[c]["nk"][16, :16].reshape(4, 4, 8, 64) for c in range(NCORES)], 0)[None]
    nvs = np.concatenate([R[c]["nv"][16, :16].reshape(4, 4, 8, 64) for c in range(NCORES)], 0)[None]
    nvc = np.concatenate([R[c]["nvc"][:16].reshape(4, 4, 512) for c in range(NCORES)], 0)[None]
    return (y_prompt.astype(f32), y_sample.astype(f32), nkp.astype(f32), nvp.astype(f32),
            nks.astype(f32), nvs.astype(f32), nvc.astype(f32))
```

```python
import numpy as np
from contextlib import ExitStack
import concourse.bass as bass
import concourse.mybir as mybir
from concourse.bass_utils import run_bass_kernel_spmd

F32 = mybir.dt.float32
BF16 = mybir.dt.bfloat16
AF = mybir.ActivationFunctionType
ALU = mybir.AluOpType
AX = mybir.AxisListType

ENGS = ["pe", "act", "dve", "pool", "sp"]
NCORES = 8
D = 1024
SH = 2048
NT = 16
EPS = 1e-6
LN3 = float(np.log(3.0))


import os
KSTOP = int(os.environ.get("KSTOP", "99"))


class _Stop(Exception):
    pass


class Buf:
    __slots__ = ("w", "r")

    def __init__(self):
        self.w = None
        self.r = []


class Sched:
    def __init__(self, nc, stack, n_dma_sems=24):
        self.nc = nc
        self.ops = {e: [] for e in ENGS}
        self.sem = {}
        self.cnt = {e: 0 for e in ENGS}
        for e in ENGS:
            self.sem[e] = stack.enter_context(nc.semaphore("s_" + e))
        self.dsems = {}
        self.dpos = {}
        for q, n in (("sp", n_dma_sems), ("act", 8), ("dve", 8), ("pool", 8)):
            self.dsems[q] = [[stack.enter_context(nc.semaphore(f"d_{q}{i}")), 0] for i in range(n)]
            self.dpos[q] = 0
        self.known = {}
        self.cap = None

    def capture(self, fn, *args):
        lst = []
        self.cap = lst
        fn(*args)
        self.cap = None
        return lst

    def replay(self, lists):
        lists = [l for l in lists if l]
        pos = [0] * len(lists)
        while True:
            best, bf = -1, 2.0
            for i, l in enumerate(lists):
                if pos[i] < len(l):
                    f = pos[i] / len(l)
                    if f < bf:
                        best, bf = i, f
            if best < 0:
                break
            kind, a, kw = lists[best][pos[best]]
            pos[best] += 1
            if kind == "op":
                self.op(*a, **kw)
            else:
                self.dma(*a, **kw)

    def _wait(self, eng, tok):
        if tok is None:
            return
        key, semobj, val, prod = tok
        if prod == eng and eng == "pe":
            return
        kk = (eng, key)
        if self.known.get(kk, 0) >= val:
            return
        self.known[kk] = val
        self.ops[eng].append(lambda e, s=semobj, v=val: e.wait_ge(s, v))

    def _deps(self, eng, reads, writes):
        for b in reads:
            self._wait(eng, b.w)
        for b in writes:
            self._wait(eng, b.w)
            for t in b.r:
                self._wait(eng, t)

    def _commit(self, tok, reads, writes):
        for b in reads:
            b.r.append(tok)
            if len(b.r) > 24:
                b.r = b.r[-24:]
        for b in writes:
            b.w = tok
            b.r = []

    def op(self, eng, fn, reads=(), writes=(), sig=True):
        if self.cap is not None:
            self.cap.append(("op", (eng, fn), dict(reads=reads, writes=writes, sig=sig)))
            return None
        self._deps(eng, reads, writes)
        if sig:
            self.cnt[eng] += 1
            v = self.cnt[eng]
            s = self.sem[eng]
            self.ops[eng].append(lambda e, f=fn, s=s: f(e).then_inc(s, 1))
            tok = (eng, s, v, eng)
            self._commit(tok, reads, writes)
            return tok
        self.ops[eng].append(lambda e, f=fn: f(e))
        return None

    def dma(self, out, in_, reads=(), writes=(), q="sp"):
        if self.cap is not None:
            self.cap.append(("dma", (out, in_), dict(reads=reads, writes=writes, q=q)))
            return None
        self._deps(q, reads, writes)
        pool = self.dsems[q]
        i = self.dpos[q]
        self.dpos[q] = (i + 1) % len(pool)
        ent = pool[i]
        key = f"d_{q}{i}"
        if ent[1] > 0:
            self._wait(q, (key, ent[0], ent[1], None))
        ent[1] += 16
        s, v = ent[0], ent[1]
        self.ops[q].append(lambda e, o=out, i_=in_, s=s: e.dma_start(out=o, in_=i_).then_inc(s, 16))
        tok = (key, s, v, None)
        self._commit(tok, reads, writes)
        return tok

    def barrier(self):
        toks = []
        for e in ENGS:
            if self.cnt[e] > 0:
                toks.append((e, self.sem[e], self.cnt[e], e))
        for q, pool in self.dsems.items():
            for i, ent in enumerate(pool):
                if ent[1] > 0:
                    toks.append((f"d_{q}{i}", ent[0], ent[1], None))
        for e in ENGS:
            for t in toks:
                if t[3] == e:
                    continue
                self._wait(e, t)

    def emit(self):
        nc = self.nc
        with nc.Block() as block:
            @block.tensor
            def _(e):
                for f in self.ops["pe"]:
                    f(e)

            @block.scalar
            def _(e):
                for f in self.ops["act"]:
                    f(e)

            @block.vector
            def _(e):
                for f in self.ops["dve"]:
                    f(e)

            @block.gpsimd
            def _(e):
                for f in self.ops["pool"]:
                    f(e)

            @block.sync
            def _(e):
                for f in self.ops["sp"]:
                    f(e)


class Ring:
    def __init__(self, tiles):
        self.tiles = tiles
        self.bufs = [Buf() for _ in tiles]
        self.i = 0

    def next(self):
        j = self.i % len(self.tiles)
        self.i += 1
        return self.tiles[j], self.bufs[j]


def build_program():
    nc = bass.Bass("TRN2", target_bir_lowering=False)

    def din(name, shape, dt=F32):
        return nc.dram_tensor(name, list(shape), dt, kind="ExternalInput").ap()

    def dout(name, shape, dt=F32):
        return nc.dram_tensor(name, list(shape), dt, kind="ExternalOutput").ap()

    def dscr(name, shape, dt):
        return nc.dram_tensor(name, list(shape), dt).ap()

    xall = din("xall", [33, 128, D])
    pall = din("pall", [17, 128, 256])
    cs = din("cs", [33, 128, 16])
    ck = din("ck", [4, 2048, 512])
    cv = din("cv", [4, 2048, 512])
    w_in = din("w_in", [D, 2560])
    w_out = din("w_out", [D, D])
    w_up = din("w_up", [D, 4096])
    w_down = din("w_down", [4096, D])
    w_gate = din("w_gate", [D, D])
    w_ple = din("w_ple", [256, D])
    gains = din("gains", [128, 24])
    rowv = din("rowv", [3, 1024])
    wsT = din("wsT", [2, 128, 1024])
    trilm = din("trilm", [2, 128, 128])
    bsp = din("bsp", [2, 128, 8])
    ident_in = din("ident", [128, 128])
    amask = din("amask", [2, 128, 512])
    onesp = din("onesp", [128, 256])
    diagm = din("diagm", [8, 512])
    selc = din("selc", [8, 256])

    y_o = dout("y", [17, 128, D])
    nk_o = dout("nk", [17, 128, 512])
    nv_o = dout("nv", [17, 128, 512])
    nvc_o = dout("nvc", [128, 512])

    vsc = dscr("vsc", [4096, 1024], BF16)
    qsc = dscr("qsc", [16, 512], F32)
    wo_s = dscr("wo_s", [128, 8, 1024], BF16)
    wg_s = dscr("wg_s", [128, 8, 1024], BF16)
    wp_s = dscr("wp_s", [128, 2, 1024], BF16)
    wd_s = dscr("wd_s", [128, 32, 1024], BF16)
    wu_s = dscr("wu_s", [8, 128, 8, 512], BF16)

    with ExitStack() as st:
        S = Sched(nc, st)

        def sbt(stack, name, shape, dt):
            return stack.enter_context(nc.sbuf_tensor("sb_" + name, list(shape), dt))

        def pst(stack, name, shape, dt=F32):
            return stack.enter_context(nc.psum_tensor("ps_" + name, list(shape), dt))

        try:
            ident_f = sbt(st, "ident_f", [128, 128], F32)
            ident = sbt(st, "ident", [128, 128], BF16)
            gn = sbt(st, "gn", [128, 24], F32)
            mixT = sbt(st, "mixT", [128, 8, SH], BF16)
            mixTs = sbt(st, "mixTs", [128, 8, 128], BF16)
            b_c = Buf()
            S.dma(ident_f[:], ident_in, writes=[b_c])
            S.dma(gn[:], gains, writes=[b_c])
            S.op("dve", lambda e: e.tensor_copy(out=ident[:], in_=ident_f[:]), reads=[b_c], writes=[b_c])
            S.op("pool", lambda e: e.memset(mixTs[:], 0.0), writes=[b_c])

            with ExitStack() as sa:
                winb = sbt(sa, "winb", [128, 8, 2560], BF16)
                b_win = Buf()
                b_wblk = [[Buf() for _ in range(8)] for _ in range(5)]
                for blk_ in (1, 2, 0, 3, 4):
                    for kc in range(8):
                        S.dma(winb[:, kc, blk_ * 512:(blk_ + 1) * 512], w_in[kc * 128:(kc + 1) * 128, blk_ * 512:(blk_ + 1) * 512],
                              writes=[b_wblk[blk_][kc]], q="pool")
                b_wsc = Buf()
                wq = []
                for j in range(4):
                    wq.append((wo_s[:, 2 * j:2 * j + 2, :], w_out[j * 256:(j + 1) * 256, :].rearrange("(c p) n -> p c n", p=128)))
                for kc in range(8):
                    for hf in range(2):
                        wq.append((wu_s[4 * hf:4 * hf + 4, :, kc, :].rearrange("u p n -> p u n"),
                                   w_up[kc * 128:(kc + 1) * 128, hf * 2048:(hf + 1) * 2048].rearrange("p (u n) -> p u n", u=4)))
                for j in range(16):
                    wq.append((wd_s[:, 2 * j:2 * j + 2, :], w_down[j * 256:(j + 1) * 256, :].rearrange("(c p) n -> p c n", p=128)))
                for j in range(4):
                    wq.append((wg_s[:, 2 * j:2 * j + 2, :], w_gate[j * 256:(j + 1) * 256, :].rearrange("(c p) n -> p c n", p=128)))
                wq.append((wp_s[:, :, :], w_ple.rearrange("(c p) n -> p c n", p=128)))

                def wq_issue(n):
                    for _ in range(n):
                        if wq:
                            o_, i_ = wq.pop(0)
                            S.dma(o_, i_, writes=[b_wsc], q="pool")

                kT = sbt(sa, "kT", [128, 4, 4096], BF16)
                qT = sbt(sa, "qT", [128, 4, SH], BF16)
                with ExitStack() as s1:
                    xr = Ring([sbt(s1, f"xt{i}", [128, D], F32) for i in range(3)])
                    csr = Ring([sbt(s1, f"cst{i}", [128, 16], F32) for i in range(6)])
                    xb_r = Ring([sbt(s1, f"xb{i}", [128, D], BF16) for i in range(2)])
                    xT_r = Ring([sbt(s1, f"xT{i}", [128, 8, 128], BF16) for i in range(2)])
                    st_r = Ring([sbt(s1, f"st1_{i}", [128, 8], F32) for i in range(4)])
                    st2 = sbt(s1, "st2", [128, 8], F32); b_st2 = Buf()
                    bnst = sbt(s1, "bnst", [128, 6], F32)
                    bnag = sbt(s1, "bnag", [128, 2], F32)
                    qkf_r = Ring([sbt(s1, f"qkf{i}", [128, 1024], F32) for i in range(2)])
                    vf_r = Ring([sbt(s1, f"vf{i}", [128, 512], F32) for i in range(2)])
                    vpad_r = Ring([sbt(s1, f"vpad{i}", [128, 8, 128], BF16) for i in range(1)])
                    rtmp = sbt(s1, "rtmp", [128, 4, 16, 8], F32); b_rtmp = Buf()
                    qkb = sbt(s1, "qkb", [128, 1024], BF16); b_qkb = Buf()
                    ub_r = Ring([sbt(s1, f"ub{i}", [128, 512], BF16) for i in range(2)])
                    vcf_r = Ring([sbt(s1, f"vcf{i}", [128, 512], F32) for i in range(2)])
                    vnf = sbt(s1, "vnf", [128, 512], F32); b_vnf = Buf()
                    vnb = sbt(s1, "vnb", [128, 512], BF16); b_vnb = Buf()
                    gtmp = None; b_gtmp = None
                    gb = sbt(s1, "gb", [128, 512], BF16); b_gb = Buf()
                    lng = sbt(s1, "lng", [128, 1024], F32)
                    wsf = qkf_r.tiles[0]; b_wsf = qkf_r.bufs[0]
                    wsb = sbt(s1, "wsb", [128, 2, 1024], BF16)
                    trm = sbt(s1, "trm", [128, 2, 128], F32)
                    bsb = sbt(s1, "bsb", [128, 2, 8], F32)
                    kt_r = Ring([sbt(s1, f"skt{i}", [128, 512], F32) for i in range(4)])
                    vt_r = Ring([sbt(s1, f"svt{i}", [128, 512], F32) for i in range(4)])
                    qbc_r = Ring([sbt(s1, f"qbc{i}", [128, 512], F32) for i in range(1)])
                    prod = sbt(s1, "prod", [128, 512], F32); b_prod = Buf(); gtmp = prod; b_gtmp = b_prod
                    ssc_r = Ring([sbt(s1, f"ssc{i}", [128, 32], F32) for i in range(2)])
                    pex_r = Ring([sbt(s1, f"pex{i}", [128, 32], F32) for i in range(2)])
                    onesc = sbt(s1, "onesc", [128, 1], F32)
                    rden = sbt(s1, "rden", [8, 1], F32); b_rden = Buf()
                    msk8 = sbt(s1, "msk8", [8, 512], F32); b_msk8 = Buf()
                    dgm = sbt(s1, "dgm", [8, 512], F32)
                    sel = sbt(s1, "sel", [8, 256], F32)
                    attb = sbt(s1, "attb", [128, 512], BF16); b_attb = Buf()
                    pz = Ring([pst(s1, f"pz{i}", [128, 512]) for i in range(2)])
                    pT = pst(s1, "pT", [128, 8, 128], BF16); b_pT = Buf()
                    pTx = pst(s1, "pTx", [128, 8, 128], BF16); b_pTx = Buf()
                    pmix = pst(s1, "pmix", [128, 512]); b_pmix = Buf()
                    pacc = pst(s1, "pacc", [8, 512]); b_pacc = Buf()
                    pden = pst(s1, "pden", [8, 512]); b_pden = Buf()
                    prow = pst(s1, "prow", [16, 512]); b_prow = Buf()

                    b_k1 = Buf()
                    S.dma(lng[:], rowv[1:2, :].partition_broadcast(128), writes=[b_k1])
                    S.dma(trm[:], trilm.rearrange("t p n -> p t n"), writes=[b_k1])
                    S.dma(bsb[:], bsp.rearrange("t p n -> p t n"), writes=[b_k1])
                    S.dma(dgm[:], diagm, writes=[b_k1])
                    S.dma(sel[:], selc, writes=[b_k1])
                    for t2 in range(2):
                        S.dma(wsf[:], wsT[t2], writes=[b_wsf])
                        S.op("dve", lambda e, t2=t2: e.tensor_tensor(
                            out=wsb[:, t2, :].rearrange("p (g i) -> p g i", g=8),
                            in0=wsf[:].rearrange("p (g i) -> p g i", g=8),
                            in1=trm[:, t2, :].unsqueeze(1).broadcast_to([128, 8, 128]), op=ALU.mult),
                            reads=[b_k1, b_wsf], writes=[b_k1])
                    S.op("pool", lambda e: e.memset(onesc[:], 1.0), writes=[b_k1])
                    for (sc_, bsc_) in zip(ssc_r.tiles, ssc_r.bufs):
                        S.op("pool", lambda e, sc_=sc_: e.memset(sc_[:], 0.0), writes=[bsc_])
                    S.op("pool", lambda e: e.memset(attb[:], 0.0), writes=[b_attb])
                    for (vp, bvp) in zip(vpad_r.tiles, vpad_r.bufs):
                        S.op("pool", lambda e, vp=vp: e.memset(vp[:], 0.0), writes=[bvp])

                    b_nk = Buf(); b_nv = Buf(); b_qsc = Buf(); b_vsc = Buf(); b_out = Buf()

                    order = [32] + list(range(32))
                    loads = {}

                    def issue_load(t):
                        xt, bx = xr.next()
                        ct, bct = csr.next()
                        S.dma(xt[:], xall[t], writes=[bx])
                        S.dma(ct[:], cs[t], writes=[bct])
                        loads[t] = (xt, bx, ct, bct)

                    ctxs = {}

                    f1ctx = {}

                    def f1a(t, part):
                        if part == "act":
                            xt, bx, ct, bct = loads.pop(t)
                            xb, b_xb = xb_r.next()
                            xT, b_xT = xT_r.next()
                            st1, b_st1 = st_r.next()
                            S.op("act", lambda e: e.activation(out=qkb[:], in_=xt[:], func=AF.Square, accum_out=st1[:, 0:1]),
                                 reads=[bx], writes=[b_qkb, b_st1])
                            S.op("act", lambda e: e.activation(out=st1[:, 1:2], in_=st1[:, 0:1], func=AF.Sqrt, scale=1.0 / D, bias=EPS),
                                 reads=[b_st1], writes=[b_st1])
                            f1ctx[t] = (xt, bx, ct, bct, xb, b_xb, xT, b_xT, st1, b_st1)
                        else:
                            st1, b_st1 = f1ctx[t][8], f1ctx[t][9]
                            S.op("dve", lambda e: e.reciprocal(out=st1[:, 2:3], in_=st1[:, 1:2]), reads=[b_st1], writes=[b_st1])

                    def f1b(t, part):
                        xt, bx, ct, bct, xb, b_xb, xT, b_xT, st1, b_st1 = f1ctx[t]
                        if part == "cast":
                            S.op("dve", lambda e: e.tensor_copy(out=xb[:], in_=xt[:]), reads=[bx], writes=[b_xb])
                            return
                        for c in range(8):
                            S.op("pe", lambda e, c=c: e.transpose(out=pTx[:, c, :], in_=xb[:, c * 128:(c + 1) * 128], identity=ident[:]),
                                 reads=[b_xb, b_c], writes=[b_pTx], sig=(c == 7))
                        for c in range(8):
                            S.op("dve", lambda e, c=c: e.tensor_scalar(out=xT[:, c, :], in0=pTx[:, c, :], scalar1=gn[:, c:c + 1], scalar2=None, op0=ALU.mult),
                                 reads=[b_pTx, b_c], writes=[b_xT], sig=(c == 7))

                    def f2(t):
                        xt, bx, ct, bct, xb, b_xb, xT, b_xT, st1, b_st1 = f1ctx.pop(t)
                        halo = t < 16
                        rstd = st1[:, 2:3]

                        def proj(col0):
                            pzt, bpz = pz.next()
                            for c in range(8):
                                S.op("pe", lambda e, c=c: e.matmul(pzt[:], lhsT=xT[:, c, :], rhs=winb[:, c, col0:col0 + 512],
                                                                   start=(c == 0), stop=(c == 7)),
                                     reads=[b_xT, b_wblk[col0 // 512][c]], writes=[bpz], sig=(c == 7))
                            return pzt, bpz

                        qkf, bqk = qkf_r.next()
                        vf, bvf = vf_r.next()
                        ub, b_ub = ub_r.next()
                        vcf, b_vcf = vcf_r.next()
                        if not halo:
                            pq, bpq = proj(0)
                            S.op("act", lambda e: e.activation(out=qkf[:, 0:512], in_=pq[:], func=AF.Copy, scale=rstd),
                                 reads=[bpq, b_st1], writes=[bqk])
                        pk, bpk = proj(512)
                        S.op("act", lambda e: e.activation(out=qkf[:, 512:1024], in_=pk[:], func=AF.Copy, scale=rstd),
                             reads=[bpk, b_st1], writes=[bqk])
                        pv, bpv = proj(1024)
                        S.op("act", lambda e: e.activation(out=vf[:], in_=pv[:], func=AF.Copy, scale=rstd),
                             reads=[bpv, b_st1], writes=[bvf])
                        if not halo:
                            pu, bpu = proj(1536)
                            S.op("act", lambda e: e.activation(out=ub[:], in_=pu[:], func=AF.Gelu, scale=rstd),
                                 reads=[bpu, b_st1], writes=[b_ub])
                            pc, bpc = proj(2048)
                            S.op("act", lambda e: e.activation(out=vcf[:], in_=pc[:], func=AF.Gelu, scale=rstd),
                                 reads=[bpc, b_st1], writes=[b_vcf])
                        ctxs[t] = (ct, bct, qkf, bqk, vf, bvf, ub, b_ub, vcf, b_vcf)

                    def lna(t, part):
                        if t < 16:
                            return
                        vcf, b_vcf = ctxs[t][8], ctxs[t][9]
                        if part == "stats":
                            S.op("dve", lambda e: e.bn_stats(out=bnst[:], in_=vcf[:]), reads=[b_vcf], writes=[b_st2])
                            S.op("dve", lambda e: e.bn_aggr(out=bnag[:], in_=bnst[:]), reads=[b_st2], writes=[b_st2])
                            S.op("act", lambda e: e.activation(out=st2[:, 3:4], in_=bnag[:, 1:2], func=AF.Sqrt, scale=1.0, bias=EPS),
                                 reads=[b_st2], writes=[b_st2])
                        else:
                            S.op("dve", lambda e: e.reciprocal(out=st2[:, 4:5], in_=st2[:, 3:4]), reads=[b_st2], writes=[b_st2])

                    bctx = {}

                    def tileA_back(t, part):
                        ct, bct, qkf, bqk, vf, bvf, ub, b_ub, vcf, b_vcf = ctxs[t]
                        halo = t < 16
                        samp = t == 32
                        h0 = 8 if halo else 0
                        nh = 16 - h0
                        if part == "a":
                            v3 = qkf[:].rearrange("p (h d) -> p h d", d=64)
                            x1 = v3[:, h0:16, 0:8]
                            x2 = v3[:, h0:16, 8:16]
                            cosb = ct[:, 0:8].unsqueeze(1).broadcast_to([128, nh, 8])
                            sinb = ct[:, 8:16].unsqueeze(1).broadcast_to([128, nh, 8])
                            tm = [rtmp[:, i, h0:16, :] for i in range(4)]
                            S.op("dve", lambda e: e.tensor_tensor(out=tm[0], in0=x1, in1=cosb, op=ALU.mult), reads=[bqk, bct], writes=[b_rtmp])
                            S.op("dve", lambda e: e.tensor_tensor(out=tm[1], in0=x2, in1=sinb, op=ALU.mult), reads=[bqk, bct], writes=[b_rtmp])
                            S.op("dve", lambda e: e.tensor_tensor(out=tm[2], in0=x2, in1=cosb, op=ALU.mult), reads=[bqk, bct], writes=[b_rtmp])
                            S.op("dve", lambda e: e.tensor_tensor(out=tm[3], in0=x1, in1=sinb, op=ALU.mult), reads=[bqk, bct], writes=[b_rtmp])
                            S.op("dve", lambda e: e.tensor_tensor(out=x1, in0=tm[0], in1=tm[1], op=ALU.subtract), reads=[b_rtmp], writes=[bqk])
                            S.op("dve", lambda e: e.tensor_tensor(out=x2, in0=tm[2], in1=tm[3], op=ALU.add), reads=[b_rtmp], writes=[bqk])
                            if not halo:
                                ot = t - 16
                                S.dma(nk_o[ot], qkf[:, 512:1024], reads=[bqk], writes=[b_nk])
                                S.dma(nv_o[ot], vf[:], reads=[bvf], writes=[b_nv])
                            if samp:
                                S.dma(qsc, qkf[0:16, 0:512], reads=[bqk], writes=[b_qsc])
                            else:
                                vp, bvp = vpad_r.next()
                                vp4 = vp[:].rearrange("p (c hh) n -> p c hh n", hh=2)
                                vf4 = vf[:].rearrange("p (c hh d) -> p c hh d", hh=2, d=64)
                                for hh in range(2):
                                    S.op("pool", lambda e, hh=hh: e.tensor_copy(out=vp4[:, :, hh, hh * 64:(hh + 1) * 64], in_=vf4[:, :, hh, :]),
                                         reads=[bvf], writes=[bvp])
                                S.dma(vsc[t * 128:(t + 1) * 128, :], vp[:].rearrange("p h n -> p (h n)"), reads=[bvp], writes=[b_vsc])
                                S.op("pool" if halo else "dve", lambda e: e.tensor_copy(out=qkb[:, h0 * 64:1024], in_=qkf[:, h0 * 64:1024]),
                                     reads=[bqk], writes=[b_qkb])
                            return
                        if part == "b":
                            if samp:
                                return
                            j0 = 4 if halo else 0
                            for j in range(j0, 8):
                                S.op("pe", lambda e, j=j: e.transpose(out=pT[:, j, :], in_=qkb[:, j * 128:(j + 1) * 128], identity=ident[:]),
                                     reads=[b_qkb, b_c], writes=[b_pT], sig=(j == 7))
                            S.op("dve", lambda e: e.tensor_copy(out=kT[:, :, t * 128:(t + 1) * 128], in_=pT[:, 4:8, :]), reads=[b_pT], writes=[b_out])
                            if not halo:
                                S.op("dve", lambda e: e.tensor_copy(out=qT[:, :, (t - 16) * 128:(t - 15) * 128], in_=pT[:, 0:4, :]),
                                     reads=[b_pT], writes=[b_out])
                            return
                        if halo:
                            return
                        ws_i = 1 if samp else 0
                        if part == "c":
                            S.op("dve", lambda e: e.tensor_scalar(out=vnf[:], in0=vcf[:], scalar1=bnag[:, 0:1], scalar2=st2[:, 4:5],
                                                                 op0=ALU.subtract, op1=ALU.mult), reads=[b_vcf, b_st2], writes=[b_vnf])
                            S.op("pool", lambda e: e.tensor_tensor(out=vnf[:], in0=vnf[:], in1=lng[:, 0:512], op=ALU.mult), reads=[b_vnf, b_k1], writes=[b_vnf])
                            if samp:
                                S.op("pool", lambda e: e.tensor_tensor(out=vnf[:], in0=vnf[:], in1=lng[:, 512:1024], op=ALU.add), reads=[b_vnf, b_k1], writes=[b_vnf])
                                S.op("pool", lambda e: e.tensor_copy(out=vnb[:], in_=vnf[:]), reads=[b_vnf], writes=[b_vnb])
                                S.dma(nvc_o, vnf[:], reads=[b_vnf], writes=[b_out])
                            else:
                                S.op("pool", lambda e: e.tensor_tensor(out=vnb[:], in0=vnf[:], in1=lng[:, 512:1024], op=ALU.add), reads=[b_vnf, b_k1], writes=[b_vnb])
                            return
                        if part == "d":
                            for g in range(8):
                                S.op("pe", lambda e, g=g: e.matmul(pmix[:, g * 64:(g + 1) * 64], lhsT=wsb[:, ws_i, g * 128:(g + 1) * 128],
                                                                   rhs=vnb[:, g * 64:(g + 1) * 64], start=True, stop=True),
                                     reads=[b_vnb, b_k1], writes=[b_pmix], sig=(g == 7))
                            S.op("dve", lambda e: e.tensor_tensor(out=gtmp[:].rearrange("p (g c) -> p g c", g=8),
                                                                 in0=pmix[:].rearrange("p (g c) -> p g c", g=8),
                                                                 in1=bsb[:, ws_i, :].unsqueeze(2).broadcast_to([128, 8, 64]), op=ALU.add),
                                 reads=[b_pmix, b_k1], writes=[b_gtmp])
                            S.op("dve", lambda e: e.tensor_tensor(out=gb[:], in0=gtmp[:], in1=ub[:], op=ALU.mult), reads=[b_gtmp, b_ub], writes=[b_gb])
                            return
                        if part == "e":
                            for j in range(4):
                                S.op("pe", lambda e, j=j: e.transpose(out=pT[:, j, :], in_=gb[:, j * 128:(j + 1) * 128], identity=ident[:]),
                                     reads=[b_gb, b_c], writes=[b_pT], sig=(j == 3))
                            if samp:
                                S.op("dve", lambda e: e.tensor_copy(out=mixTs[:, 4:8, :], in_=pT[:, 0:4, :]), reads=[b_pT], writes=[b_out])
                            else:
                                S.op("dve", lambda e: e.tensor_copy(out=mixT[:, 4:8, (t - 16) * 128:(t - 15) * 128], in_=pT[:, 0:4, :]),
                                     reads=[b_pT], writes=[b_out])

                    state = {"first": True}

                    sctx = {}

                    def sample_unit(bt, part):
                        b, tt = bt // 4, bt % 4
                        specs = []
                        for g, d in enumerate((1, 4, 16, 0)):
                            specs.append((g, d, 1 if d == 0 else 128))
                        if part == "kdma":
                            qb, bqb = qbc_r.next()
                            S.dma(qb[:], qsc[bt:bt + 1, :].partition_broadcast(128), reads=[b_qsc], writes=[bqb])
                            kts = []
                            for (g, d, np_) in specs:
                                ktile, bkt = kt_r.next()
                                if d == 0:
                                    S.dma(ktile[0:1, :], nk_o[16, bt:bt + 1, :], reads=[b_nk], writes=[bkt])
                                else:
                                    r0 = 2048 + tt - 128 * d
                                    if d == 1 and tt > 0:
                                        nc_ = 128 - tt
                                        S.dma(ktile[0:nc_, :], ck[b, r0:2048, :], writes=[bkt])
                                        S.dma(ktile[nc_:128, :], nk_o[16, 4 * b:4 * b + tt, :], reads=[b_nk], writes=[bkt])
                                    else:
                                        S.dma(ktile[:], ck[b, r0:r0 + 127 * d + 1:d, :], writes=[bkt])
                                kts.append((ktile, bkt))
                            sctx[bt] = dict(qb=qb, bqb=bqb, kts=kts)
                            return
                        c_ = sctx[bt]
                        if part == "vdma":
                            vts = []
                            for (g, d, np_) in specs:
                                vtile, bvt = vt_r.next()
                                if d == 0:
                                    S.dma(vtile[0:1, :], nv_o[16, bt:bt + 1, :], reads=[b_nv], writes=[bvt])
                                else:
                                    r0 = 2048 + tt - 128 * d
                                    if d == 1 and tt > 0:
                                        nc_ = 128 - tt
                                        S.dma(vtile[0:nc_, :], cv[b, r0:2048, :], writes=[bvt])
                                        S.dma(vtile[nc_:128, :], nv_o[16, 4 * b:4 * b + tt, :], reads=[b_nv], writes=[bvt])
                                    else:
                                        S.dma(vtile[:], cv[b, r0:r0 + 127 * d + 1:d, :], writes=[bvt])
                                vts.append((vtile, bvt))
                            c_["vts"] = vts
                            return
                        if part == "s1":
                            qb, bqb = c_["qb"], c_["bqb"]
                            sc, bsc = ssc_r.next()
                            pe_, bpe = pex_r.next()

                            def score(g, np_, ktile, bkt):
                                S.op("dve", lambda e: e.tensor_tensor(out=prod[0:np_, :], in0=ktile[0:np_, :], in1=qb[0:np_, :], op=ALU.mult),
                                     reads=[bkt, bqb], writes=[b_prod])
                                S.op("dve", lambda e: e.tensor_reduce(out=sc[0:np_, g * 8:(g + 1) * 8], in_=prod[0:np_, :].rearrange("p (h d) -> p h d", h=8),
                                                                     axis=AX.X, op=ALU.add), reads=[b_prod], writes=[bsc])
                            for (g, d, np_), (ktile, bkt) in zip(specs, c_["kts"]):
                                score(g, np_, ktile, bkt)
                            S.op("act", lambda e: e.activation(out=pe_[:], in_=sc[:], func=AF.Exp, scale=0.125), reads=[bsc], writes=[bpe])
                            S.op("dve", lambda e: e.tensor_scalar(out=pe_[0:1, 24:32], in0=pe_[0:1, 24:32], scalar1=3.0, scalar2=None, op0=ALU.mult),
                                 reads=[bpe], writes=[bpe])
                            c_["pe"], c_["bpe"] = pe_, bpe
                            return
                        pe_, bpe = c_["pe"], c_["bpe"]

                        def pv(g, np_, vtile, bvt):
                            S.op("pe", lambda e: e.matmul(pacc[:], lhsT=pe_[0:np_, g * 8:(g + 1) * 8], rhs=vtile[0:np_, :], start=(g == 0), stop=(g == 3)),
                                 reads=[bpe, bvt], writes=[b_pacc])
                            S.op("pe", lambda e: e.matmul(pden[:, 0:1], lhsT=pe_[0:np_, g * 8:(g + 1) * 8], rhs=onesc[0:np_, :], start=(g == 0), stop=(g == 3)),
                                 reads=[bpe, b_k1], writes=[b_pden])
                        for (g, d, np_), (vtile, bvt) in zip(specs, c_["vts"]):
                            pv(g, np_, vtile, bvt)
                        S.op("dve", lambda e: e.reciprocal(out=rden[:], in_=pden[:, 0:1]), reads=[b_pden], writes=[b_rden])
                        S.op("dve", lambda e: e.scalar_tensor_tensor(out=msk8[:], in0=pacc[:], scalar=rden[:], in1=dgm[:],
                                                                    op0=ALU.mult, op1=ALU.mult),
                             reads=[b_pacc, b_rden, b_k1], writes=[b_msk8])
                        S.op("pe", lambda e: e.matmul(prow[:], lhsT=sel[:, bt * 16:(bt + 1) * 16], rhs=msk8[:],
                                                      start=(bt == 0), stop=(bt == 15)),
                             reads=[b_msk8, b_k1], writes=[b_prow])
                        sctx.pop(bt)

                    def sample_finish():
                        S.op("dve", lambda e: e.tensor_copy(out=attb[0:16, :], in_=prow[:]), reads=[b_prow], writes=[b_attb])
                        for j in range(4):
                            S.op("pe", lambda e, j=j: e.transpose(out=pT[:, j, :], in_=attb[:, j * 128:(j + 1) * 128], identity=ident[:]),
                                 reads=[b_attb, b_c], writes=[b_pT], sig=(j == 3))
                        S.op("act", lambda e: e.activation(out=mixTs[:, 0:4, :], in_=pT[:, 0:4, :], func=AF.Copy), reads=[b_pT], writes=[b_out])

                    NO = len(order)
                    for k in range(3):
                        issue_load(order[k])
                    for k in range(3):
                        f1a(order[k], "act")
                    f1a(order[0], "recip"); f1a(order[1], "recip")
                    f1b(order[0], "cast"); f1b(order[0], "T"); f2(order[0])
                    f1b(order[1], "cast"); f1b(order[1], "T")
                    lna(order[0], "stats")
                    f1b(order[2], "cast")
                    issue_load(order[3])
                    for i, t in enumerate(order):
                        if i + 4 < NO:
                            issue_load(order[i + 4])
                        if 1 <= i <= 16:
                            sample_unit(i - 1, "kdma")
                        if i + 2 < NO:
                            f1b(order[i + 2], "T")
                        if i >= 1:
                            tileA_back(order[i - 1], "e")
                            ctxs.pop(order[i - 1])
                        if i + 1 < NO:
                            f2(order[i + 1])
                        if i + 2 < NO:
                            f1a(order[i + 2], "recip")
                        tileA_back(t, "a")
                        lna(t, "recip")
                        tileA_back(t, "c")
                        tileA_back(t, "b")
                        tileA_back(t, "d")
                        if 2 <= i <= 17:
                            sample_unit(i - 2, "s2")
                        if 1 <= i <= 16:
                            sample_unit(i - 1, "vdma")
                        wq_issue(2)
                        if i + 3 < NO:
                            f1a(order[i + 3], "act")
                            f1b(order[i + 3], "cast")
                        if 1 <= i <= 16:
                            sample_unit(i - 1, "s1")
                        if i + 1 < NO:
                            lna(order[i + 1], "stats")
                        if i == 17:
                            sample_finish()
                    tileA_back(order[-1], "e")
                    ctxs.pop(order[-1])
                    wq_issue(100)
                    S.barrier()
                    if KSTOP <= 1:
                        S.emit(); raise _Stop(nc)

                with ExitStack() as s2:
                    mk_f = sbt(s2, "mk_f", [128, 2, 512], F32)
                    mk = sbt(s2, "mk", [128, 2, 512], BF16)
                    onp_f = sbt(s2, "onp_f", [128, 256], F32)
                    onp = sbt(s2, "onp", [128, 2, 128], BF16)
                    accs = [(sbt(s2, f"accN{i}", [128, SH], F32), sbt(s2, f"accD{i}", [128, SH], F32), Buf(), Buf()) for i in range(2)]
                    PT_r = Ring([sbt(s2, f"PT{i}", [128, 512], BF16) for i in range(4)])
                    V_r = Ring([sbt(s2, f"Vt{i}", [128, 2, 2, 128], BF16) for i in range(6)])
                    ST_r = Ring([pst(s2, f"ST{i}", [128, 2, 512]) for i in range(3)])
                    NUM_r = Ring([pst(s2, f"NUM{i}", [128, 512]) for i in range(1)])
                    DEN_r = Ring([pst(s2, f"DEN{i}", [128, 512]) for i in range(1)])
                    b_k2 = Buf()
                    S.dma(mk_f[:], amask.rearrange("t p n -> p t n"), writes=[b_k2])
                    S.dma(onp_f[:], onesp, writes=[b_k2])
                    S.op("dve", lambda e: e.tensor_copy(out=mk[:], in_=mk_f[:]), reads=[b_k2], writes=[b_k2])
                    S.op("dve", lambda e: e.tensor_copy(out=onp[:].rearrange("p h n -> p (h n)"), in_=onp_f[:]), reads=[b_k2], writes=[b_k2])

                    def blocks_of(d, G):
                        if d == 1:
                            return [128 * (4 * G + j) for j in range(4)]
                        if d == 4:
                            return [512 * G + r for r in range(4)]
                        return [4 * G + r for r in range(4)]

                    def acc_view(acc, d, G):
                        if d == 1:
                            return acc[:, 512 * G:512 * (G + 1)].rearrange("p (j i) -> p j i", j=4)
                        if d == 4:
                            return acc[:, 512 * G:512 * (G + 1)].rearrange("p (i r) -> p r i", r=4)
                        return acc[:].rearrange("p (i r) -> p r i", r=16)[:, 4 * G:4 * G + 4, :]

                    DILS = tuple(int(v) for v in os.environ.get("KDILS", "1,4,16").split(","))
                    blks = []
                    for c in range(4):
                        for di, d in enumerate(DILS):
                            for G in range(4):
                                for j, q0 in enumerate(blocks_of(d, G)):
                                    blks.append(dict(c=c, di=di, d=d, G=G, j=j, q0=q0))
                    nblk = len(blks)

                    def att_vload(B):
                        d, c = B["d"], B["c"]
                        kc0 = 2048 + B["q0"]
                        kp0 = kc0 - 128 * d
                        Vt, bV = V_r.next()
                        for kt, k0 in enumerate((kp0, kc0)):
                            S.dma(Vt[:, kt, :, :].rearrange("p h n -> p (h n)"),
                                  vsc[k0:k0 + 127 * d + 1:d, 256 * c:256 * (c + 1)], writes=[bV])
                        B["Vt"], B["bV"] = Vt, bV

                    def att_front(B, idx):
                        d, c, q0 = B["d"], B["c"], B["q0"]
                        kc0 = 2048 + q0
                        kp0 = kc0 - 128 * d
                        STp, bS = ST_r.next()
                        n = 0
                        for hh in range(2):
                            for kt, k0 in enumerate((kp0, kc0)):
                                n += 1
                                S.op("pe", lambda e, hh=hh, kt=kt, k0=k0: e.matmul(
                                    STp[:, hh, kt * 128:(kt + 1) * 128],
                                    lhsT=kT[hh * 64:(hh + 1) * 64, c, k0:k0 + 127 * d + 1:d],
                                    rhs=qT[hh * 64:(hh + 1) * 64, c, q0:q0 + 127 * d + 1:d], start=True, stop=True),
                                    writes=[bS], sig=(n == 4))
                        PT, bP = PT_r.next()
                        S.op("act", lambda e: e.activation(out=PT[:].rearrange("p (h x) -> p h x", h=2), in_=STp[:, :, 0:256], func=AF.Exp, scale=0.125),
                             reads=[bS], writes=[bP])
                        mi = 1 if kp0 < 2048 else 0
                        S.op("dve" if idx % 4 != 3 else "pool", lambda e: e.tensor_tensor(out=PT[:], in0=PT[:], in1=mk[:, mi, :], op=ALU.mult),
                             reads=[bP, b_k2], writes=[bP])
                        B["PT"], B["bP"] = PT, bP

                    cur = {}

                    def att_back(B):
                        c, di, d, G, j = B["c"], B["di"], B["d"], B["G"], B["j"]
                        PT, bP, Vt, bV = B["PT"], B["bP"], B["Vt"], B["bV"]
                        if j == 0:
                            cur["NUM"], cur["bN"] = NUM_r.next()
                            cur["DEN"], cur["bD"] = DEN_r.next()
                        NUM, bN, DEN, bD = cur["NUM"], cur["bN"], cur["DEN"], cur["bD"]
                        n = 0
                        for hh in range(2):
                            for kt in range(2):
                                n += 1
                                S.op("pe", lambda e, hh=hh, kt=kt, n=n: e.matmul(
                                    NUM[:, j * 128:(j + 1) * 128], lhsT=Vt[:, kt, hh, :],
                                    rhs=PT[:, (hh * 2 + kt) * 128:(hh * 2 + kt + 1) * 128], start=(n == 1), stop=(n == 4)),
                                    reads=[bP, bV], writes=[bN], sig=False)
                        n = 0
                        for hh in range(2):
                            for kt in range(2):
                                n += 1
                                S.op("pe", lambda e, hh=hh, kt=kt, n=n: e.matmul(
                                    DEN[:, j * 128:(j + 1) * 128], lhsT=onp[:, hh, :],
                                    rhs=PT[:, (hh * 2 + kt) * 128:(hh * 2 + kt + 1) * 128], start=(n == 1), stop=(n == 4)),
                                    reads=[bP, bV, b_k2], writes=[bN, bD], sig=(n == 4))
                        if j != 3:
                            return
                        aN, aD, baN, baD = accs[c % 2]
                        nv_ = acc_view(aN, d, G)
                        dv_ = acc_view(aD, d, G)
                        N3 = NUM[:].rearrange("p (j i) -> p j i", j=4)
                        D3 = DEN[:].rearrange("p (j i) -> p j i", j=4)
                        if di == 0:
                            S.op("act", lambda e: e.activation(out=nv_, in_=N3, func=AF.Copy), reads=[bN], writes=[baN])
                            S.op("dve", lambda e: e.tensor_copy(out=dv_, in_=D3), reads=[bD], writes=[baD])
                        else:
                            S.op("dve", lambda e: e.tensor_tensor(out=nv_, in0=N3, in1=nv_, op=ALU.add), reads=[bN, baN], writes=[baN])
                            S.op("dve", lambda e: e.tensor_tensor(out=dv_, in0=D3, in1=dv_, op=ALU.add), reads=[bD, baD], writes=[baD])
                        if di == len(DILS) - 1 and G == 3:
                            for G2 in range(4):
                                att_final(c, G2)

                    def att_final(c, G):
                        aN, aD, baN, baD = accs[c % 2]
                        sl = slice(512 * G, 512 * (G + 1))
                        S.op("dve", lambda e: e.reciprocal(out=aD[:, sl], in_=aD[:, sl]), reads=[baD], writes=[baD])
                        S.op("pool", lambda e: e.tensor_tensor(out=mixT[:, c, sl], in0=aN[:, sl], in1=aD[:, sl], op=ALU.mult),
                             reads=[baN, baD], writes=[baN, baD])

                    for k in range(min(4, nblk)):
                        att_vload(blks[k])
                    att_front(blks[0], 0)
                    if nblk > 1:
                        att_front(blks[1], 1)
                    for idx in range(nblk):
                        if idx + 4 < nblk:
                            att_vload(blks[idx + 4])
                        if idx + 2 < nblk:
                            att_front(blks[idx + 2], idx + 2)
                        att_back(blks[idx])
                    S.barrier()
                    if KSTOP <= 2:
                        S.emit(); raise _Stop(nc)

            with ExitStack() as s3:
                wdn = sbt(s3, "wdn", [128, 32, 1024], BF16)
                wo = sbt(s3, "wo", [128, 8, 1024], BF16)
                wg = sbt(s3, "wg", [128, 8, 1024], BF16)
                wp = sbt(s3, "wp", [128, 2, 1024], BF16)
                fgb = sbt(s3, "fgb", [128, 1024], F32)
                wu_r = Ring([sbt(s3, f"wu{i}", [128, 8, 512], BF16) for i in range(2)])
                xh_r = Ring([sbt(s3, f"xh{i}", [128, D], F32) for i in range(4)])
                pt_r = Ring([sbt(s3, f"ptl{i}", [128, 256], F32) for i in range(4)])
                hb = sbt(s3, "hb", [128, D], BF16); b_hb = Buf()
                junk2 = hb; b_junk2 = b_hb
                hT = sbt(s3, "hT", [128, 8, 256], BF16); b_hT = Buf()
                h2T = sbt(s3, "h2T", [128, 8, 128], BF16); b_h2T = Buf()
                actT = sbt(s3, "actT", [128, 32, 256], BF16); b_actT = Buf()
                rl_r = Ring([sbt(s3, f"rl{i}", [128, 256], F32) for i in range(2)])
                gate_r = Ring([sbt(s3, f"gate{i}", [128, 512], F32) for i in range(1)])
                pb16 = sbt(s3, "pb16", [128, 256], BF16); b_pb16 = Buf()
                pTp = sbt(s3, "pTp", [128, 2, 128], BF16); b_pTp = Buf()
                stt_all = [sbt(s3, f"stt{i}", [128, 2, 16], F32) for i in range(2)]; cur_stt = [stt_all[0]]
                pbk = Ring([pst(s3, f"pb{i}", [128, 512]) for i in range(3)])
                ps1 = Ring([pst(s3, "ps1", [128, 512])])
                pT3 = pst(s3, "pT3", [128, 8, 128], BF16); b_pT3 = Buf()
                hb2 = sbt(s3, "hb2", [128, D], BF16); b_hb2 = Buf()
                pa_r = Ring([pst(s3, f"pa{i}", [128, 512]) for i in range(2)])
                pT2 = pst(s3, "pT2", [128, 8, 128], BF16); b_pT2 = Buf()
                b_w = Buf(); b_yo = Buf()
                b_wo = Buf(); b_wg = Buf(); b_wp = Buf(); b_wd = Buf()
                S.dma(wo[:], wo_s, reads=[b_wsc], writes=[b_wo])
                S.dma(fgb[:], rowv[0:1, :].partition_broadcast(128), writes=[b_w])
                for j in range(4):
                    S.dma(wdn[:, 8 * j:8 * j + 8, :], wd_s[:, 8 * j:8 * j + 8, :], reads=[b_wsc], writes=[b_wd])
                S.dma(wg[:], wg_s, reads=[b_wsc], writes=[b_wg])
                S.dma(wp[:], wp_s, reads=[b_wsc], writes=[b_wp])

                def stats_and_T(b_st, xh, bxh, si, col, dstT, b_dstT, ncol_off, pTb=None, b_pTb=None, hbb=None, b_hbb=None, gcol0=0):
                    stt = cur_stt[0]
                    pTb = pT2 if pTb is None else pTb
                    b_pTb = b_pT2 if b_pTb is None else b_pTb
                    hbb = hb if hbb is None else hbb
                    b_hbb = b_hb if b_hbb is None else b_hbb
                    S.op("act", lambda e: e.activation(out=hbb[:], in_=xh[:], func=AF.Square, accum_out=stt[:, si, col:col + 1]),
                         reads=[bxh], writes=[b_hbb, b_st])
                    S.op("act", lambda e: e.activation(out=stt[:, si, col + 1:col + 2], in_=stt[:, si, col:col + 1], func=AF.Sqrt,
                                                       scale=1.0 / D, bias=EPS), reads=[b_st], writes=[b_st])
                    S.op("dve", lambda e: e.reciprocal(out=stt[:, si, col + 2:col + 3], in_=stt[:, si, col + 1:col + 2]), reads=[b_st], writes=[b_st])
                    if dstT is None:
                        return
                    S.op("act", lambda e: e.activation(out=hbb[:], in_=xh[:], func=AF.Copy), reads=[bxh], writes=[b_hbb])
                    for c in range(8):
                        S.op("pe", lambda e, c=c: e.transpose(out=pTb[:, c, :], in_=hbb[:, c * 128:(c + 1) * 128], identity=ident[:]),
                             reads=[b_hbb, b_c], writes=[b_pTb], sig=(c == 7))
                    for c in range(8):
                        S.op("dve", lambda e, c=c: e.tensor_scalar(out=dstT[:, c, ncol_off:ncol_off + 128], in0=pTb[:, c, :],
                                                                  scalar1=gn[:, gcol0 + c:gcol0 + c + 1], scalar2=None, op0=ALU.mult),
                             reads=[b_pTb, b_c], writes=[b_dstT], sig=(c == 7))

                def mm_group(out_ap, bout, pairs, reads):
                    n = len(pairs)
                    for i, (l, r) in enumerate(pairs):
                        S.op("pe", lambda e, l=l, r=r, i=i: e.matmul(out_ap, lhsT=l, rhs=r, start=(i == 0), stop=(i == n - 1)),
                             reads=reads, writes=[bout], sig=(i == n - 1))

                def stage1(b_st, xh, bxh, mc, si):
                    stt = cur_stt[0]
                    for hf in range(2):
                        pb_, bpb = ps1.next()
                        mm_group(pb_[:], bpb, [(mc[:, fc, :], wo[:, fc, hf * 512:(hf + 1) * 512]) for fc in range(8)], [b_wo])
                        xs = xh[:, hf * 512:(hf + 1) * 512]
                        S.op("dve", lambda e, pb_=pb_, xs=xs: e.tensor_tensor(out=xs, in0=pb_[:], in1=xs, op=ALU.add),
                             reads=[bpb, bxh], writes=[bxh])
                    stats_and_T(b_st, xh, bxh, si, 0, hT, b_hT, si * 128, pT3, b_pT3, hb2, b_hb2, gcol0=8)
                    S.op("dve", lambda e: e.tensor_tensor(out=stt[:, si, 3:4], in0=stt[:, si, 2:3], in1=stt[:, si, 2:3], op=ALU.mult),
                         reads=[b_st], writes=[b_st])

                wu_pending = []

                def wu_load(u):
                    wu, bwu = wu_r.next()
                    S.dma(wu[:], wu_s[u], reads=[b_wsc], writes=[bwu])
                    wu_pending.append((wu, bwu))

                def stage2_unit(u, T):
                    wu, bwu = wu_pending.pop(0)
                    for f4 in range(4):
                        ffc = 4 * u + f4
                        pa, bpa = pa_r.next()
                        mm_group(pa[:, 0:T], bpa, [(wu[:, kc, f4 * 128:(f4 + 1) * 128], hT[:, kc, 0:T]) for kc in range(8)], [bwu, b_hT])
                        rl, brl = rl_r.next()
                        S.op("act", lambda e, rl=rl, pa=pa: e.activation(out=rl[:, 0:T], in_=pa[:, 0:T], func=AF.Relu), reads=[bpa], writes=[brl])
                        S.op("pool" if ffc % 2 else "dve",
                             lambda e, rl=rl, ffc=ffc: e.tensor_tensor(out=actT[:, ffc, 0:T], in0=rl[:, 0:T], in1=rl[:, 0:T], op=ALU.mult),
                             reads=[brl], writes=[b_actT])

                def stage34(b_st, xh, bxh, ptl, bpt, si, s):
                    stt = cur_stt[0]
                    for hf in range(2):
                        pb_, bpb = pbk.next()
                        mm_group(pb_[:], bpb, [(actT[:, ffc, si * 128:(si + 1) * 128], wdn[:, ffc, hf * 512:(hf + 1) * 512]) for ffc in range(32)],
                                 [b_actT, b_wd])
                        xs = xh[:, hf * 512:(hf + 1) * 512]
                        S.op("dve", lambda e, pb_=pb_, xs=xs: e.scalar_tensor_tensor(out=xs, in0=pb_[:], scalar=stt[:, si, 3:4], in1=xs,
                                                                                    op0=ALU.mult, op1=ALU.add),
                             reads=[bpb, bxh, b_st], writes=[bxh])
                    stats_and_T(b_st, xh, bxh, si, 4, h2T, b_h2T, 0, gcol0=16)
                    S.op("pool", lambda e: e.tensor_copy(out=pb16[:], in_=ptl[:]), reads=[bpt], writes=[b_pb16])
                    for c in range(2):
                        S.op("pe", lambda e, c=c: e.transpose(out=pT2[:, c, :], in_=pb16[:, c * 128:(c + 1) * 128], identity=ident[:]),
                             reads=[b_pb16, b_c], writes=[b_pT2], sig=(c == 1))
                    S.op("dve", lambda e: e.tensor_copy(out=pTp[:], in_=pT2[:, 0:2, :]), reads=[b_pT2], writes=[b_pTp])
                    for hf in range(2):
                        pg, bpg = pbk.next()
                        mm_group(pg[:], bpg, [(h2T[:, kc, :], wg[:, kc, hf * 512:(hf + 1) * 512]) for kc in range(8)], [b_h2T, b_wg])
                        gt, bgt = gate_r.next()
                        S.op("act", lambda e, gt=gt, pg=pg: e.activation(out=gt[:], in_=pg[:], func=AF.Sigmoid, scale=stt[:, si, 6:7]),
                             reads=[bpg, b_st], writes=[bgt])
                        pp, bpp = pbk.next()
                        mm_group(pp[:], bpp, [(pTp[:, kc, :], wp[:, kc, hf * 512:(hf + 1) * 512]) for kc in range(2)], [b_pTp, b_wp])
                        xs = xh[:, hf * 512:(hf + 1) * 512]
                        S.op("dve", lambda e, gt=gt, pp=pp: e.tensor_tensor(out=gt[:], in0=pp[:], in1=gt[:], op=ALU.mult), reads=[bpp, bgt], writes=[bgt])
                        S.op("dve", lambda e, gt=gt, xs=xs: e.tensor_tensor(out=xs, in0=gt[:], in1=xs, op=ALU.add), reads=[bgt, bxh], writes=[bxh])
                    stats_and_T(b_st, xh, bxh, si, 8, None, None, 0)
                    S.op("dve", lambda e: e.scalar_tensor_tensor(out=xh[:], in0=xh[:], scalar=stt[:, si, 10:11], in1=fgb[:], op0=ALU.mult, op1=ALU.mult),
                         reads=[bxh, b_st, b_w], writes=[bxh])
                    S.dma(y_o[s], xh[:], reads=[bxh], writes=[b_yo])

                passes = [([2 * i, 2 * i + 1], False) for i in range(8)] + [([16], True)]
                pinfo = {}

                def pass_loads(p):
                    subs, samp = passes[p]
                    tiles = []
                    for si, s in enumerate(subs):
                        xh, bxh = xh_r.next()
                        ptl, bpt = pt_r.next()
                        S.dma(xh[:], xall[32 if samp else 16 + s], writes=[bxh])
                        S.dma(ptl[:], pall[s], writes=[bpt])
                        tiles.append((xh, bxh, ptl, bpt))
                    pinfo[p] = (tiles, Buf(), stt_all[p % 2])

                def pass_stage1(p):
                    subs, samp = passes[p]
                    tiles, b_st, sttp = pinfo[p]
                    cur_stt[0] = sttp
                    for si, s in enumerate(subs):
                        xh, bxh, ptl, bpt = tiles[si]
                        mc = mixTs[:, :, :] if samp else mixT[:, :, s * 128:(s + 1) * 128]
                        stage1(b_st, xh, bxh, mc, si)

                def pass_stage34(p):
                    subs, samp = passes[p]
                    tiles, b_st, sttp = pinfo[p]
                    cur_stt[0] = sttp
                    for si, s in enumerate(subs):
                        xh, bxh, ptl, bpt = tiles[si]
                        stage34(b_st, xh, bxh, ptl, bpt, si, s)

                pass_loads(0)
                wu_load(0); wu_load(1)
                pass_stage1(0)
                for p in range(len(passes)):
                    subs, samp = passes[p]
                    T = 128 * len(subs)
                    if p + 1 < len(passes):
                        pass_loads(p + 1)
                    for u in range(8):
                        stage2_unit(u, T)
                        g_next = p * 8 + u + 2
                        if g_next < 8 * len(passes):
                            wu_load(g_next % 8)
                    streams = [S.capture(pass_stage34, p)]
                    if p + 1 < len(passes):
                        streams.append(S.capture(pass_stage1, p + 1))
                    S.replay(streams)
                S.barrier()
        except ZeroDivisionError:
            pass
        S.emit()
    return nc


_PROGRAM = None


def _rope_tables(pos):
    pos = np.asarray(pos, dtype=np.float32)
    inv = (np.float32(500000.0) ** (-np.arange(0, 16, 2, dtype=np.float32) / np.float32(16))).astype(np.float32)
    ang = (pos[:, None] * inv[None, :]).astype(np.float32)
    return np.concatenate([np.cos(ang).astype(np.float32), np.sin(ang).astype(np.float32)], axis=1)


def kernel(x_prompt, x_sample, cache_k, cache_v, p_prompt, p_sample,
           norm1_g, w_in, ln_v_g, ln_v_b, w_spatial, b_spatial, w_out,
           norm2_g, w_up, w_down, gate_norm_g, w_gate, w_ple, final_g):
    global _PROGRAM
    f32 = np.float32
    x_prompt = np.asarray(x_prompt, f32); x_sample = np.asarray(x_sample, f32)
    cache_k = np.asarray(cache_k, f32); cache_v = np.asarray(cache_v, f32)
    p_prompt = np.asarray(p_prompt, f32); p_sample = np.asarray(p_sample, f32)
    xp = x_prompt[0]
    pp = p_prompt[0, 0]
    def gl(g):
        return np.ascontiguousarray(np.asarray(g, f32).reshape(8, 128).T)
    gains = np.concatenate([gl(norm1_g[0]), gl(norm2_g[0]), gl(gate_norm_g[0])], axis=1)
    rowv = np.zeros((3, 1024), f32)
    rowv[0] = np.asarray(final_g, f32)
    rowv[1, :512] = np.asarray(ln_v_g[0], f32)
    rowv[1, 512:] = np.asarray(ln_v_b[0], f32)
    ws = np.asarray(w_spatial[0], f32)
    bs = np.asarray(b_spatial[0], f32)
    wsT = np.zeros((2, 128, 8, 128), f32)
    wsT[0] = np.transpose(ws, (2, 0, 1))
    trilm = np.zeros((2, 128, 128), f32)
    trilm[0] = np.triu(np.ones((128, 128), f32))
    bsp = np.zeros((2, 128, 8), f32)
    bsp[0] = bs.T
    for b in range(4):
        for j in range(4):
            for i in range(4):
                wsT[1, 4 * b + j, :, 4 * b + i] = ws[:, i, j]
                if j <= i:
                    trilm[1, 4 * b + j, 4 * b + i] = 1.0
        bsp[1, 4 * b:4 * b + 4, :] = bs[:, :4].T
    wsT = wsT.reshape(2, 128, 1024)
    ident = np.eye(128, dtype=f32)
    ik = np.arange(128)[:, None]; iq = np.arange(128)[None, :]
    m_prev = (ik >= iq).astype(f32); m_cur = (ik <= iq).astype(f32)
    mN = np.concatenate([m_prev, m_cur, m_prev, m_cur], axis=1)
    mH = np.concatenate([np.zeros_like(m_prev), m_cur, np.zeros_like(m_prev), m_cur], axis=1)
    onesp = np.zeros((128, 2, 128), f32)
    onesp[:, 0, :64] = 1.0; onesp[:, 1, 64:] = 1.0
    onesp = onesp.reshape(128, 256)
    diagm = np.zeros((8, 8, 64), f32)
    for h in range(8):
        diagm[h, h, :] = 1.0
    diagm = diagm.reshape(8, 512)
    selc = np.zeros((8, 16, 16), f32)
    for bt in range(16):
        selc[:, bt, bt] = 1.0
    selc = selc.reshape(8, 256)
    shared = dict(w_in=np.ascontiguousarray(w_in[0], f32), w_out=np.ascontiguousarray(w_out[0], f32),
                  w_up=np.ascontiguousarray(w_up[0], f32), w_down=np.ascontiguousarray(w_down[0], f32),
                  w_gate=np.ascontiguousarray(w_gate[0], f32), w_ple=np.ascontiguousarray(w_ple[0], f32),
                  gains=gains, rowv=rowv, wsT=wsT, trilm=trilm, bsp=bsp, ident=ident,
                  onesp=onesp, diagm=diagm, selc=selc)
    in_maps = []
    for c in range(NCORES):
        xall = np.zeros((33, 128, D), f32)
        if c > 0:
            xall[0:16] = xp[(c - 1) * SH:c * SH].reshape(16, 128, D)
        xall[16:32] = xp[c * SH:(c + 1) * SH].reshape(16, 128, D)
        xall[32, :16] = x_sample[4 * c:4 * c + 4].reshape(16, D)
        pall = np.zeros((17, 128, 256), f32)
        pall[:16] = pp[c * SH:(c + 1) * SH].reshape(16, 128, 256)
        pall[16, :16] = p_sample[0, 4 * c:4 * c + 4].reshape(16, 256)
        pos = np.zeros((33, 128), f32)
        pos[:32] = ((c - 1) * SH + np.arange(2 * SH)).reshape(32, 128)
        pos[32, :16] = 16384 + np.tile(np.arange(4), 4)
        cs = _rope_tables(pos.reshape(-1)).reshape(33, 128, 16)
        m = dict(shared)
        m.update(xall=xall, pall=pall, cs=cs,
                 ck=np.ascontiguousarray(cache_k[0, 4 * c:4 * c + 4].reshape(4, 2048, 512)),
                 cv=np.ascontiguousarray(cache_v[0, 4 * c:4 * c + 4].reshape(4, 2048, 512)),
                 amask=np.stack([mN, mN if c > 0 else mH], axis=0))
        in_maps.append(m)
    if _PROGRAM is None:
        try:
            _PROGRAM = build_program()
        except _Stop as e_:
            _PROGRAM = e_.args[0]
    res = run_bass_kernel_spmd(_PROGRAM, in_maps, core_ids=list(range(NCORES)))
    R = res.results
    y_prompt = np.concatenate([R[c]["y"][:16].reshape(SH, D) for c in range(NCORES)], 0)[None]
    y_sample = np.concatenate([R[c]["y"][16, :16].reshape(4, 4, D) for c in range(NCORES)], 0)
    nkp = R[7]["nk"][:16].reshape(1, 1, SH, 8, 64)
    nvp = R[7]["nv"][:16].reshape(1, 1, SH, 8, 64)
    nks = np.concatenate([R[c]["nk"][16, :16].reshape(4, 4, 8, 64) for c in range(NCORES)], 0)[None]
    nvs = np.concatenate([R[c]["nv"][16, :16].reshape(4, 4, 8, 64) for c in range(NCORES)], 0)[None]
    nvc = np.concatenate([R[c]["nvc"][:16].reshape(4, 4, 512) for c in range(NCORES)], 0)[None]
    return (y_prompt.astype(f32), y_sample.astype(f32), nkp.astype(f32), nvp.astype(f32),
            nks.astype(f32), nvs.astype(f32), nvc.astype(f32))
```

```python
import numpy as np
from contextlib import ExitStack
import concourse.bass as bass
import concourse.mybir as mybir
from concourse.bass_utils import run_bass_kernel_spmd

F32 = mybir.dt.float32
BF16 = mybir.dt.bfloat16
AF = mybir.ActivationFunctionType
ALU = mybir.AluOpType
AX = mybir.AxisListType

ENGS = ["pe", "act", "dve", "pool", "sp"]
NCORES = 8
D = 1024
SH = 2048
NT = 16
EPS = 1e-6
LN3 = float(np.log(3.0))


import os
KSTOP = int(os.environ.get("KSTOP", "99"))


class _Stop(Exception):
    pass


class Buf:
    __slots__ = ("w", "r")

    def __init__(self):
        self.w = None
        self.r = []


class Sched:
    def __init__(self, nc, stack, n_dma_sems=24):
        self.nc = nc
        self.ops = {e: [] for e in ENGS}
        self.sem = {}
        self.cnt = {e: 0 for e in ENGS}
        for e in ENGS:
            self.sem[e] = stack.enter_context(nc.semaphore("s_" + e))
        self.dsems = {}
        self.dpos = {}
        for q, n in (("sp", n_dma_sems), ("act", 8), ("dve", 8), ("pool", 8)):
            self.dsems[q] = [[stack.enter_context(nc.semaphore(f"d_{q}{i}")), 0] for i in range(n)]
            self.dpos[q] = 0
        self.known = {}
        self.cap = None

    def capture(self, fn, *args):
        lst = []
        self.cap = lst
        fn(*args)
        self.cap = None
        return lst

    def replay(self, lists):
        lists = [l for l in lists if l]
        pos = [0] * len(lists)
        while True:
            best, bf = -1, 2.0
            for i, l in enumerate(lists):
                if pos[i] < len(l):
                    f = pos[i] / len(l)
                    if f < bf:
                        best, bf = i, f
            if best < 0:
                break
            kind, a, kw = lists[best][pos[best]]
            pos[best] += 1
            if kind == "op":
                self.op(*a, **kw)
            else:
                self.dma(*a, **kw)

    def _wait(self, eng, tok):
        if tok is None:
            return
        key, semobj, val, prod = tok
        if prod == eng and eng == "pe":
            return
        kk = (eng, key)
        if self.known.get(kk, 0) >= val:
            return
        self.known[kk] = val
        self.ops[eng].append(lambda e, s=semobj, v=val: e.wait_ge(s, v))

    def _deps(self, eng, reads, writes):
        for b in reads:
            self._wait(eng, b.w)
        for b in writes:
            self._wait(eng, b.w)
            for t in b.r:
                self._wait(eng, t)

    def _commit(self, tok, reads, writes):
        for b in reads:
            b.r.append(tok)
            if len(b.r) > 24:
                b.r = b.r[-24:]
        for b in writes:
            b.w = tok
            b.r = []

    def op(self, eng, fn, reads=(), writes=(), sig=True):
        if self.cap is not None:
            self.cap.append(("op", (eng, fn), dict(reads=reads, writes=writes, sig=sig)))
            return None
        self._deps(eng, reads, writes)
        if sig:
            self.cnt[eng] += 1
            v = self.cnt[eng]
            s = self.sem[eng]
            self.ops[eng].append(lambda e, f=fn, s=s: f(e).then_inc(s, 1))
            tok = (eng, s, v, eng)
            self._commit(tok, reads, writes)
            return tok
        self.ops[eng].append(lambda e, f=fn: f(e))
        return None

    def dma(self, out, in_, reads=(), writes=(), q="sp"):
        if self.cap is not None:
            self.cap.append(("dma", (out, in_), dict(reads=reads, writes=writes, q=q)))
            return None
        self._deps(q, reads, writes)
        pool = self.dsems[q]
        i = self.dpos[q]
        self.dpos[q] = (i + 1) % len(pool)
        ent = pool[i]
        key = f"d_{q}{i}"
        if ent[1] > 0:
            self._wait(q, (key, ent[0], ent[1], None))
        ent[1] += 16
        s, v = ent[0], ent[1]
        self.ops[q].append(lambda e, o=out, i_=in_, s=s: e.dma_start(out=o, in_=i_).then_inc(s, 16))
        tok = (key, s, v, None)
        self._commit(tok, reads, writes)
        return tok

    def barrier(self):
        toks = []
        for e in ENGS:
            if self.cnt[e] > 0:
                toks.append((e, self.sem[e], self.cnt[e], e))
        for q, pool in self.dsems.items():
            for i, ent in enumerate(pool):
                if ent[1] > 0:
                    toks.append((f"d_{q}{i}", ent[0], ent[1], None))
        for e in ENGS:
            for t in toks:
                if t[3] == e:
                    continue
                self._wait(e, t)

    def emit(self):
        nc = self.nc
        with nc.Block() as block:
            @block.tensor
            def _(e):
                for f in self.ops["pe"]:
                    f(e)

            @block.scalar
            def _(e):
                for f in self.ops["act"]:
                    f(e)

            @block.vector
            def _(e):
                for f in self.ops["dve"]:
                    f(e)

            @block.gpsimd
            def _(e):
                for f in self.ops["pool"]:
                    f(e)

            @block.sync
            def _(e):
                for f in self.ops["sp"]:
                    f(e)


class Ring:
    def __init__(self, tiles):
        self.tiles = tiles
        self.bufs = [Buf() for _ in tiles]
        self.i = 0

    def next(self):
        j = self.i % len(self.tiles)
        self.i += 1
        return self.tiles[j], self.bufs[j]


def build_program():
    nc = bass.Bass("TRN2", target_bir_lowering=False)

    def din(name, shape, dt=F32):
        return nc.dram_tensor(name, list(shape), dt, kind="ExternalInput").ap()

    def dout(name, shape, dt=F32):
        return nc.dram_tensor(name, list(shape), dt, kind="ExternalOutput").ap()

    def dscr(name, shape, dt):
        return nc.dram_tensor(name, list(shape), dt).ap()

    xall = din("xall", [33, 128, D])
    pall = din("pall", [17, 128, 256])
    cs = din("cs", [33, 128, 16])
    ck = din("ck", [4, 2048, 512])
    cv = din("cv", [4, 2048, 512])
    w_in = din("w_in", [D, 2560])
    w_out = din("w_out", [D, D])
    w_up = din("w_up", [D, 4096])
    w_down = din("w_down", [4096, D])
    w_gate = din("w_gate", [D, D])
    w_ple = din("w_ple", [256, D])
    gains = din("gains", [128, 24])
    rowv = din("rowv", [3, 1024])
    wsT = din("wsT", [2, 128, 1024])
    trilm = din("trilm", [2, 128, 128])
    bsp = din("bsp", [2, 128, 8])
    ident_in = din("ident", [128, 128])
    amask = din("amask", [2, 128, 512])
    onesp = din("onesp", [128, 256])
    diagm = din("diagm", [8, 512])
    selc = din("selc", [8, 256])

    y_o = dout("y", [17, 128, D])
    nk_o = dout("nk", [17, 128, 512])
    nv_o = dout("nv", [17, 128, 512])
    nvc_o = dout("nvc", [128, 512])

    vsc = dscr("vsc", [4096, 1024], BF16)
    qsc = dscr("qsc", [16, 512], F32)
    wo_s = dscr("wo_s", [128, 8, 1024], BF16)
    wg_s = dscr("wg_s", [128, 8, 1024], BF16)
    wp_s = dscr("wp_s", [128, 2, 1024], BF16)
    wd_s = dscr("wd_s", [128, 32, 1024], BF16)
    wu_s = dscr("wu_s", [8, 128, 8, 512], BF16)

    with ExitStack() as st:
        S = Sched(nc, st)

        def sbt(stack, name, shape, dt):
            return stack.enter_context(nc.sbuf_tensor("sb_" + name, list(shape), dt))

        def pst(stack, name, shape, dt=F32):
            return stack.enter_context(nc.psum_tensor("ps_" + name, list(shape), dt))

        try:
            ident_f = sbt(st, "ident_f", [128, 128], F32)
            ident = sbt(st, "ident", [128, 128], BF16)
            gn = sbt(st, "gn", [128, 24], F32)
            mixT = sbt(st, "mixT", [128, 8, SH], BF16)
            mixTs = sbt(st, "mixTs", [128, 8, 128], BF16)
            b_c = Buf()
            S.dma(ident_f[:], ident_in, writes=[b_c])
            S.dma(gn[:], gains, writes=[b_c])
            S.op("dve", lambda e: e.tensor_copy(out=ident[:], in_=ident_f[:]), reads=[b_c], writes=[b_c])
            S.op("pool", lambda e: e.memset(mixTs[:], 0.0), writes=[b_c])

            with ExitStack() as sa:
                winb = sbt(sa, "winb", [128, 8, 2560], BF16)
                b_win = Buf()
                b_wblk = [[Buf() for _ in range(8)] for _ in range(5)]
                for blk_ in (1, 2, 0, 3, 4):
                    for kc in range(8):
                        S.dma(winb[:, kc, blk_ * 512:(blk_ + 1) * 512], w_in[kc * 128:(kc + 1) * 128, blk_ * 512:(blk_ + 1) * 512],
                              writes=[b_wblk[blk_][kc]], q="pool")
                b_wsc = Buf()
                wq = []
                for j in range(4):
                    wq.append((wo_s[:, 2 * j:2 * j + 2, :], w_out[j * 256:(j + 1) * 256, :].rearrange("(c p) n -> p c n", p=128)))
                for kc in range(8):
                    for hf in range(2):
                        wq.append((wu_s[4 * hf:4 * hf + 4, :, kc, :].rearrange("u p n -> p u n"),
                                   w_up[kc * 128:(kc + 1) * 128, hf * 2048:(hf + 1) * 2048].rearrange("p (u n) -> p u n", u=4)))
                for j in range(16):
                    wq.append((wd_s[:, 2 * j:2 * j + 2, :], w_down[j * 256:(j + 1) * 256, :].rearrange("(c p) n -> p c n", p=128)))
                for j in range(4):
                    wq.append((wg_s[:, 2 * j:2 * j + 2, :], w_gate[j * 256:(j + 1) * 256, :].rearrange("(c p) n -> p c n", p=128)))
                wq.append((wp_s[:, :, :], w_ple.rearrange("(c p) n -> p c n", p=128)))

                def wq_issue(n):
                    for _ in range(n):
                        if wq:
                            o_, i_ = wq.pop(0)
                            S.dma(o_, i_, writes=[Buf()], q="pool")

                kT = sbt(sa, "kT", [128, 4, 4096], BF16)
                qT = sbt(sa, "qT", [128, 4, SH], BF16)
                with ExitStack() as s1:
                    xr = Ring([sbt(s1, f"xt{i}", [128, D], F32) for i in range(3)])
                    csr = Ring([sbt(s1, f"cst{i}", [128, 16], F32) for i in range(6)])
                    xb_r = Ring([sbt(s1, f"xb{i}", [128, D], BF16) for i in range(2)])
                    xT_r = Ring([sbt(s1, f"xT{i}", [128, 8, 128], BF16) for i in range(2)])
                    st_r = Ring([sbt(s1, f"st1_{i}", [128, 8], F32) for i in range(4)])
                    st2 = sbt(s1, "st2", [128, 8], F32); b_st2 = Buf()
                    bnst = sbt(s1, "bnst", [128, 6], F32)
                    bnag = sbt(s1, "bnag", [128, 2], F32)
                    qkf_r = Ring([sbt(s1, f"qkf{i}", [128, 1024], F32) for i in range(2)])
                    vf_r = Ring([sbt(s1, f"vf{i}", [128, 512], F32) for i in range(2)])
                    vpad_r = Ring([sbt(s1, f"vpad{i}", [128, 8, 128], BF16) for i in range(1)])
                    rtmp = sbt(s1, "rtmp", [128, 4, 16, 8], F32); b_rtmp = Buf()
                    qkb = sbt(s1, "qkb", [128, 1024], BF16); b_qkb = Buf()
                    ub_r = Ring([sbt(s1, f"ub{i}", [128, 512], BF16) for i in range(2)])
                    vcf_r = Ring([sbt(s1, f"vcf{i}", [128, 512], F32) for i in range(2)])
                    vnf = sbt(s1, "vnf", [128, 512], F32); b_vnf = Buf()
                    vnb = sbt(s1, "vnb", [128, 512], BF16); b_vnb = Buf()
                    gtmp = None; b_gtmp = None
                    gb = sbt(s1, "gb", [128, 512], BF16); b_gb = Buf()
                    lng = sbt(s1, "lng", [128, 1024], F32)
                    wsf = qkf_r.tiles[0]; b_wsf = qkf_r.bufs[0]
                    wsb = sbt(s1, "wsb", [128, 2, 1024], BF16)
                    trm = sbt(s1, "trm", [128, 2, 128], F32)
                    bsb = sbt(s1, "bsb", [128, 2, 8], F32)
                    kt_r = Ring([sbt(s1, f"skt{i}", [128, 512], F32) for i in range(4)])
                    vt_r = Ring([sbt(s1, f"svt{i}", [128, 512], F32) for i in range(4)])
                    qbc_r = Ring([sbt(s1, f"qbc{i}", [128, 512], F32) for i in range(1)])
                    prod = sbt(s1, "prod", [128, 512], F32); b_prod = Buf(); gtmp = prod; b_gtmp = b_prod
                    ssc_r = Ring([sbt(s1, f"ssc{i}", [128, 32], F32) for i in range(2)])
                    pex_r = Ring([sbt(s1, f"pex{i}", [128, 32], F32) for i in range(2)])
                    onesc = sbt(s1, "onesc", [128, 1], F32)
                    rden = sbt(s1, "rden", [8, 1], F32); b_rden = Buf()
                    msk8 = sbt(s1, "msk8", [8, 512], F32); b_msk8 = Buf()
                    dgm = sbt(s1, "dgm", [8, 512], F32)
                    sel = sbt(s1, "sel", [8, 256], F32)
                    attb = sbt(s1, "attb", [128, 512], BF16); b_attb = Buf()
                    pz = Ring([pst(s1, f"pz{i}", [128, 512]) for i in range(2)])
                    pT = pst(s1, "pT", [128, 8, 128], BF16); b_pT = Buf()
                    pTx = pst(s1, "pTx", [128, 8, 128], BF16); b_pTx = Buf()
                    pmix = pst(s1, "pmix", [128, 512]); b_pmix = Buf()
                    pacc = pst(s1, "pacc", [8, 512]); b_pacc = Buf()
                    pden = pst(s1, "pden", [8, 512]); b_pden = Buf()
                    prow = pst(s1, "prow", [16, 512]); b_prow = Buf()

                    b_k1 = Buf()
                    S.dma(lng[:], rowv[1:2, :].partition_broadcast(128), writes=[b_k1])
                    S.dma(trm[:], trilm.rearrange("t p n -> p t n"), writes=[b_k1])
                    S.dma(bsb[:], bsp.rearrange("t p n -> p t n"), writes=[b_k1])
                    S.dma(dgm[:], diagm, writes=[b_k1])
                    S.dma(sel[:], selc, writes=[b_k1])
                    for t2 in range(2):
                        S.dma(wsf[:], wsT[t2], writes=[b_wsf])
                        S.op("dve", lambda e, t2=t2: e.tensor_tensor(
                            out=wsb[:, t2, :].rearrange("p (g i) -> p g i", g=8),
                            in0=wsf[:].rearrange("p (g i) -> p g i", g=8),
                            in1=trm[:, t2, :].unsqueeze(1).broadcast_to([128, 8, 128]), op=ALU.mult),
                            reads=[b_k1, b_wsf], writes=[b_k1])
                    S.op("pool", lambda e: e.memset(onesc[:], 1.0), writes=[b_k1])
                    for (sc_, bsc_) in zip(ssc_r.tiles, ssc_r.bufs):
                        S.op("pool", lambda e, sc_=sc_: e.memset(sc_[:], 0.0), writes=[bsc_])
                    S.op("pool", lambda e: e.memset(attb[:], 0.0), writes=[b_attb])
                    for (vp, bvp) in zip(vpad_r.tiles, vpad_r.bufs):
                        S.op("pool", lambda e, vp=vp: e.memset(vp[:], 0.0), writes=[bvp])

                    b_nk = Buf(); b_nv = Buf(); b_qsc = Buf(); b_vsc = Buf(); b_out = Buf()

                    order = [32] + list(range(32))
                    loads = {}

                    def issue_load(t):
                        xt, bx = xr.next()
                        ct, bct = csr.next()
                        S.dma(xt[:], xall[t], writes=[bx])
                        S.dma(ct[:], cs[t], writes=[bct])
                        loads[t] = (xt, bx, ct, bct)

                    ctxs = {}

                    f1ctx = {}

                    def f1a(t, part):
                        if part == "act":
                            xt, bx, ct, bct = loads.pop(t)
                            xb, b_xb = xb_r.next()
                            xT, b_xT = xT_r.next()
                            st1, b_st1 = st_r.next()
                            S.op("act", lambda e: e.activation(out=qkb[:], in_=xt[:], func=AF.Square, accum_out=st1[:, 0:1]),
                                 reads=[bx], writes=[b_qkb, b_st1])
                            S.op("act", lambda e: e.activation(out=st1[:, 1:2], in_=st1[:, 0:1], func=AF.Sqrt, scale=1.0 / D, bias=EPS),
                                 reads=[b_st1], writes=[b_st1])
                            f1ctx[t] = (xt, bx, ct, bct, xb, b_xb, xT, b_xT, st1, b_st1)
                        else:
                            st1, b_st1 = f1ctx[t][8], f1ctx[t][9]
                            S.op("dve", lambda e: e.reciprocal(out=st1[:, 2:3], in_=st1[:, 1:2]), reads=[b_st1], writes=[b_st1])

                    def f1b(t, part):
                        xt, bx, ct, bct, xb, b_xb, xT, b_xT, st1, b_st1 = f1ctx[t]
                        if part == "cast":
                            S.op("dve", lambda e: e.tensor_copy(out=xb[:], in_=xt[:]), reads=[bx], writes=[b_xb])
                            return
                        for c in range(8):
                            S.op("pe", lambda e, c=c: e.transpose(out=pTx[:, c, :], in_=xb[:, c * 128:(c + 1) * 128], identity=ident[:]),
                                 reads=[b_xb, b_c], writes=[b_pTx], sig=(c == 7))
                        for c in range(8):
                            S.op("dve", lambda e, c=c: e.tensor_scalar(out=xT[:, c, :], in0=pTx[:, c, :], scalar1=gn[:, c:c + 1], scalar2=None, op0=ALU.mult),
                                 reads=[b_pTx, b_c], writes=[b_xT], sig=(c == 7))

                    def f2(t):
                        xt, bx, ct, bct, xb, b_xb, xT, b_xT, st1, b_st1 = f1ctx.pop(t)
                        halo = t < 16
                        rstd = st1[:, 2:3]

                        def proj(col0):
                            pzt, bpz = pz.next()
                            for c in range(8):
                                S.op("pe", lambda e, c=c: e.matmul(pzt[:], lhsT=xT[:, c, :], rhs=winb[:, c, col0:col0 + 512],
                                                                   start=(c == 0), stop=(c == 7)),
                                     reads=[b_xT, b_wblk[col0 // 512][c]], writes=[bpz], sig=(c == 7))
                            return pzt, bpz

                        qkf, bqk = qkf_r.next()
                        vf, bvf = vf_r.next()
                        ub, b_ub = ub_r.next()
                        vcf, b_vcf = vcf_r.next()
                        if not halo:
                            pq, bpq = proj(0)
                            S.op("act", lambda e: e.activation(out=qkf[:, 0:512], in_=pq[:], func=AF.Copy, scale=rstd),
                                 reads=[bpq, b_st1], writes=[bqk])
                        pk, bpk = proj(512)
                        S.op("act", lambda e: e.activation(out=qkf[:, 512:1024], in_=pk[:], func=AF.Copy, scale=rstd),
                             reads=[bpk, b_st1], writes=[bqk])
                        pv, bpv = proj(1024)
                        S.op("act", lambda e: e.activation(out=vf[:], in_=pv[:], func=AF.Copy, scale=rstd),
                             reads=[bpv, b_st1], writes=[bvf])
                        if not halo:
                            pu, bpu = proj(1536)
                            S.op("act", lambda e: e.activation(out=ub[:], in_=pu[:], func=AF.Gelu, scale=rstd),
                                 reads=[bpu, b_st1], writes=[b_ub])
                            pc, bpc = proj(2048)
                            S.op("act", lambda e: e.activation(out=vcf[:], in_=pc[:], func=AF.Gelu, scale=rstd),
                                 reads=[bpc, b_st1], writes=[b_vcf])
                        ctxs[t] = (ct, bct, qkf, bqk, vf, bvf, ub, b_ub, vcf, b_vcf)

                    def lna(t, part):
                        if t < 16:
                            return
                        vcf, b_vcf = ctxs[t][8], ctxs[t][9]
                        if part == "stats":
                            S.op("dve", lambda e: e.bn_stats(out=bnst[:], in_=vcf[:]), reads=[b_vcf], writes=[b_st2])
                            S.op("dve", lambda e: e.bn_aggr(out=bnag[:], in_=bnst[:]), reads=[b_st2], writes=[b_st2])
                            S.op("act", lambda e: e.activation(out=st2[:, 3:4], in_=bnag[:, 1:2], func=AF.Sqrt, scale=1.0, bias=EPS),
                                 reads=[b_st2], writes=[b_st2])
                        else:
                            S.op("dve", lambda e: e.reciprocal(out=st2[:, 4:5], in_=st2[:, 3:4]), reads=[b_st2], writes=[b_st2])

                    bctx = {}

                    def tileA_back(t, part):
                        ct, bct, qkf, bqk, vf, bvf, ub, b_ub, vcf, b_vcf = ctxs[t]
                        halo = t < 16
                        samp = t == 32
                        h0 = 8 if halo else 0
                        nh = 16 - h0
                        if part == "a":
                            v3 = qkf[:].rearrange("p (h d) -> p h d", d=64)
                            x1 = v3[:, h0:16, 0:8]
                            x2 = v3[:, h0:16, 8:16]
                            cosb = ct[:, 0:8].unsqueeze(1).broadcast_to([128, nh, 8])
                            sinb = ct[:, 8:16].unsqueeze(1).broadcast_to([128, nh, 8])
                            tm = [rtmp[:, i, h0:16, :] for i in range(4)]
                            S.op("dve", lambda e: e.tensor_tensor(out=tm[0], in0=x1, in1=cosb, op=ALU.mult), reads=[bqk, bct], writes=[b_rtmp])
                            S.op("dve", lambda e: e.tensor_tensor(out=tm[1], in0=x2, in1=sinb, op=ALU.mult), reads=[bqk, bct], writes=[b_rtmp])
                            S.op("dve", lambda e: e.tensor_tensor(out=tm[2], in0=x2, in1=cosb, op=ALU.mult), reads=[bqk, bct], writes=[b_rtmp])
                            S.op("dve", lambda e: e.tensor_tensor(out=tm[3], in0=x1, in1=sinb, op=ALU.mult), reads=[bqk, bct], writes=[b_rtmp])
                            S.op("dve", lambda e: e.tensor_tensor(out=x1, in0=tm[0], in1=tm[1], op=ALU.subtract), reads=[b_rtmp], writes=[bqk])
                            S.op("dve", lambda e: e.tensor_tensor(out=x2, in0=tm[2], in1=tm[3], op=ALU.add), reads=[b_rtmp], writes=[bqk])
                            if not halo:
                                ot = t - 16
                                S.dma(nk_o[ot], qkf[:, 512:1024], reads=[bqk], writes=[b_nk])
                                S.dma(nv_o[ot], vf[:], reads=[bvf], writes=[b_nv])
                            if samp:
                                S.dma(qsc, qkf[0:16, 0:512], reads=[bqk], writes=[b_qsc])
                            else:
                                vp, bvp = vpad_r.next()
                                vp4 = vp[:].rearrange("p (c hh) n -> p c hh n", hh=2)
                                vf4 = vf[:].rearrange("p (c hh d) -> p c hh d", hh=2, d=64)
                                for hh in range(2):
                                    S.op("pool", lambda e, hh=hh: e.tensor_copy(out=vp4[:, :, hh, hh * 64:(hh + 1) * 64], in_=vf4[:, :, hh, :]),
                                         reads=[bvf], writes=[bvp])
                                S.dma(vsc[t * 128:(t + 1) * 128, :], vp[:].rearrange("p h n -> p (h n)"), reads=[bvp], writes=[b_vsc])
                                S.op("pool" if halo else "dve", lambda e: e.tensor_copy(out=qkb[:, h0 * 64:1024], in_=qkf[:, h0 * 64:1024]),
                                     reads=[bqk], writes=[b_qkb])
                            return
                        if part == "b":
                            if samp:
                                return
                            j0 = 4 if halo else 0
                            for j in range(j0, 8):
                                S.op("pe", lambda e, j=j: e.transpose(out=pT[:, j, :], in_=qkb[:, j * 128:(j + 1) * 128], identity=ident[:]),
                                     reads=[b_qkb, b_c], writes=[b_pT], sig=(j == 7))
                            S.op("dve", lambda e: e.tensor_copy(out=kT[:, :, t * 128:(t + 1) * 128], in_=pT[:, 4:8, :]), reads=[b_pT], writes=[b_out])
                            if not halo:
                                S.op("dve", lambda e: e.tensor_copy(out=qT[:, :, (t - 16) * 128:(t - 15) * 128], in_=pT[:, 0:4, :]),
                                     reads=[b_pT], writes=[b_out])
                            return
                        if halo:
                            return
                        ws_i = 1 if samp else 0
                        if part == "c":
                            S.op("dve", lambda e: e.tensor_scalar(out=vnf[:], in0=vcf[:], scalar1=bnag[:, 0:1], scalar2=st2[:, 4:5],
                                                                 op0=ALU.subtract, op1=ALU.mult), reads=[b_vcf, b_st2], writes=[b_vnf])
                            S.op("pool", lambda e: e.tensor_tensor(out=vnf[:], in0=vnf[:], in1=lng[:, 0:512], op=ALU.mult), reads=[b_vnf, b_k1], writes=[b_vnf])
                            if samp:
                                S.op("pool", lambda e: e.tensor_tensor(out=vnf[:], in0=vnf[:], in1=lng[:, 512:1024], op=ALU.add), reads=[b_vnf, b_k1], writes=[b_vnf])
                                S.op("pool", lambda e: e.tensor_copy(out=vnb[:], in_=vnf[:]), reads=[b_vnf], writes=[b_vnb])
                                S.dma(nvc_o, vnf[:], reads=[b_vnf], writes=[b_out])
                            else:
                                S.op("pool", lambda e: e.tensor_tensor(out=vnb[:], in0=vnf[:], in1=lng[:, 512:1024], op=ALU.add), reads=[b_vnf, b_k1], writes=[b_vnb])
                            return
                        if part == "d":
                            for g in range(8):
                                S.op("pe", lambda e, g=g: e.matmul(pmix[:, g * 64:(g + 1) * 64], lhsT=wsb[:, ws_i, g * 128:(g + 1) * 128],
                                                                   rhs=vnb[:, g * 64:(g + 1) * 64], start=True, stop=True),
                                     reads=[b_vnb, b_k1], writes=[b_pmix], sig=(g == 7))
                            S.op("dve", lambda e: e.tensor_tensor(out=gtmp[:].rearrange("p (g c) -> p g c", g=8),
                                                                 in0=pmix[:].rearrange("p (g c) -> p g c", g=8),
                                                                 in1=bsb[:, ws_i, :].unsqueeze(2).broadcast_to([128, 8, 64]), op=ALU.add),
                                 reads=[b_pmix, b_k1], writes=[b_gtmp])
                            S.op("dve", lambda e: e.tensor_tensor(out=gb[:], in0=gtmp[:], in1=ub[:], op=ALU.mult), reads=[b_gtmp, b_ub], writes=[b_gb])
                            return
                        if part == "e":
                            for j in range(4):
                                S.op("pe", lambda e, j=j: e.transpose(out=pT[:, j, :], in_=gb[:, j * 128:(j + 1) * 128], identity=ident[:]),
                                     reads=[b_gb, b_c], writes=[b_pT], sig=(j == 3))
                            if samp:
                                S.op("dve", lambda e: e.tensor_copy(out=mixTs[:, 4:8, :], in_=pT[:, 0:4, :]), reads=[b_pT], writes=[b_out])
                            else:
                                S.op("dve", lambda e: e.tensor_copy(out=mixT[:, 4:8, (t - 16) * 128:(t - 15) * 128], in_=pT[:, 0:4, :]),
                                     reads=[b_pT], writes=[b_out])

                    state = {"first": True}

                    sctx = {}

                    def sample_unit(bt, part):
                        b, tt = bt // 4, bt % 4
                        specs = []
                        for g, d in enumerate((1, 4, 16, 0)):
                            specs.append((g, d, 1 if d == 0 else 128))
                        if part == "kdma":
                            qb, bqb = qbc_r.next()
                            S.dma(qb[:], qsc[bt:bt + 1, :].partition_broadcast(128), reads=[b_qsc], writes=[bqb])
                            kts = []
                            for (g, d, np_) in specs:
                                ktile, bkt = kt_r.next()
                                if d == 0:
                                    S.dma(ktile[0:1, :], nk_o[16, bt:bt + 1, :], reads=[b_nk], writes=[bkt])
                                else:
                                    r0 = 2048 + tt - 128 * d
                                    if d == 1 and tt > 0:
                                        nc_ = 128 - tt
                                        S.dma(ktile[0:nc_, :], ck[b, r0:2048, :], writes=[bkt])
                                        S.dma(ktile[nc_:128, :], nk_o[16, 4 * b:4 * b + tt, :], reads=[b_nk], writes=[bkt])
                                    else:
                                        S.dma(ktile[:], ck[b, r0:r0 + 127 * d + 1:d, :], writes=[bkt])
                                kts.append((ktile, bkt))
                            sctx[bt] = dict(qb=qb, bqb=bqb, kts=kts)
                            return
                        c_ = sctx[bt]
                        if part == "vdma":
                            vts = []
                            for (g, d, np_) in specs:
                                vtile, bvt = vt_r.next()
                                if d == 0:
                                    S.dma(vtile[0:1, :], nv_o[16, bt:bt + 1, :], reads=[b_nv], writes=[bvt])
                                else:
                                    r0 = 2048 + tt - 128 * d
                                    if d == 1 and tt > 0:
                                        nc_ = 128 - tt
                                        S.dma(vtile[0:nc_, :], cv[b, r0:2048, :], writes=[bvt])
                                        S.dma(vtile[nc_:128, :], nv_o[16, 4 * b:4 * b + tt, :], reads=[b_nv], writes=[bvt])
                                    else:
                                        S.dma(vtile[:], cv[b, r0:r0 + 127 * d + 1:d, :], writes=[bvt])
                                vts.append((vtile, bvt))
                            c_["vts"] = vts
                            return
                        if part == "s1":
                            qb, bqb = c_["qb"], c_["bqb"]
                            sc, bsc = ssc_r.next()
                            pe_, bpe = pex_r.next()

                            def score(g, np_, ktile, bkt):
                                S.op("dve", lambda e: e.tensor_tensor(out=prod[0:np_, :], in0=ktile[0:np_, :], in1=qb[0:np_, :], op=ALU.mult),
                                     reads=[bkt, bqb], writes=[b_prod])
                                S.op("dve", lambda e: e.tensor_reduce(out=sc[0:np_, g * 8:(g + 1) * 8], in_=prod[0:np_, :].rearrange("p (h d) -> p h d", h=8),
                                                                     axis=AX.X, op=ALU.add), reads=[b_prod], writes=[bsc])
                            for (g, d, np_), (ktile, bkt) in zip(specs, c_["kts"]):
                                score(g, np_, ktile, bkt)
                            S.op("act", lambda e: e.activation(out=pe_[:], in_=sc[:], func=AF.Exp, scale=0.125), reads=[bsc], writes=[bpe])
                            S.op("dve", lambda e: e.tensor_scalar(out=pe_[0:1, 24:32], in0=pe_[0:1, 24:32], scalar1=3.0, scalar2=None, op0=ALU.mult),
                                 reads=[bpe], writes=[bpe])
                            c_["pe"], c_["bpe"] = pe_, bpe
                            return
                        pe_, bpe = c_["pe"], c_["bpe"]

                        def pv(g, np_, vtile, bvt):
                            S.op("pe", lambda e: e.matmul(pacc[:], lhsT=pe_[0:np_, g * 8:(g + 1) * 8], rhs=vtile[0:np_, :], start=(g == 0), stop=(g == 3)),
                                 reads=[bpe, bvt], writes=[b_pacc])
                            S.op("pe", lambda e: e.matmul(pden[:, 0:1], lhsT=pe_[0:np_, g * 8:(g + 1) * 8], rhs=onesc[0:np_, :], start=(g == 0), stop=(g == 3)),
                                 reads=[bpe, b_k1], writes=[b_pden])
                        for (g, d, np_), (vtile, bvt) in zip(specs, c_["vts"]):
                            pv(g, np_, vtile, bvt)
                        S.op("dve", lambda e: e.reciprocal(out=rden[:], in_=pden[:, 0:1]), reads=[b_pden], writes=[b_rden])
                        S.op("dve", lambda e: e.scalar_tensor_tensor(out=msk8[:], in0=pacc[:], scalar=rden[:], in1=dgm[:],
                                                                    op0=ALU.mult, op1=ALU.mult),
                             reads=[b_pacc, b_rden, b_k1], writes=[b_msk8])
                        S.op("pe", lambda e: e.matmul(prow[:], lhsT=sel[:, bt * 16:(bt + 1) * 16], rhs=msk8[:],
                                                      start=(bt == 0), stop=(bt == 15)),
                             reads=[b_msk8, b_k1], writes=[b_prow])
                        sctx.pop(bt)

                    def sample_finish():
                        S.op("dve", lambda e: e.tensor_copy(out=attb[0:16, :], in_=prow[:]), reads=[b_prow], writes=[b_attb])
                        for j in range(4):
                            S.op("pe", lambda e, j=j: e.transpose(out=pT[:, j, :], in_=attb[:, j * 128:(j + 1) * 128], identity=ident[:]),
                                 reads=[b_attb, b_c], writes=[b_pT], sig=(j == 3))
                        S.op("act", lambda e: e.activation(out=mixTs[:, 0:4, :], in_=pT[:, 0:4, :], func=AF.Copy), reads=[b_pT], writes=[b_out])

                    NO = len(order)
                    for k in range(3):
                        issue_load(order[k])
                    for k in range(3):
                        f1a(order[k], "act")
                    f1a(order[0], "recip"); f1a(order[1], "recip")
                    f1b(order[0], "cast"); f1b(order[0], "T"); f2(order[0])
                    f1b(order[1], "cast"); f1b(order[1], "T")
                    lna(order[0], "stats")
                    f1b(order[2], "cast")
                    issue_load(order[3])
                    for i, t in enumerate(order):
                        if i + 4 < NO:
                            issue_load(order[i + 4])
                        if 1 <= i <= 16:
                            sample_unit(i - 1, "kdma")
                        if i + 2 < NO:
                            f1b(order[i + 2], "T")
                        if i >= 1:
                            tileA_back(order[i - 1], "e")
                            ctxs.pop(order[i - 1])
                        if i + 1 < NO:
                            f2(order[i + 1])
                        if i + 2 < NO:
                            f1a(order[i + 2], "recip")
                        tileA_back(t, "a")
                        lna(t, "recip")
                        tileA_back(t, "c")
                        tileA_back(t, "b")
                        tileA_back(t, "d")
                        if 2 <= i <= 17:
                            sample_unit(i - 2, "s2")
                        if 1 <= i <= 16:
                            sample_unit(i - 1, "vdma")
                        wq_issue(2)
                        if i + 3 < NO:
                            f1a(order[i + 3], "act")
                            f1b(order[i + 3], "cast")
                        if 1 <= i <= 16:
                            sample_unit(i - 1, "s1")
                        if i + 1 < NO:
                            lna(order[i + 1], "stats")
                        if i == 17:
                            sample_finish()
                    tileA_back(order[-1], "e")
                    ctxs.pop(order[-1])
                    wq_issue(100)
                    S.barrier()
                    if KSTOP <= 1:
                        S.emit(); raise _Stop(nc)

                with ExitStack() as s2:
                    mk_f = sbt(s2, "mk_f", [128, 2, 512], F32)
                    mk = sbt(s2, "mk", [128, 2, 512], BF16)
                    onp_f = sbt(s2, "onp_f", [128, 256], F32)
                    onp = sbt(s2, "onp", [128, 2, 128], BF16)
                    accs = [(sbt(s2, f"accN{i}", [128, SH], F32), sbt(s2, f"accD{i}", [128, SH], F32), Buf(), Buf()) for i in range(2)]
                    PT_r = Ring([sbt(s2, f"PT{i}", [128, 512], BF16) for i in range(4)])
                    V_r = Ring([sbt(s2, f"Vt{i}", [128, 2, 2, 128], BF16) for i in range(6)])
                    ST_r = Ring([pst(s2, f"ST{i}", [128, 2, 512]) for i in range(3)])
                    NUM_r = Ring([pst(s2, f"NUM{i}", [128, 512]) for i in range(1)])
                    DEN_r = Ring([pst(s2, f"DEN{i}", [128, 512]) for i in range(1)])
                    b_k2 = Buf()
                    S.dma(mk_f[:], amask.rearrange("t p n -> p t n"), writes=[b_k2])
                    S.dma(onp_f[:], onesp, writes=[b_k2])
                    S.op("dve", lambda e: e.tensor_copy(out=mk[:], in_=mk_f[:]), reads=[b_k2], writes=[b_k2])
                    S.op("dve", lambda e: e.tensor_copy(out=onp[:].rearrange("p h n -> p (h n)"), in_=onp_f[:]), reads=[b_k2], writes=[b_k2])

                    def blocks_of(d, G):
                        if d == 1:
                            return [128 * (4 * G + j) for j in range(4)]
                        if d == 4:
                            return [512 * G + r for r in range(4)]
                        return [4 * G + r for r in range(4)]

                    def acc_view(acc, d, G):
                        if d == 1:
                            return acc[:, 512 * G:512 * (G + 1)].rearrange("p (j i) -> p j i", j=4)
                        if d == 4:
                            return acc[:, 512 * G:512 * (G + 1)].rearrange("p (i r) -> p r i", r=4)
                        return acc[:].rearrange("p (i r) -> p r i", r=16)[:, 4 * G:4 * G + 4, :]

                    DILS = tuple(int(v) for v in os.environ.get("KDILS", "1,4,16").split(","))
                    blks = []
                    for c in range(4):
                        for di, d in enumerate(DILS):
                            for G in range(4):
                                for j, q0 in enumerate(blocks_of(d, G)):
                                    blks.append(dict(c=c, di=di, d=d, G=G, j=j, q0=q0))
                    nblk = len(blks)

                    v_b2 = {}

                    def att_vload(B):
                        d, c = B["d"], B["c"]
                        kc0 = 2048 + B["q0"]
                        kp0 = kc0 - 128 * d
                        Vt, bV = V_r.next()
                        bV2 = v_b2.setdefault(id(bV), Buf())
                        for kt, k0 in enumerate((kp0, kc0)):
                            S.dma(Vt[:, kt, :, :].rearrange("p h n -> p (h n)"),
                                  vsc[k0:k0 + 127 * d + 1:d, 256 * c:256 * (c + 1)], writes=[bV if kt == 0 else bV2])
                        B["Vt"], B["bV"], B["bV2"] = Vt, bV, bV2

                    def att_front(B, idx):
                        d, c, q0 = B["d"], B["c"], B["q0"]
                        kc0 = 2048 + q0
                        kp0 = kc0 - 128 * d
                        STp, bS = ST_r.next()
                        n = 0
                        for hh in range(2):
                            for kt, k0 in enumerate((kp0, kc0)):
                                n += 1
                                S.op("pe", lambda e, hh=hh, kt=kt, k0=k0: e.matmul(
                                    STp[:, hh, kt * 128:(kt + 1) * 128],
                                    lhsT=kT[hh * 64:(hh + 1) * 64, c, k0:k0 + 127 * d + 1:d],
                                    rhs=qT[hh * 64:(hh + 1) * 64, c, q0:q0 + 127 * d + 1:d], start=True, stop=True),
                                    writes=[bS], sig=(n == 4))
                        PT, bP = PT_r.next()
                        S.op("act", lambda e: e.activation(out=PT[:].rearrange("p (h x) -> p h x", h=2), in_=STp[:, :, 0:256], func=AF.Exp, scale=0.125),
                             reads=[bS], writes=[bP])
                        mi = 1 if kp0 < 2048 else 0
                        S.op("dve" if idx % 4 != 3 else "pool", lambda e: e.tensor_tensor(out=PT[:], in0=PT[:], in1=mk[:, mi, :], op=ALU.mult),
                             reads=[bP, b_k2], writes=[bP])
                        B["PT"], B["bP"] = PT, bP

                    cur = {}

                    def att_back(B):
                        c, di, d, G, j = B["c"], B["di"], B["d"], B["G"], B["j"]
                        PT, bP, Vt, bV, bV2 = B["PT"], B["bP"], B["Vt"], B["bV"], B["bV2"]
                        if j == 0:
                            cur["NUM"], cur["bN"] = NUM_r.next()
                            cur["DEN"], cur["bD"] = DEN_r.next()
                        NUM, bN, DEN, bD = cur["NUM"], cur["bN"], cur["DEN"], cur["bD"]
                        n = 0
                        for hh in range(2):
                            for kt in range(2):
                                n += 1
                                S.op("pe", lambda e, hh=hh, kt=kt, n=n: e.matmul(
                                    NUM[:, j * 128:(j + 1) * 128], lhsT=Vt[:, kt, hh, :],
                                    rhs=PT[:, (hh * 2 + kt) * 128:(hh * 2 + kt + 1) * 128], start=(n == 1), stop=(n == 4)),
                                    reads=[bP, bV, bV2], writes=[bN], sig=False)
                        n = 0
                        for hh in range(2):
                            for kt in range(2):
                                n += 1
                                S.op("pe", lambda e, hh=hh, kt=kt, n=n: e.matmul(
                                    DEN[:, j * 128:(j + 1) * 128], lhsT=onp[:, hh, :],
                                    rhs=PT[:, (hh * 2 + kt) * 128:(hh * 2 + kt + 1) * 128], start=(n == 1), stop=(n == 4)),
                                    reads=[bP, bV, bV2, b_k2], writes=[bN, bD], sig=(n == 4))
                        if j != 3:
                            return
                        aN, aD, baN, baD = accs[c % 2]
                        nv_ = acc_view(aN, d, G)
                        dv_ = acc_view(aD, d, G)
                        N3 = NUM[:].rearrange("p (j i) -> p j i", j=4)
                        D3 = DEN[:].rearrange("p (j i) -> p j i", j=4)
                        if di == 0:
                            S.op("act", lambda e: e.activation(out=nv_, in_=N3, func=AF.Copy), reads=[bN], writes=[baN])
                            S.op("dve", lambda e: e.tensor_copy(out=dv_, in_=D3), reads=[bD], writes=[baD])
                        else:
                            S.op("dve", lambda e: e.tensor_tensor(out=nv_, in0=N3, in1=nv_, op=ALU.add), reads=[bN, baN], writes=[baN])
                            S.op("dve", lambda e: e.tensor_tensor(out=dv_, in0=D3, in1=dv_, op=ALU.add), reads=[bD, baD], writes=[baD])
                        if di == len(DILS) - 1 and G == 3:
                            for G2 in range(4):
                                att_final(c, G2)

                    def att_final(c, G):
                        aN, aD, baN, baD = accs[c % 2]
                        sl = slice(512 * G, 512 * (G + 1))
                        S.op("dve", lambda e: e.reciprocal(out=aD[:, sl], in_=aD[:, sl]), reads=[baD], writes=[baD])
                        S.op("pool", lambda e: e.tensor_tensor(out=mixT[:, c, sl], in0=aN[:, sl], in1=aD[:, sl], op=ALU.mult),
                             reads=[baN, baD], writes=[baN, baD])

                    for k in range(min(4, nblk)):
                        att_vload(blks[k])
                    att_front(blks[0], 0)
                    if nblk > 1:
                        att_front(blks[1], 1)
                    for idx in range(nblk):
                        if idx + 4 < nblk:
                            att_vload(blks[idx + 4])
                        if idx + 2 < nblk:
                            att_front(blks[idx + 2], idx + 2)
                        att_back(blks[idx])
                    S.barrier()
                    if KSTOP <= 2:
                        S.emit(); raise _Stop(nc)

            with ExitStack() as s3:
                wdn = sbt(s3, "wdn", [128, 32, 1024], BF16)
                wo = sbt(s3, "wo", [128, 8, 1024], BF16)
                wg = sbt(s3, "wg", [128, 8, 1024], BF16)
                wp = sbt(s3, "wp", [128, 2, 1024], BF16)
                fgb = sbt(s3, "fgb", [128, 1024], F32)
                wu_r = Ring([sbt(s3, f"wu{i}", [128, 8, 512], BF16) for i in range(2)])
                xh_r = Ring([sbt(s3, f"xh{i}", [128, D], F32) for i in range(4)])
                pt_r = Ring([sbt(s3, f"ptl{i}", [128, 256], F32) for i in range(4)])
                hb = sbt(s3, "hb", [128, D], BF16); b_hb = Buf()
                junk2 = hb; b_junk2 = b_hb
                hT = sbt(s3, "hT", [128, 8, 256], BF16); b_hT = Buf()
                h2T = sbt(s3, "h2T", [128, 8, 128], BF16); b_h2T = Buf()
                actT = sbt(s3, "actT", [128, 32, 256], BF16); b_actT = Buf()
                rl_r = Ring([sbt(s3, f"rl{i}", [128, 256], F32) for i in range(2)])
                gate_r = Ring([sbt(s3, f"gate{i}", [128, 512], F32) for i in range(1)])
                pb16 = sbt(s3, "pb16", [128, 256], BF16); b_pb16 = Buf()
                pTp = sbt(s3, "pTp", [128, 2, 128], BF16); b_pTp = Buf()
                stt_all = [sbt(s3, f"stt{i}", [128, 2, 16], F32) for i in range(2)]; cur_stt = [stt_all[0]]
                pbk = Ring([pst(s3, f"pb{i}", [128, 512]) for i in range(3)])
                ps1 = Ring([pst(s3, "ps1", [128, 512])])
                pT3 = pst(s3, "pT3", [128, 8, 128], BF16); b_pT3 = Buf()
                hb2 = sbt(s3, "hb2", [128, D], BF16); b_hb2 = Buf()
                pa_r = Ring([pst(s3, f"pa{i}", [128, 512]) for i in range(2)])
                pT2 = pst(s3, "pT2", [128, 8, 128], BF16); b_pT2 = Buf()
                b_w = Buf(); b_yo = Buf()
                b_wo = Buf(); b_wg = Buf(); b_wp = Buf(); b_wd = Buf()
                S.dma(wo[:], wo_s, reads=[b_wsc], writes=[b_wo])
                S.dma(fgb[:], rowv[0:1, :].partition_broadcast(128), writes=[b_w])
                b_wdl = [Buf() for _ in range(4)]
                for j in range(4):
                    S.dma(wdn[:, 8 * j:8 * j + 8, :], wd_s[:, 8 * j:8 * j + 8, :], reads=[b_wsc], writes=[b_wdl[j]])
                S.dma(wg[:], wg_s, reads=[b_wsc], writes=[b_wg])
                S.dma(wp[:], wp_s, reads=[b_wsc], writes=[b_wp])

                def stats_and_T(b_st, xh, bxh, si, col, dstT, b_dstT, ncol_off, pTb=None, b_pTb=None, hbb=None, b_hbb=None, gcol0=0):
                    stt = cur_stt[0]
                    pTb = pT2 if pTb is None else pTb
                    b_pTb = b_pT2 if b_pTb is None else b_pTb
                    hbb = hb if hbb is None else hbb
                    b_hbb = b_hb if b_hbb is None else b_hbb
                    S.op("act", lambda e: e.activation(out=hbb[:], in_=xh[:], func=AF.Square, accum_out=stt[:, si, col:col + 1]),
                         reads=[bxh], writes=[b_hbb, b_st])
                    S.op("act", lambda e: e.activation(out=stt[:, si, col + 1:col + 2], in_=stt[:, si, col:col + 1], func=AF.Sqrt,
                                                       scale=1.0 / D, bias=EPS), reads=[b_st], writes=[b_st])
                    S.op("dve", lambda e: e.reciprocal(out=stt[:, si, col + 2:col + 3], in_=stt[:, si, col + 1:col + 2]), reads=[b_st], writes=[b_st])
                    if dstT is None:
                        return
                    S.op("act", lambda e: e.activation(out=hbb[:], in_=xh[:], func=AF.Copy), reads=[bxh], writes=[b_hbb])
                    for c in range(8):
                        S.op("pe", lambda e, c=c: e.transpose(out=pTb[:, c, :], in_=hbb[:, c * 128:(c + 1) * 128], identity=ident[:]),
                             reads=[b_hbb, b_c], writes=[b_pTb], sig=(c == 7))
                    for c in range(8):
                        S.op("dve", lambda e, c=c: e.tensor_scalar(out=dstT[:, c, ncol_off:ncol_off + 128], in0=pTb[:, c, :],
                                                                  scalar1=gn[:, gcol0 + c:gcol0 + c + 1], scalar2=None, op0=ALU.mult),
                             reads=[b_pTb, b_c], writes=[b_dstT], sig=(c == 7))

                def mm_group(out_ap, bout, pairs, reads):
                    n = len(pairs)
                    for i, (l, r) in enumerate(pairs):
                        S.op("pe", lambda e, l=l, r=r, i=i: e.matmul(out_ap, lhsT=l, rhs=r, start=(i == 0), stop=(i == n - 1)),
                             reads=reads, writes=[bout], sig=(i == n - 1))

                def stage1(b_st, xh, bxh, mc, si):
                    stt = cur_stt[0]
                    for hf in range(2):
                        pb_, bpb = ps1.next()
                        mm_group(pb_[:], bpb, [(mc[:, fc, :], wo[:, fc, hf * 512:(hf + 1) * 512]) for fc in range(8)], [b_wo])
                        xs = xh[:, hf * 512:(hf + 1) * 512]
                        S.op("dve", lambda e, pb_=pb_, xs=xs: e.tensor_tensor(out=xs, in0=pb_[:], in1=xs, op=ALU.add),
                             reads=[bpb, bxh], writes=[bxh])
                    stats_and_T(b_st, xh, bxh, si, 0, hT, b_hT, si * 128, pT3, b_pT3, hb2, b_hb2, gcol0=8)
                    S.op("dve", lambda e: e.tensor_tensor(out=stt[:, si, 3:4], in0=stt[:, si, 2:3], in1=stt[:, si, 2:3], op=ALU.mult),
                         reads=[b_st], writes=[b_st])

                wu_pending = []

                def wu_load(u):
                    wu, bwu = wu_r.next()
                    S.dma(wu[:], wu_s[u], reads=[b_wsc], writes=[bwu])
                    wu_pending.append((wu, bwu))

                def stage2_unit(u, T):
                    wu, bwu = wu_pending.pop(0)
                    for f4 in range(4):
                        ffc = 4 * u + f4
                        pa, bpa = pa_r.next()
                        mm_group(pa[:, 0:T], bpa, [(wu[:, kc, f4 * 128:(f4 + 1) * 128], hT[:, kc, 0:T]) for kc in range(8)], [bwu, b_hT])
                        rl, brl = rl_r.next()
                        S.op("act", lambda e, rl=rl, pa=pa: e.activation(out=rl[:, 0:T], in_=pa[:, 0:T], func=AF.Relu), reads=[bpa], writes=[brl])
                        S.op("pool" if ffc % 2 else "dve",
                             lambda e, rl=rl, ffc=ffc: e.tensor_tensor(out=actT[:, ffc, 0:T], in0=rl[:, 0:T], in1=rl[:, 0:T], op=ALU.mult),
                             reads=[brl], writes=[b_actT])

                def stage34(b_st, xh, bxh, ptl, bpt, si, s):
                    stt = cur_stt[0]
                    for hf in range(2):
                        pb_, bpb = pbk.next()
                        mm_group(pb_[:], bpb, [(actT[:, ffc, si * 128:(si + 1) * 128], wdn[:, ffc, hf * 512:(hf + 1) * 512]) for ffc in range(32)],
                                 [b_actT] + b_wdl)
                        xs = xh[:, hf * 512:(hf + 1) * 512]
                        S.op("dve", lambda e, pb_=pb_, xs=xs: e.scalar_tensor_tensor(out=xs, in0=pb_[:], scalar=stt[:, si, 3:4], in1=xs,
                                                                                    op0=ALU.mult, op1=ALU.add),
                             reads=[bpb, bxh, b_st], writes=[bxh])
                    stats_and_T(b_st, xh, bxh, si, 4, h2T, b_h2T, 0, gcol0=16)
                    S.op("pool", lambda e: e.tensor_copy(out=pb16[:], in_=ptl[:]), reads=[bpt], writes=[b_pb16])
                    for c in range(2):
                        S.op("pe", lambda e, c=c: e.transpose(out=pT2[:, c, :], in_=pb16[:, c * 128:(c + 1) * 128], identity=ident[:]),
                             reads=[b_pb16, b_c], writes=[b_pT2], sig=(c == 1))
                    S.op("dve", lambda e: e.tensor_copy(out=pTp[:], in_=pT2[:, 0:2, :]), reads=[b_pT2], writes=[b_pTp])
                    for hf in range(2):
                        pg, bpg = pbk.next()
                        mm_group(pg[:], bpg, [(h2T[:, kc, :], wg[:, kc, hf * 512:(hf + 1) * 512]) for kc in range(8)], [b_h2T, b_wg])
                        gt, bgt = gate_r.next()
                        S.op("act", lambda e, gt=gt, pg=pg: e.activation(out=gt[:], in_=pg[:], func=AF.Sigmoid, scale=stt[:, si, 6:7]),
                             reads=[bpg, b_st], writes=[bgt])
                        pp, bpp = pbk.next()
                        mm_group(pp[:], bpp, [(pTp[:, kc, :], wp[:, kc, hf * 512:(hf + 1) * 512]) for kc in range(2)], [b_pTp, b_wp])
                        xs = xh[:, hf * 512:(hf + 1) * 512]
                        S.op("dve", lambda e, gt=gt, pp=pp: e.tensor_tensor(out=gt[:], in0=pp[:], in1=gt[:], op=ALU.mult), reads=[bpp, bgt], writes=[bgt])
                        S.op("dve", lambda e, gt=gt, xs=xs: e.tensor_tensor(out=xs, in0=gt[:], in1=xs, op=ALU.add), reads=[bgt, bxh], writes=[bxh])
                    stats_and_T(b_st, xh, bxh, si, 8, None, None, 0)
                    S.op("dve", lambda e: e.scalar_tensor_tensor(out=xh[:], in0=xh[:], scalar=stt[:, si, 10:11], in1=fgb[:], op0=ALU.mult, op1=ALU.mult),
                         reads=[bxh, b_st, b_w], writes=[bxh])
                    S.dma(y_o[s], xh[:], reads=[bxh], writes=[b_yo])

                passes = [([2 * i, 2 * i + 1], False) for i in range(8)] + [([16], True)]
                pinfo = {}

                def pass_loads(p):
                    subs, samp = passes[p]
                    tiles = []
                    for si, s in enumerate(subs):
                        xh, bxh = xh_r.next()
                        ptl, bpt = pt_r.next()
                        S.dma(xh[:], xall[32 if samp else 16 + s], writes=[bxh])
                        S.dma(ptl[:], pall[s], writes=[bpt])
                        tiles.append((xh, bxh, ptl, bpt))
                    pinfo[p] = (tiles, Buf(), stt_all[p % 2])

                def pass_stage1(p):
                    subs, samp = passes[p]
                    tiles, b_st, sttp = pinfo[p]
                    cur_stt[0] = sttp
                    for si, s in enumerate(subs):
                        xh, bxh, ptl, bpt = tiles[si]
                        mc = mixTs[:, :, :] if samp else mixT[:, :, s * 128:(s + 1) * 128]
                        stage1(b_st, xh, bxh, mc, si)

                def pass_stage34(p):
                    subs, samp = passes[p]
                    tiles, b_st, sttp = pinfo[p]
                    cur_stt[0] = sttp
                    for si, s in enumerate(subs):
                        xh, bxh, ptl, bpt = tiles[si]
                        stage34(b_st, xh, bxh, ptl, bpt, si, s)

                pass_loads(0)
                wu_load(0); wu_load(1)
                pass_stage1(0)
                for p in range(len(passes)):
                    subs, samp = passes[p]
                    T = 128 * len(subs)
                    if p + 1 < len(passes):
                        pass_loads(p + 1)
                    for u in range(8):
                        stage2_unit(u, T)
                        g_next = p * 8 + u + 2
                        if g_next < 8 * len(passes):
                            wu_load(g_next % 8)
                    streams = [S.capture(pass_stage34, p)]
                    if p + 1 < len(passes):
                        streams.append(S.capture(pass_stage1, p + 1))
                    S.replay(streams)
                S.barrier()
        except ZeroDivisionError:
            pass
        S.emit()
    return nc


_PROGRAM = None


def _rope_tables(pos):
    pos = np.asarray(pos, dtype=np.float32)
    inv = (np.float32(500000.0) ** (-np.arange(0, 16, 2, dtype=np.float32) / np.float32(16))).astype(np.float32)
    ang = (pos[:, None] * inv[None, :]).astype(np.float32)
    return np.concatenate([np.cos(ang).astype(np.float32), np.sin(ang).astype(np.float32)], axis=1)


def kernel(x_prompt, x_sample, cache_k, cache_v, p_prompt, p_sample,
           norm1_g, w_in, ln_v_g, ln_v_b, w_spatial, b_spatial, w_out,
           norm2_g, w_up, w_down, gate_norm_g, w_gate, w_ple, final_g):
    global _PROGRAM
    f32 = np.float32
    x_prompt = np.asarray(x_prompt, f32); x_sample = np.asarray(x_sample, f32)
    cache_k = np.asarray(cache_k, f32); cache_v = np.asarray(cache_v, f32)
    p_prompt = np.asarray(p_prompt, f32); p_sample = np.asarray(p_sample, f32)
    xp = x_prompt[0]
    pp = p_prompt[0, 0]
    def gl(g):
        return np.ascontiguousarray(np.asarray(g, f32).reshape(8, 128).T)
    gains = np.concatenate([gl(norm1_g[0]), gl(norm2_g[0]), gl(gate_norm_g[0])], axis=1)
    rowv = np.zeros((3, 1024), f32)
    rowv[0] = np.asarray(final_g, f32)
    rowv[1, :512] = np.asarray(ln_v_g[0], f32)
    rowv[1, 512:] = np.asarray(ln_v_b[0], f32)
    ws = np.asarray(w_spatial[0], f32)
    bs = np.asarray(b_spatial[0], f32)
    wsT = np.zeros((2, 128, 8, 128), f32)
    wsT[0] = np.transpose(ws, (2, 0, 1))
    trilm = np.zeros((2, 128, 128), f32)
    trilm[0] = np.triu(np.ones((128, 128), f32))
    bsp = np.zeros((2, 128, 8), f32)
    bsp[0] = bs.T
    for b in range(4):
        for j in range(4):
            for i in range(4):
                wsT[1, 4 * b + j, :, 4 * b + i] = ws[:, i, j]
                if j <= i:
                    trilm[1, 4 * b + j, 4 * b + i] = 1.0
        bsp[1, 4 * b:4 * b + 4, :] = bs[:, :4].T
    wsT = wsT.reshape(2, 128, 1024)
    ident = np.eye(128, dtype=f32)
    ik = np.arange(128)[:, None]; iq = np.arange(128)[None, :]
    m_prev = (ik >= iq).astype(f32); m_cur = (ik <= iq).astype(f32)
    mN = np.concatenate([m_prev, m_cur, m_prev, m_cur], axis=1)
    mH = np.concatenate([np.zeros_like(m_prev), m_cur, np.zeros_like(m_prev), m_cur], axis=1)
    onesp = np.zeros((128, 2, 128), f32)
    onesp[:, 0, :64] = 1.0; onesp[:, 1, 64:] = 1.0
    onesp = onesp.reshape(128, 256)
    diagm = np.zeros((8, 8, 64), f32)
    for h in range(8):
        diagm[h, h, :] = 1.0
    diagm = diagm.reshape(8, 512)
    selc = np.zeros((8, 16, 16), f32)
    for bt in range(16):
        selc[:, bt, bt] = 1.0
    selc = selc.reshape(8, 256)
    shared = dict(w_in=np.ascontiguousarray(w_in[0], f32), w_out=np.ascontiguousarray(w_out[0], f32),
                  w_up=np.ascontiguousarray(w_up[0], f32), w_down=np.ascontiguousarray(w_down[0], f32),
                  w_gate=np.ascontiguousarray(w_gate[0], f32), w_ple=np.ascontiguousarray(w_ple[0], f32),
                  gains=gains, rowv=rowv, wsT=wsT, trilm=trilm, bsp=bsp, ident=ident,
                  onesp=onesp, diagm=diagm, selc=selc)
    in_maps = []
    for c in range(NCORES):
        xall = np.zeros((33, 128, D), f32)
        if c > 0:
            xall[0:16] = xp[(c - 1) * SH:c * SH].reshape(16, 128, D)
        xall[16:32] = xp[c * SH:(c + 1) * SH].reshape(16, 128, D)
        xall[32, :16] = x_sample[4 * c:4 * c + 4].reshape(16, D)
        pall = np.zeros((17, 128, 256), f32)
        pall[:16] = pp[c * SH:(c + 1) * SH].reshape(16, 128, 256)
        pall[16, :16] = p_sample[0, 4 * c:4 * c + 4].reshape(16, 256)
        pos = np.zeros((33, 128), f32)
        pos[:32] = ((c - 1) * SH + np.arange(2 * SH)).reshape(32, 128)
        pos[32, :16] = 16384 + np.tile(np.arange(4), 4)
        cs = _rope_tables(pos.reshape(-1)).reshape(33, 128, 16)
        m = dict(shared)
        m.update(xall=xall, pall=pall, cs=cs,
                 ck=np.ascontiguousarray(cache_k[0, 4 * c:4 * c + 4].reshape(4, 2048, 512)),
                 cv=np.ascontiguousarray(cache_v[0, 4 * c:4 * c + 4].reshape(4, 2048, 512)),
                 amask=np.stack([mN, mN if c > 0 else mH], axis=0))
        in_maps.append(m)
    if _PROGRAM is None:
        try:
            _PROGRAM = build_program()
        except _Stop as e_:
            _PROGRAM = e_.args[0]
    res = run_bass_kernel_spmd(_PROGRAM, in_maps, core_ids=list(range(NCORES)))
    R = res.results
    y_prompt = np.concatenate([R[c]["y"][:16].reshape(SH, D) for c in range(NCORES)], 0)[None]
    y_sample = np.concatenate([R[c]["y"][16, :16].reshape(4, 4, D) for c in range(NCORES)], 0)
    nkp = R[7]["nk"][:16].reshape(1, 1, SH, 8, 64)
    nvp = R[7]["nv"][:16].reshape(1, 1, SH, 8, 64)
    nks = np.concatenate([R[c]["nk"][16, :16].reshape(4, 4, 8, 64) for c in range(NCORES)], 0)[None]
    nvs = np.concatenate([R[c]["nv"][16, :16].reshape(4, 4, 8, 64) for c in range(NCORES)], 0)[None]
    nvc = np.concatenate([R[c]["nvc"][:16].reshape(4, 4, 512) for c in range(NCORES)], 0)[None]
    return (y_prompt.astype(f32), y_sample.astype(f32), nkp.astype(f32), nvp.astype(f32),
            nks.astype(f32), nvs.astype(f32), nvc.astype(f32))
```

```python
import numpy as np
from contextlib import ExitStack
import concourse.bass as bass
import concourse.mybir as mybir
from concourse.bass_utils import run_bass_kernel_spmd

F32 = mybir.dt.float32
BF16 = mybir.dt.bfloat16
AF = mybir.ActivationFunctionType
ALU = mybir.AluOpType
AX = mybir.AxisListType

ENGS = ["pe", "act", "dve", "pool", "sp"]
NCORES = 8
D = 1024
SH = 2048
NT = 16
EPS = 1e-6
LN3 = float(np.log(3.0))


import os
KSTOP = int(os.environ.get("KSTOP", "99"))


class _Stop(Exception):
    pass


class Buf:
    __slots__ = ("w", "r")

    def __init__(self):
        self.w = None
        self.r = []


class Sched:
    def __init__(self, nc, stack, n_dma_sems=24):
        self.nc = nc
        self.ops = {e: [] for e in ENGS}
        self.sem = {}
        self.cnt = {e: 0 for e in ENGS}
        for e in ENGS:
            self.sem[e] = stack.enter_context(nc.semaphore("s_" + e))
        self.dsems = {}
        self.dpos = {}
        for q, n in (("sp", n_dma_sems), ("act", 8), ("dve", 8), ("pool", 8)):
            self.dsems[q] = [[stack.enter_context(nc.semaphore(f"d_{q}{i}")), 0] for i in range(n)]
            self.dpos[q] = 0
        self.known = {}
        self.cap = None

    def capture(self, fn, *args):
        lst = []
        self.cap = lst
        fn(*args)
        self.cap = None
        return lst

    def replay(self, lists):
        lists = [l for l in lists if l]
        pos = [0] * len(lists)
        while True:
            best, bf = -1, 2.0
            for i, l in enumerate(lists):
                if pos[i] < len(l):
                    f = pos[i] / len(l)
                    if f < bf:
                        best, bf = i, f
            if best < 0:
                break
            kind, a, kw = lists[best][pos[best]]
            pos[best] += 1
            if kind == "op":
                self.op(*a, **kw)
            else:
                self.dma(*a, **kw)

    def _wait(self, eng, tok):
        if tok is None:
            return
        key, semobj, val, prod = tok
        if prod == eng and eng == "pe":
            return
        kk = (eng, key)
        if self.known.get(kk, 0) >= val:
            return
        self.known[kk] = val
        self.ops[eng].append(lambda e, s=semobj, v=val: e.wait_ge(s, v))

    def _deps(self, eng, reads, writes):
        for b in reads:
            self._wait(eng, b.w)
        for b in writes:
            self._wait(eng, b.w)
            for t in b.r:
                self._wait(eng, t)

    def _commit(self, tok, reads, writes):
        for b in reads:
            b.r.append(tok)
            if len(b.r) > 24:
                b.r = b.r[-24:]
        for b in writes:
            b.w = tok
            b.r = []

    def op(self, eng, fn, reads=(), writes=(), sig=True):
        if self.cap is not None:
            self.cap.append(("op", (eng, fn), dict(reads=reads, writes=writes, sig=sig)))
            return None
        self._deps(eng, reads, writes)
        if sig:
            self.cnt[eng] += 1
            v = self.cnt[eng]
            s = self.sem[eng]
            self.ops[eng].append(lambda e, f=fn, s=s: f(e).then_inc(s, 1))
            tok = (eng, s, v, eng)
            self._commit(tok, reads, writes)
            return tok
        self.ops[eng].append(lambda e, f=fn: f(e))
        return None

    def dma(self, out, in_, reads=(), writes=(), q="sp"):
        if self.cap is not None:
            self.cap.append(("dma", (out, in_), dict(reads=reads, writes=writes, q=q)))
            return None
        self._deps(q, reads, writes)
        pool = self.dsems[q]
        i = self.dpos[q]
        self.dpos[q] = (i + 1) % len(pool)
        ent = pool[i]
        key = f"d_{q}{i}"
        if ent[1] > 0:
            self._wait(q, (key, ent[0], ent[1], None))
        ent[1] += 16
        s, v = ent[0], ent[1]
        self.ops[q].append(lambda e, o=out, i_=in_, s=s: e.dma_start(out=o, in_=i_).then_inc(s, 16))
        tok = (key, s, v, None)
        self._commit(tok, reads, writes)
        return tok

    def barrier(self):
        toks = []
        for e in ENGS:
            if self.cnt[e] > 0:
                toks.append((e, self.sem[e], self.cnt[e], e))
        for q, pool in self.dsems.items():
            for i, ent in enumerate(pool):
                if ent[1] > 0:
                    toks.append((f"d_{q}{i}", ent[0], ent[1], None))
        for e in ENGS:
            for t in toks:
                if t[3] == e:
                    continue
                self._wait(e, t)

    def emit(self):
        nc = self.nc
        with nc.Block() as block:
            @block.tensor
            def _(e):
                for f in self.ops["pe"]:
                    f(e)

            @block.scalar
            def _(e):
                for f in self.ops["act"]:
                    f(e)

            @block.vector
            def _(e):
                for f in self.ops["dve"]:
                    f(e)

            @block.gpsimd
            def _(e):
                for f in self.ops["pool"]:
                    f(e)

            @block.sync
            def _(e):
                for f in self.ops["sp"]:
                    f(e)


class Ring:
    def __init__(self, tiles):
        self.tiles = tiles
        self.bufs = [Buf() for _ in tiles]
        self.i = 0

    def next(self):
        j = self.i % len(self.tiles)
        self.i += 1
        return self.tiles[j], self.bufs[j]


def build_program():
    nc = bass.Bass("TRN2", target_bir_lowering=False)

    def din(name, shape, dt=F32):
        return nc.dram_tensor(name, list(shape), dt, kind="ExternalInput").ap()

    def dout(name, shape, dt=F32):
        return nc.dram_tensor(name, list(shape), dt, kind="ExternalOutput").ap()

    def dscr(name, shape, dt):
        return nc.dram_tensor(name, list(shape), dt).ap()

    xall = din("xall", [33, 128, D])
    pall = din("pall", [17, 128, 256])
    cs = din("cs", [33, 128, 16])
    ck = din("ck", [4, 2048, 512])
    cv = din("cv", [4, 2048, 512])
    w_in = din("w_in", [D, 2560])
    w_out = din("w_out", [D, D])
    w_up = din("w_up", [D, 4096])
    w_down = din("w_down", [4096, D])
    w_gate = din("w_gate", [D, D])
    w_ple = din("w_ple", [256, D])
    gains = din("gains", [128, 24])
    rowv = din("rowv", [3, 1024])
    wsT = din("wsT", [2, 128, 1024])
    trilm = din("trilm", [2, 128, 128])
    bsp = din("bsp", [2, 128, 8])
    ident_in = din("ident", [128, 128])
    amask = din("amask", [2, 128, 512])
    onesp = din("onesp", [128, 256])
    diagm = din("diagm", [8, 512])
    selc = din("selc", [8, 256])

    y_o = dout("y", [17, 128, D])
    nk_o = dout("nk", [17, 128, 512])
    nv_o = dout("nv", [17, 128, 512])
    nvc_o = dout("nvc", [128, 512])

    vsc = dscr("vsc", [4096, 1024], BF16)
    qsc = dscr("qsc", [16, 512], F32)
    wo_s = dscr("wo_s", [128, 8, 1024], BF16)
    wg_s = dscr("wg_s", [128, 8, 1024], BF16)
    wp_s = dscr("wp_s", [128, 2, 1024], BF16)
    wd_s = dscr("wd_s", [128, 32, 1024], BF16)
    wu_s = dscr("wu_s", [8, 128, 8, 512], BF16)

    with ExitStack() as st:
        S = Sched(nc, st)

        def sbt(stack, name, shape, dt):
            return stack.enter_context(nc.sbuf_tensor("sb_" + name, list(shape), dt))

        def pst(stack, name, shape, dt=F32):
            return stack.enter_context(nc.psum_tensor("ps_" + name, list(shape), dt))

        try:
            ident_f = sbt(st, "ident_f", [128, 128], F32)
            ident = sbt(st, "ident", [128, 128], BF16)
            gn = sbt(st, "gn", [128, 24], F32)
            mixT = sbt(st, "mixT", [128, 8, SH], BF16)
            mixTs = sbt(st, "mixTs", [128, 8, 128], BF16)
            b_c = Buf()
            b_c1 = Buf(); b_c2 = Buf()
            S.dma(ident_f[:], ident_in, writes=[b_c1])
            S.dma(gn[:], gains, writes=[b_c2])
            S.op("dve", lambda e: e.tensor_copy(out=ident[:], in_=ident_f[:]), reads=[b_c1, b_c2], writes=[b_c])
            S.op("pool", lambda e: e.memset(mixTs[:], 0.0), writes=[b_c])

            with ExitStack() as sa:
                winb = sbt(sa, "winb", [128, 8, 2560], BF16)
                b_win = Buf()
                b_wblk = [[Buf() for _ in range(8)] for _ in range(5)]
                for blk_ in (1, 2, 0, 3, 4):
                    for kc in range(8):
                        S.dma(winb[:, kc, blk_ * 512:(blk_ + 1) * 512], w_in[kc * 128:(kc + 1) * 128, blk_ * 512:(blk_ + 1) * 512],
                              writes=[b_wblk[blk_][kc]], q="pool")
                b_wsc = Buf()
                wq = []
                for j in range(4):
                    wq.append((wo_s[:, 2 * j:2 * j + 2, :], w_out[j * 256:(j + 1) * 256, :].rearrange("(c p) n -> p c n", p=128)))
                for kc in range(8):
                    for hf in range(2):
                        wq.append((wu_s[4 * hf:4 * hf + 4, :, kc, :].rearrange("u p n -> p u n"),
                                   w_up[kc * 128:(kc + 1) * 128, hf * 2048:(hf + 1) * 2048].rearrange("p (u n) -> p u n", u=4)))
                for j in range(16):
                    wq.append((wd_s[:, 2 * j:2 * j + 2, :], w_down[j * 256:(j + 1) * 256, :].rearrange("(c p) n -> p c n", p=128)))
                for j in range(4):
                    wq.append((wg_s[:, 2 * j:2 * j + 2, :], w_gate[j * 256:(j + 1) * 256, :].rearrange("(c p) n -> p c n", p=128)))
                wq.append((wp_s[:, :, :], w_ple.rearrange("(c p) n -> p c n", p=128)))

                def wq_issue(n):
                    for _ in range(n):
                        if wq:
                            o_, i_ = wq.pop(0)
                            S.dma(o_, i_, writes=[Buf()], q="pool")

                kT = sbt(sa, "kT", [128, 4, 4096], BF16)
                qT = sbt(sa, "qT", [128, 4, SH], BF16)
                with ExitStack() as s1:
                    xr = Ring([sbt(s1, f"xt{i}", [128, D], F32) for i in range(3)])
                    csr = Ring([sbt(s1, f"cst{i}", [128, 16], F32) for i in range(6)])
                    xb_r = Ring([sbt(s1, f"xb{i}", [128, D], BF16) for i in range(2)])
                    xT_r = Ring([sbt(s1, f"xT{i}", [128, 8, 128], BF16) for i in range(2)])
                    st_r = Ring([sbt(s1, f"st1_{i}", [128, 8], F32) for i in range(4)])
                    st2 = sbt(s1, "st2", [128, 8], F32); b_st2 = Buf()
                    bnst = sbt(s1, "bnst", [128, 6], F32)
                    bnag = sbt(s1, "bnag", [128, 2], F32)
                    qkf_r = Ring([sbt(s1, f"qkf{i}", [128, 1024], F32) for i in range(2)])
                    vf_r = Ring([sbt(s1, f"vf{i}", [128, 512], F32) for i in range(2)])
                    vpad_r = Ring([sbt(s1, f"vpad{i}", [128, 8, 128], BF16) for i in range(1)])
                    rtmp = sbt(s1, "rtmp", [128, 4, 16, 8], F32); b_rtmp = Buf()
                    qkb = sbt(s1, "qkb", [128, 1024], BF16); b_qkb = Buf()
                    ub_r = Ring([sbt(s1, f"ub{i}", [128, 512], BF16) for i in range(2)])
                    vcf_r = Ring([sbt(s1, f"vcf{i}", [128, 512], F32) for i in range(2)])
                    vnf = sbt(s1, "vnf", [128, 512], F32); b_vnf = Buf()
                    vnb = sbt(s1, "vnb", [128, 512], BF16); b_vnb = Buf()
                    gtmp = None; b_gtmp = None
                    gb = sbt(s1, "gb", [128, 512], BF16); b_gb = Buf()
                    lng = sbt(s1, "lng", [128, 1024], F32)
                    wsf = qkf_r.tiles[0]; b_wsf = qkf_r.bufs[0]
                    wsb = sbt(s1, "wsb", [128, 2, 1024], BF16)
                    trm = sbt(s1, "trm", [128, 2, 128], F32)
                    bsb = sbt(s1, "bsb", [128, 2, 8], F32)
                    kt_r = Ring([sbt(s1, f"skt{i}", [128, 512], F32) for i in range(4)])
                    vt_r = Ring([sbt(s1, f"svt{i}", [128, 512], F32) for i in range(4)])
                    qbc_r = Ring([sbt(s1, f"qbc{i}", [128, 512], F32) for i in range(1)])
                    prod = sbt(s1, "prod", [128, 512], F32); b_prod = Buf(); gtmp = prod; b_gtmp = b_prod
                    ssc_r = Ring([sbt(s1, f"ssc{i}", [128, 32], F32) for i in range(2)])
                    pex_r = Ring([sbt(s1, f"pex{i}", [128, 32], F32) for i in range(2)])
                    onesc = sbt(s1, "onesc", [128, 1], F32)
                    rden = sbt(s1, "rden", [8, 1], F32); b_rden = Buf()
                    msk8 = sbt(s1, "msk8", [8, 512], F32); b_msk8 = Buf()
                    dgm = sbt(s1, "dgm", [8, 512], F32)
                    sel = sbt(s1, "sel", [8, 256], F32)
                    attb = sbt(s1, "attb", [128, 512], BF16); b_attb = Buf()
                    pz = Ring([pst(s1, f"pz{i}", [128, 512]) for i in range(2)])
                    pT = pst(s1, "pT", [128, 8, 128], BF16); b_pT = Buf()
                    pTx = pst(s1, "pTx", [128, 8, 128], BF16); b_pTx = Buf()
                    pmix = pst(s1, "pmix", [128, 512]); b_pmix = Buf()
                    pacc = pst(s1, "pacc", [8, 512]); b_pacc = Buf()
                    pden = pst(s1, "pden", [8, 512]); b_pden = Buf()
                    prow = pst(s1, "prow", [16, 512]); b_prow = Buf()

                    b_k1 = Buf()
                    k1l = [Buf() for _ in range(5)]
                    S.dma(lng[:], rowv[1:2, :].partition_broadcast(128), writes=[k1l[0]])
                    S.dma(trm[:], trilm.rearrange("t p n -> p t n"), writes=[k1l[1]])
                    S.dma(bsb[:], bsp.rearrange("t p n -> p t n"), writes=[k1l[2]])
                    S.dma(dgm[:], diagm, writes=[k1l[3]])
                    S.dma(sel[:], selc, writes=[k1l[4]])
                    for t2 in range(2):
                        S.dma(wsf[:], wsT[t2], writes=[b_wsf])
                        S.op("dve", lambda e, t2=t2: e.tensor_tensor(
                            out=wsb[:, t2, :].rearrange("p (g i) -> p g i", g=8),
                            in0=wsf[:].rearrange("p (g i) -> p g i", g=8),
                            in1=trm[:, t2, :].unsqueeze(1).broadcast_to([128, 8, 128]), op=ALU.mult),
                            reads=[b_k1, b_wsf] + k1l, writes=[b_k1])
                    S.op("pool", lambda e: e.memset(onesc[:], 1.0), writes=[b_k1])
                    for (sc_, bsc_) in zip(ssc_r.tiles, ssc_r.bufs):
                        S.op("pool", lambda e, sc_=sc_: e.memset(sc_[:], 0.0), writes=[bsc_])
                    S.op("pool", lambda e: e.memset(attb[:], 0.0), writes=[b_attb])
                    for (vp, bvp) in zip(vpad_r.tiles, vpad_r.bufs):
                        S.op("pool", lambda e, vp=vp: e.memset(vp[:], 0.0), writes=[bvp])

                    b_nk = Buf(); b_nv = Buf(); b_qsc = Buf(); b_vsc = Buf(); b_out = Buf()

                    order = [32] + list(range(32))
                    loads = {}

                    def issue_load(t):
                        xt, bx = xr.next()
                        ct, bct = csr.next()
                        S.dma(xt[:], xall[t], writes=[bx])
                        S.dma(ct[:], cs[t], writes=[bct])
                        loads[t] = (xt, bx, ct, bct)

                    ctxs = {}

                    f1ctx = {}

                    def f1a(t, part):
                        if part == "act":
                            xt, bx, ct, bct = loads.pop(t)
                            xb, b_xb = xb_r.next()
                            xT, b_xT = xT_r.next()
                            st1, b_st1 = st_r.next()
                            S.op("act", lambda e: e.activation(out=qkb[:], in_=xt[:], func=AF.Square, accum_out=st1[:, 0:1]),
                                 reads=[bx], writes=[b_qkb, b_st1])
                            S.op("act", lambda e: e.activation(out=st1[:, 1:2], in_=st1[:, 0:1], func=AF.Sqrt, scale=1.0 / D, bias=EPS),
                                 reads=[b_st1], writes=[b_st1])
                            f1ctx[t] = (xt, bx, ct, bct, xb, b_xb, xT, b_xT, st1, b_st1)
                        else:
                            st1, b_st1 = f1ctx[t][8], f1ctx[t][9]
                            S.op("dve", lambda e: e.reciprocal(out=st1[:, 2:3], in_=st1[:, 1:2]), reads=[b_st1], writes=[b_st1])

                    def f1b(t, part):
                        xt, bx, ct, bct, xb, b_xb, xT, b_xT, st1, b_st1 = f1ctx[t]
                        if part == "cast":
                            S.op("dve", lambda e: e.tensor_copy(out=xb[:], in_=xt[:]), reads=[bx], writes=[b_xb])
                            return
                        for c in range(8):
                            S.op("pe", lambda e, c=c: e.transpose(out=pTx[:, c, :], in_=xb[:, c * 128:(c + 1) * 128], identity=ident[:]),
                                 reads=[b_xb, b_c], writes=[b_pTx], sig=(c == 7))
                        for c in range(8):
                            S.op("dve", lambda e, c=c: e.tensor_scalar(out=xT[:, c, :], in0=pTx[:, c, :], scalar1=gn[:, c:c + 1], scalar2=None, op0=ALU.mult),
                                 reads=[b_pTx, b_c], writes=[b_xT], sig=(c == 7))

                    def f2(t):
                        xt, bx, ct, bct, xb, b_xb, xT, b_xT, st1, b_st1 = f1ctx.pop(t)
                        halo = t < 16
                        rstd = st1[:, 2:3]

                        def proj(col0):
                            pzt, bpz = pz.next()
                            for c in range(8):
                                S.op("pe", lambda e, c=c: e.matmul(pzt[:], lhsT=xT[:, c, :], rhs=winb[:, c, col0:col0 + 512],
                                                                   start=(c == 0), stop=(c == 7)),
                                     reads=[b_xT, b_wblk[col0 // 512][c]], writes=[bpz], sig=(c == 7))
                            return pzt, bpz

                        qkf, bqk = qkf_r.next()
                        vf, bvf = vf_r.next()
                        ub, b_ub = ub_r.next()
                        vcf, b_vcf = vcf_r.next()
                        if not halo:
                            pq, bpq = proj(0)
                            S.op("act", lambda e: e.activation(out=qkf[:, 0:512], in_=pq[:], func=AF.Copy, scale=rstd),
                                 reads=[bpq, b_st1], writes=[bqk])
                        pk, bpk = proj(512)
                        S.op("act", lambda e: e.activation(out=qkf[:, 512:1024], in_=pk[:], func=AF.Copy, scale=rstd),
                             reads=[bpk, b_st1], writes=[bqk])
                        pv, bpv = proj(1024)
                        S.op("act", lambda e: e.activation(out=vf[:], in_=pv[:], func=AF.Copy, scale=rstd),
                             reads=[bpv, b_st1], writes=[bvf])
                        if not halo:
                            pu, bpu = proj(1536)
                            S.op("act", lambda e: e.activation(out=ub[:], in_=pu[:], func=AF.Gelu, scale=rstd),
                                 reads=[bpu, b_st1], writes=[b_ub])
                            pc, bpc = proj(2048)
                            S.op("act", lambda e: e.activation(out=vcf[:], in_=pc[:], func=AF.Gelu, scale=rstd),
                                 reads=[bpc, b_st1], writes=[b_vcf])
                        ctxs[t] = (ct, bct, qkf, bqk, vf, bvf, ub, b_ub, vcf, b_vcf)

                    def lna(t, part):
                        if t < 16:
                            return
                        vcf, b_vcf = ctxs[t][8], ctxs[t][9]
                        if part == "stats":
                            S.op("dve", lambda e: e.bn_stats(out=bnst[:], in_=vcf[:]), reads=[b_vcf], writes=[b_st2])
                            S.op("dve", lambda e: e.bn_aggr(out=bnag[:], in_=bnst[:]), reads=[b_st2], writes=[b_st2])
                            S.op("act", lambda e: e.activation(out=st2[:, 3:4], in_=bnag[:, 1:2], func=AF.Sqrt, scale=1.0, bias=EPS),
                                 reads=[b_st2], writes=[b_st2])
                        else:
                            S.op("dve", lambda e: e.reciprocal(out=st2[:, 4:5], in_=st2[:, 3:4]), reads=[b_st2], writes=[b_st2])

                    bctx = {}

                    def tileA_back(t, part):
                        ct, bct, qkf, bqk, vf, bvf, ub, b_ub, vcf, b_vcf = ctxs[t]
                        halo = t < 16
                        samp = t == 32
                        h0 = 8 if halo else 0
                        nh = 16 - h0
                        if part == "a":
                            v3 = qkf[:].rearrange("p (h d) -> p h d", d=64)
                            x1 = v3[:, h0:16, 0:8]
                            x2 = v3[:, h0:16, 8:16]
                            cosb = ct[:, 0:8].unsqueeze(1).broadcast_to([128, nh, 8])
                            sinb = ct[:, 8:16].unsqueeze(1).broadcast_to([128, nh, 8])
                            tm = [rtmp[:, i, h0:16, :] for i in range(4)]
                            S.op("dve", lambda e: e.tensor_tensor(out=tm[0], in0=x1, in1=cosb, op=ALU.mult), reads=[bqk, bct], writes=[b_rtmp])
                            S.op("dve", lambda e: e.tensor_tensor(out=tm[1], in0=x2, in1=sinb, op=ALU.mult), reads=[bqk, bct], writes=[b_rtmp])
                            S.op("dve", lambda e: e.tensor_tensor(out=tm[2], in0=x2, in1=cosb, op=ALU.mult), reads=[bqk, bct], writes=[b_rtmp])
                            S.op("dve", lambda e: e.tensor_tensor(out=tm[3], in0=x1, in1=sinb, op=ALU.mult), reads=[bqk, bct], writes=[b_rtmp])
                            S.op("dve", lambda e: e.tensor_tensor(out=x1, in0=tm[0], in1=tm[1], op=ALU.subtract), reads=[b_rtmp], writes=[bqk])
                            S.op("dve", lambda e: e.tensor_tensor(out=x2, in0=tm[2], in1=tm[3], op=ALU.add), reads=[b_rtmp], writes=[bqk])
                            if not halo:
                                ot = t - 16
                                S.dma(nk_o[ot], qkf[:, 512:1024], reads=[bqk], writes=[b_nk])
                                S.dma(nv_o[ot], vf[:], reads=[bvf], writes=[b_nv])
                            if samp:
                                S.dma(qsc, qkf[0:16, 0:512], reads=[bqk], writes=[b_qsc])
                            else:
                                vp, bvp = vpad_r.next()
                                vp4 = vp[:].rearrange("p (c hh) n -> p c hh n", hh=2)
                                vf4 = vf[:].rearrange("p (c hh d) -> p c hh d", hh=2, d=64)
                                for hh in range(2):
                                    S.op("pool", lambda e, hh=hh: e.tensor_copy(out=vp4[:, :, hh, hh * 64:(hh + 1) * 64], in_=vf4[:, :, hh, :]),
                                         reads=[bvf], writes=[bvp])
                                S.dma(vsc[t * 128:(t + 1) * 128, :], vp[:].rearrange("p h n -> p (h n)"), reads=[bvp], writes=[b_vsc])
                                S.op("pool" if halo else "dve", lambda e: e.tensor_copy(out=qkb[:, h0 * 64:1024], in_=qkf[:, h0 * 64:1024]),
                                     reads=[bqk], writes=[b_qkb])
                            return
                        if part == "b":
                            if samp:
                                return
                            j0 = 4 if halo else 0
                            for j in range(j0, 8):
                                S.op("pe", lambda e, j=j: e.transpose(out=pT[:, j, :], in_=qkb[:, j * 128:(j + 1) * 128], identity=ident[:]),
                                     reads=[b_qkb, b_c], writes=[b_pT], sig=(j == 7))
                            S.op("dve", lambda e: e.tensor_copy(out=kT[:, :, t * 128:(t + 1) * 128], in_=pT[:, 4:8, :]), reads=[b_pT], writes=[b_out])
                            if not halo:
                                S.op("dve", lambda e: e.tensor_copy(out=qT[:, :, (t - 16) * 128:(t - 15) * 128], in_=pT[:, 0:4, :]),
                                     reads=[b_pT], writes=[b_out])
                            return
                        if halo:
                            return
                        ws_i = 1 if samp else 0
                        if part == "c":
                            S.op("dve", lambda e: e.tensor_scalar(out=vnf[:], in0=vcf[:], scalar1=bnag[:, 0:1], scalar2=st2[:, 4:5],
                                                                 op0=ALU.subtract, op1=ALU.mult), reads=[b_vcf, b_st2], writes=[b_vnf])
                            S.op("pool", lambda e: e.tensor_tensor(out=vnf[:], in0=vnf[:], in1=lng[:, 0:512], op=ALU.mult), reads=[b_vnf, b_k1], writes=[b_vnf])
                            if samp:
                                S.op("pool", lambda e: e.tensor_tensor(out=vnf[:], in0=vnf[:], in1=lng[:, 512:1024], op=ALU.add), reads=[b_vnf, b_k1], writes=[b_vnf])
                                S.op("pool", lambda e: e.tensor_copy(out=vnb[:], in_=vnf[:]), reads=[b_vnf], writes=[b_vnb])
                                S.dma(nvc_o, vnf[:], reads=[b_vnf], writes=[b_out])
                            else:
                                S.op("pool", lambda e: e.tensor_tensor(out=vnb[:], in0=vnf[:], in1=lng[:, 512:1024], op=ALU.add), reads=[b_vnf, b_k1], writes=[b_vnb])
                            return
                        if part == "d":
                            for g in range(8):
                                S.op("pe", lambda e, g=g: e.matmul(pmix[:, g * 64:(g + 1) * 64], lhsT=wsb[:, ws_i, g * 128:(g + 1) * 128],
                                                                   rhs=vnb[:, g * 64:(g + 1) * 64], start=True, stop=True),
                                     reads=[b_vnb, b_k1], writes=[b_pmix], sig=(g == 7))
                            S.op("dve", lambda e: e.tensor_tensor(out=gtmp[:].rearrange("p (g c) -> p g c", g=8),
                                                                 in0=pmix[:].rearrange("p (g c) -> p g c", g=8),
                                                                 in1=bsb[:, ws_i, :].unsqueeze(2).broadcast_to([128, 8, 64]), op=ALU.add),
                                 reads=[b_pmix, b_k1], writes=[b_gtmp])
                            S.op("dve", lambda e: e.tensor_tensor(out=gb[:], in0=gtmp[:], in1=ub[:], op=ALU.mult), reads=[b_gtmp, b_ub], writes=[b_gb])
                            return
                        if part == "e":
                            for j in range(4):
                                S.op("pe", lambda e, j=j: e.transpose(out=pT[:, j, :], in_=gb[:, j * 128:(j + 1) * 128], identity=ident[:]),
                                     reads=[b_gb, b_c], writes=[b_pT], sig=(j == 3))
                            if samp:
                                S.op("dve", lambda e: e.tensor_copy(out=mixTs[:, 4:8, :], in_=pT[:, 0:4, :]), reads=[b_pT], writes=[b_out])
                            else:
                                S.op("dve", lambda e: e.tensor_copy(out=mixT[:, 4:8, (t - 16) * 128:(t - 15) * 128], in_=pT[:, 0:4, :]),
                                     reads=[b_pT], writes=[b_out])

                    state = {"first": True}

                    sctx = {}
                    kv_b2 = {}

                    def sample_unit(bt, part):
                        b, tt = bt // 4, bt % 4
                        specs = []
                        for g, d in enumerate((1, 4, 16, 0)):
                            specs.append((g, d, 1 if d == 0 else 128))
                        if part == "kdma":
                            qb, bqb = qbc_r.next()
                            S.dma(qb[:], qsc[bt:bt + 1, :].partition_broadcast(128), reads=[b_qsc], writes=[bqb])
                            kts = []
                            for (g, d, np_) in specs:
                                ktile, bkt = kt_r.next()
                                if d == 0:
                                    S.dma(ktile[0:1, :], nk_o[16, bt:bt + 1, :], reads=[b_nk], writes=[bkt])
                                else:
                                    r0 = 2048 + tt - 128 * d
                                    if d == 1 and tt > 0:
                                        nc_ = 128 - tt
                                        S.dma(ktile[0:nc_, :], ck[b, r0:2048, :], writes=[bkt])
                                        S.dma(ktile[nc_:128, :], nk_o[16, 4 * b:4 * b + tt, :], reads=[b_nk], writes=[kv_b2.setdefault(id(bkt), Buf())])
                                    else:
                                        S.dma(ktile[:], ck[b, r0:r0 + 127 * d + 1:d, :], writes=[bkt])
                                kts.append((ktile, bkt))
                            sctx[bt] = dict(qb=qb, bqb=bqb, kts=kts)
                            return
                        c_ = sctx[bt]
                        if part == "vdma":
                            vts = []
                            for (g, d, np_) in specs:
                                vtile, bvt = vt_r.next()
                                if d == 0:
                                    S.dma(vtile[0:1, :], nv_o[16, bt:bt + 1, :], reads=[b_nv], writes=[bvt])
                                else:
                                    r0 = 2048 + tt - 128 * d
                                    if d == 1 and tt > 0:
                                        nc_ = 128 - tt
                                        S.dma(vtile[0:nc_, :], cv[b, r0:2048, :], writes=[bvt])
                                        S.dma(vtile[nc_:128, :], nv_o[16, 4 * b:4 * b + tt, :], reads=[b_nv], writes=[kv_b2.setdefault(id(bvt), Buf())])
                                    else:
                                        S.dma(vtile[:], cv[b, r0:r0 + 127 * d + 1:d, :], writes=[bvt])
                                vts.append((vtile, bvt))
                            c_["vts"] = vts
                            return
                        if part == "s1":
                            qb, bqb = c_["qb"], c_["bqb"]
                            sc, bsc = ssc_r.next()
                            pe_, bpe = pex_r.next()

                            def score(g, np_, ktile, bkt):
                                S.op("dve", lambda e: e.tensor_tensor(out=prod[0:np_, :], in0=ktile[0:np_, :], in1=qb[0:np_, :], op=ALU.mult),
                                     reads=[bkt, kv_b2.setdefault(id(bkt), Buf()), bqb], writes=[b_prod])
                                S.op("dve", lambda e: e.tensor_reduce(out=sc[0:np_, g * 8:(g + 1) * 8], in_=prod[0:np_, :].rearrange("p (h d) -> p h d", h=8),
                                                                     axis=AX.X, op=ALU.add), reads=[b_prod], writes=[bsc])
                            for (g, d, np_), (ktile, bkt) in zip(specs, c_["kts"]):
                                score(g, np_, ktile, bkt)
                            S.op("act", lambda e: e.activation(out=pe_[:], in_=sc[:], func=AF.Exp, scale=0.125), reads=[bsc], writes=[bpe])
                            S.op("dve", lambda e: e.tensor_scalar(out=pe_[0:1, 24:32], in0=pe_[0:1, 24:32], scalar1=3.0, scalar2=None, op0=ALU.mult),
                                 reads=[bpe], writes=[bpe])
                            c_["pe"], c_["bpe"] = pe_, bpe
                            return
                        pe_, bpe = c_["pe"], c_["bpe"]

                        def pv(g, np_, vtile, bvt):
                            S.op("pe", lambda e: e.matmul(pacc[:], lhsT=pe_[0:np_, g * 8:(g + 1) * 8], rhs=vtile[0:np_, :], start=(g == 0), stop=(g == 3)),
                                 reads=[bpe, bvt, kv_b2.setdefault(id(bvt), Buf())], writes=[b_pacc])
                            S.op("pe", lambda e: e.matmul(pden[:, 0:1], lhsT=pe_[0:np_, g * 8:(g + 1) * 8], rhs=onesc[0:np_, :], start=(g == 0), stop=(g == 3)),
                                 reads=[bpe, b_k1], writes=[b_pden])
                        for (g, d, np_), (vtile, bvt) in zip(specs, c_["vts"]):
                            pv(g, np_, vtile, bvt)
                        S.op("dve", lambda e: e.reciprocal(out=rden[:], in_=pden[:, 0:1]), reads=[b_pden], writes=[b_rden])
                        S.op("dve", lambda e: e.scalar_tensor_tensor(out=msk8[:], in0=pacc[:], scalar=rden[:], in1=dgm[:],
                                                                    op0=ALU.mult, op1=ALU.mult),
                             reads=[b_pacc, b_rden, b_k1], writes=[b_msk8])
                        S.op("pe", lambda e: e.matmul(prow[:], lhsT=sel[:, bt * 16:(bt + 1) * 16], rhs=msk8[:],
                                                      start=(bt == 0), stop=(bt == 15)),
                             reads=[b_msk8, b_k1], writes=[b_prow])
                        sctx.pop(bt)

                    def sample_finish():
                        S.op("dve", lambda e: e.tensor_copy(out=attb[0:16, :], in_=prow[:]), reads=[b_prow], writes=[b_attb])
                        for j in range(4):
                            S.op("pe", lambda e, j=j: e.transpose(out=pT[:, j, :], in_=attb[:, j * 128:(j + 1) * 128], identity=ident[:]),
                                 reads=[b_attb, b_c], writes=[b_pT], sig=(j == 3))
                        S.op("act", lambda e: e.activation(out=mixTs[:, 0:4, :], in_=pT[:, 0:4, :], func=AF.Copy), reads=[b_pT], writes=[b_out])

                    NO = len(order)
                    for k in range(3):
                        issue_load(order[k])
                    for k in range(3):
                        f1a(order[k], "act")
                    f1a(order[0], "recip"); f1a(order[1], "recip")
                    f1b(order[0], "cast"); f1b(order[0], "T"); f2(order[0])
                    f1b(order[1], "cast"); f1b(order[1], "T")
                    lna(order[0], "stats")
                    f1b(order[2], "cast")
                    issue_load(order[3])
                    for i, t in enumerate(order):
                        if i + 4 < NO:
                            issue_load(order[i + 4])
                        if 1 <= i <= 16:
                            sample_unit(i - 1, "kdma")
                        if i + 2 < NO:
                            f1b(order[i + 2], "T")
                        if i >= 1:
                            tileA_back(order[i - 1], "e")
                            ctxs.pop(order[i - 1])
                        if i + 1 < NO:
                            f2(order[i + 1])
                        if i + 2 < NO:
                            f1a(order[i + 2], "recip")
                        tileA_back(t, "a")
                        lna(t, "recip")
                        tileA_back(t, "c")
                        tileA_back(t, "b")
                        tileA_back(t, "d")
                        if 2 <= i <= 17:
                            sample_unit(i - 2, "s2")
                        if 1 <= i <= 16:
                            sample_unit(i - 1, "vdma")
                        wq_issue(2)
                        if i + 3 < NO:
                            f1a(order[i + 3], "act")
                            f1b(order[i + 3], "cast")
                        if 1 <= i <= 16:
                            sample_unit(i - 1, "s1")
                        if i + 1 < NO:
                            lna(order[i + 1], "stats")
                        if i == 17:
                            sample_finish()
                    tileA_back(order[-1], "e")
                    ctxs.pop(order[-1])
                    wq_issue(100)
                    S.barrier()
                    if KSTOP <= 1:
                        S.emit(); raise _Stop(nc)

                with ExitStack() as s2:
                    mk_f = sbt(s2, "mk_f", [128, 2, 512], F32)
                    mk = sbt(s2, "mk", [128, 2, 512], BF16)
                    onp_f = sbt(s2, "onp_f", [128, 256], F32)
                    onp = sbt(s2, "onp", [128, 2, 128], BF16)
                    accs = [(sbt(s2, f"accN{i}", [128, SH], F32), sbt(s2, f"accD{i}", [128, SH], F32), Buf(), Buf()) for i in range(2)]
                    PT_r = Ring([sbt(s2, f"PT{i}", [128, 512], BF16) for i in range(4)])
                    V_r = Ring([sbt(s2, f"Vt{i}", [128, 2, 2, 128], BF16) for i in range(6)])
                    ST_r = Ring([pst(s2, f"ST{i}", [128, 2, 512]) for i in range(3)])
                    NUM_r = Ring([pst(s2, f"NUM{i}", [128, 512]) for i in range(1)])
                    DEN_r = Ring([pst(s2, f"DEN{i}", [128, 512]) for i in range(1)])
                    b_k2 = Buf()
                    b_k2a = Buf(); b_k2b = Buf()
                    S.dma(mk_f[:], amask.rearrange("t p n -> p t n"), writes=[b_k2a])
                    S.dma(onp_f[:], onesp, writes=[b_k2b])
                    S.op("dve", lambda e: e.tensor_copy(out=mk[:], in_=mk_f[:]), reads=[b_k2a], writes=[b_k2])
                    S.op("dve", lambda e: e.tensor_copy(out=onp[:].rearrange("p h n -> p (h n)"), in_=onp_f[:]), reads=[b_k2b, b_k2], writes=[b_k2])

                    def blocks_of(d, G):
                        if d == 1:
                            return [128 * (4 * G + j) for j in range(4)]
                        if d == 4:
                            return [512 * G + r for r in range(4)]
                        return [4 * G + r for r in range(4)]

                    def acc_view(acc, d, G):
                        if d == 1:
                            return acc[:, 512 * G:512 * (G + 1)].rearrange("p (j i) -> p j i", j=4)
                        if d == 4:
                            return acc[:, 512 * G:512 * (G + 1)].rearrange("p (i r) -> p r i", r=4)
                        return acc[:].rearrange("p (i r) -> p r i", r=16)[:, 4 * G:4 * G + 4, :]

                    DILS = tuple(int(v) for v in os.environ.get("KDILS", "1,4,16").split(","))
                    blks = []
                    for c in range(4):
                        for di, d in enumerate(DILS):
                            for G in range(4):
                                for j, q0 in enumerate(blocks_of(d, G)):
                                    blks.append(dict(c=c, di=di, d=d, G=G, j=j, q0=q0))
                    nblk = len(blks)

                    v_b2 = {}

                    def att_vload(B):
                        d, c = B["d"], B["c"]
                        kc0 = 2048 + B["q0"]
                        kp0 = kc0 - 128 * d
                        Vt, bV = V_r.next()
                        bV2 = v_b2.setdefault(id(bV), Buf())
                        for kt, k0 in enumerate((kp0, kc0)):
                            S.dma(Vt[:, kt, :, :].rearrange("p h n -> p (h n)"),
                                  vsc[k0:k0 + 127 * d + 1:d, 256 * c:256 * (c + 1)], writes=[bV if kt == 0 else bV2])
                        B["Vt"], B["bV"], B["bV2"] = Vt, bV, bV2

                    def att_front(B, idx):
                        d, c, q0 = B["d"], B["c"], B["q0"]
                        kc0 = 2048 + q0
                        kp0 = kc0 - 128 * d
                        STp, bS = ST_r.next()
                        n = 0
                        for hh in range(2):
                            for kt, k0 in enumerate((kp0, kc0)):
                                n += 1
                                S.op("pe", lambda e, hh=hh, kt=kt, k0=k0: e.matmul(
                                    STp[:, hh, kt * 128:(kt + 1) * 128],
                                    lhsT=kT[hh * 64:(hh + 1) * 64, c, k0:k0 + 127 * d + 1:d],
                                    rhs=qT[hh * 64:(hh + 1) * 64, c, q0:q0 + 127 * d + 1:d], start=True, stop=True),
                                    writes=[bS], sig=(n == 4))
                        PT, bP = PT_r.next()
                        S.op("act", lambda e: e.activation(out=PT[:].rearrange("p (h x) -> p h x", h=2), in_=STp[:, :, 0:256], func=AF.Exp, scale=0.125),
                             reads=[bS], writes=[bP])
                        mi = 1 if kp0 < 2048 else 0
                        S.op("dve" if idx % 4 != 3 else "pool", lambda e: e.tensor_tensor(out=PT[:], in0=PT[:], in1=mk[:, mi, :], op=ALU.mult),
                             reads=[bP, b_k2], writes=[bP])
                        B["PT"], B["bP"] = PT, bP

                    cur = {}

                    def att_back(B):
                        c, di, d, G, j = B["c"], B["di"], B["d"], B["G"], B["j"]
                        PT, bP, Vt, bV, bV2 = B["PT"], B["bP"], B["Vt"], B["bV"], B["bV2"]
                        if j == 0:
                            cur["NUM"], cur["bN"] = NUM_r.next()
                            cur["DEN"], cur["bD"] = DEN_r.next()
                        NUM, bN, DEN, bD = cur["NUM"], cur["bN"], cur["DEN"], cur["bD"]
                        n = 0
                        for hh in range(2):
                            for kt in range(2):
                                n += 1
                                S.op("pe", lambda e, hh=hh, kt=kt, n=n: e.matmul(
                                    NUM[:, j * 128:(j + 1) * 128], lhsT=Vt[:, kt, hh, :],
                                    rhs=PT[:, (hh * 2 + kt) * 128:(hh * 2 + kt + 1) * 128], start=(n == 1), stop=(n == 4)),
                                    reads=[bP, bV, bV2], writes=[bN], sig=False)
                        n = 0
                        for hh in range(2):
                            for kt in range(2):
                                n += 1
                                S.op("pe", lambda e, hh=hh, kt=kt, n=n: e.matmul(
                                    DEN[:, j * 128:(j + 1) * 128], lhsT=onp[:, hh, :],
                                    rhs=PT[:, (hh * 2 + kt) * 128:(hh * 2 + kt + 1) * 128], start=(n == 1), stop=(n == 4)),
                                    reads=[bP, bV, bV2, b_k2], writes=[bN, bD], sig=(n == 4))
                        if j != 3:
                            return
                        aN, aD, baN, baD = accs[c % 2]
                        nv_ = acc_view(aN, d, G)
                        dv_ = acc_view(aD, d, G)
                        N3 = NUM[:].rearrange("p (j i) -> p j i", j=4)
                        D3 = DEN[:].rearrange("p (j i) -> p j i", j=4)
                        if di == 0:
                            S.op("act", lambda e: e.activation(out=nv_, in_=N3, func=AF.Copy), reads=[bN], writes=[baN])
                            S.op("dve", lambda e: e.tensor_copy(out=dv_, in_=D3), reads=[bD], writes=[baD])
                        else:
                            S.op("dve", lambda e: e.tensor_tensor(out=nv_, in0=N3, in1=nv_, op=ALU.add), reads=[bN, baN], writes=[baN])
                            S.op("dve", lambda e: e.tensor_tensor(out=dv_, in0=D3, in1=dv_, op=ALU.add), reads=[bD, baD], writes=[baD])
                        if di == len(DILS) - 1 and G == 3:
                            for G2 in range(4):
                                att_final(c, G2)

                    def att_final(c, G):
                        aN, aD, baN, baD = accs[c % 2]
                        sl = slice(512 * G, 512 * (G + 1))
                        S.op("dve", lambda e: e.reciprocal(out=aD[:, sl], in_=aD[:, sl]), reads=[baD], writes=[baD])
                        S.op("pool", lambda e: e.tensor_tensor(out=mixT[:, c, sl], in0=aN[:, sl], in1=aD[:, sl], op=ALU.mult),
                             reads=[baN, baD], writes=[baN, baD])

                    for k in range(min(4, nblk)):
                        att_vload(blks[k])
                    att_front(blks[0], 0)
                    if nblk > 1:
                        att_front(blks[1], 1)
                    for idx in range(nblk):
                        if idx + 4 < nblk:
                            att_vload(blks[idx + 4])
                        if idx + 2 < nblk:
                            att_front(blks[idx + 2], idx + 2)
                        att_back(blks[idx])
                    S.barrier()
                    if KSTOP <= 2:
                        S.emit(); raise _Stop(nc)

            with ExitStack() as s3:
                wdn = sbt(s3, "wdn", [128, 32, 1024], BF16)
                wo = sbt(s3, "wo", [128, 8, 1024], BF16)
                wg = sbt(s3, "wg", [128, 8, 1024], BF16)
                wp = sbt(s3, "wp", [128, 2, 1024], BF16)
                fgb = sbt(s3, "fgb", [128, 1024], F32)
                wu_r = Ring([sbt(s3, f"wu{i}", [128, 8, 512], BF16) for i in range(2)])
                xh_r = Ring([sbt(s3, f"xh{i}", [128, D], F32) for i in range(4)])
                pt_r = Ring([sbt(s3, f"ptl{i}", [128, 256], F32) for i in range(4)])
                hb = sbt(s3, "hb", [128, D], BF16); b_hb = Buf()
                junk2 = hb; b_junk2 = b_hb
                hT = sbt(s3, "hT", [128, 8, 256], BF16); b_hT = Buf()
                h2T = sbt(s3, "h2T", [128, 8, 128], BF16); b_h2T = Buf()
                actT = sbt(s3, "actT", [128, 32, 256], BF16); b_actT = Buf()
                rl_r = Ring([sbt(s3, f"rl{i}", [128, 256], F32) for i in range(2)])
                gate_r = Ring([sbt(s3, f"gate{i}", [128, 512], F32) for i in range(1)])
                pb16 = sbt(s3, "pb16", [128, 256], BF16); b_pb16 = Buf()
                pTp = sbt(s3, "pTp", [128, 2, 128], BF16); b_pTp = Buf()
                stt_all = [sbt(s3, f"stt{i}", [128, 2, 16], F32) for i in range(2)]; cur_stt = [stt_all[0]]
                pbk = Ring([pst(s3, f"pb{i}", [128, 512]) for i in range(3)])
                ps1 = Ring([pst(s3, "ps1", [128, 512])])
                pT3 = pst(s3, "pT3", [128, 8, 128], BF16); b_pT3 = Buf()
                hb2 = sbt(s3, "hb2", [128, D], BF16); b_hb2 = Buf()
                pa_r = Ring([pst(s3, f"pa{i}", [128, 512]) for i in range(2)])
                pT2 = pst(s3, "pT2", [128, 8, 128], BF16); b_pT2 = Buf()
                b_w = Buf(); b_yo = Buf()
                b_wo = Buf(); b_wg = Buf(); b_wp = Buf(); b_wd = Buf()
                S.dma(wo[:], wo_s, reads=[b_wsc], writes=[b_wo])
                S.dma(fgb[:], rowv[0:1, :].partition_broadcast(128), writes=[b_w])
                b_wdl = [Buf() for _ in range(4)]
                for j in range(4):
                    S.dma(wdn[:, 8 * j:8 * j + 8, :], wd_s[:, 8 * j:8 * j + 8, :], reads=[b_wsc], writes=[b_wdl[j]])
                S.dma(wg[:], wg_s, reads=[b_wsc], writes=[b_wg])
                S.dma(wp[:], wp_s, reads=[b_wsc], writes=[b_wp])

                def stats_and_T(b_st, xh, bxh, si, col, dstT, b_dstT, ncol_off, pTb=None, b_pTb=None, hbb=None, b_hbb=None, gcol0=0):
                    stt = cur_stt[0]
                    pTb = pT2 if pTb is None else pTb
                    b_pTb = b_pT2 if b_pTb is None else b_pTb
                    hbb = hb if hbb is None else hbb
                    b_hbb = b_hb if b_hbb is None else b_hbb
                    S.op("act", lambda e: e.activation(out=hbb[:], in_=xh[:], func=AF.Square, accum_out=stt[:, si, col:col + 1]),
                         reads=[bxh], writes=[b_hbb, b_st])
                    S.op("act", lambda e: e.activation(out=stt[:, si, col + 1:col + 2], in_=stt[:, si, col:col + 1], func=AF.Sqrt,
                                                       scale=1.0 / D, bias=EPS), reads=[b_st], writes=[b_st])
                    S.op("dve", lambda e: e.reciprocal(out=stt[:, si, col + 2:col + 3], in_=stt[:, si, col + 1:col + 2]), reads=[b_st], writes=[b_st])
                    if dstT is None:
                        return
                    S.op("act", lambda e: e.activation(out=hbb[:], in_=xh[:], func=AF.Copy), reads=[bxh], writes=[b_hbb])
                    for c in range(8):
                        S.op("pe", lambda e, c=c: e.transpose(out=pTb[:, c, :], in_=hbb[:, c * 128:(c + 1) * 128], identity=ident[:]),
                             reads=[b_hbb, b_c], writes=[b_pTb], sig=(c == 7))
                    for c in range(8):
                        S.op("dve", lambda e, c=c: e.tensor_scalar(out=dstT[:, c, ncol_off:ncol_off + 128], in0=pTb[:, c, :],
                                                                  scalar1=gn[:, gcol0 + c:gcol0 + c + 1], scalar2=None, op0=ALU.mult),
                             reads=[b_pTb, b_c], writes=[b_dstT], sig=(c == 7))

                def mm_group(out_ap, bout, pairs, reads):
                    n = len(pairs)
                    for i, (l, r) in enumerate(pairs):
                        S.op("pe", lambda e, l=l, r=r, i=i: e.matmul(out_ap, lhsT=l, rhs=r, start=(i == 0), stop=(i == n - 1)),
                             reads=reads, writes=[bout], sig=(i == n - 1))

                def stage1(b_st, xh, bxh, mc, si):
                    stt = cur_stt[0]
                    for hf in range(2):
                        pb_, bpb = ps1.next()
                        mm_group(pb_[:], bpb, [(mc[:, fc, :], wo[:, fc, hf * 512:(hf + 1) * 512]) for fc in range(8)], [b_wo])
                        xs = xh[:, hf * 512:(hf + 1) * 512]
                        S.op("dve", lambda e, pb_=pb_, xs=xs: e.tensor_tensor(out=xs, in0=pb_[:], in1=xs, op=ALU.add),
                             reads=[bpb, bxh], writes=[bxh])
                    stats_and_T(b_st, xh, bxh, si, 0, hT, b_hT, si * 128, pT3, b_pT3, hb2, b_hb2, gcol0=8)
                    S.op("dve", lambda e: e.tensor_tensor(out=stt[:, si, 3:4], in0=stt[:, si, 2:3], in1=stt[:, si, 2:3], op=ALU.mult),
                         reads=[b_st], writes=[b_st])

                wu_pending = []

                def wu_load(u):
                    wu, bwu = wu_r.next()
                    S.dma(wu[:], wu_s[u], reads=[b_wsc], writes=[bwu])
                    wu_pending.append((wu, bwu))

                def stage2_unit(u, T):
                    wu, bwu = wu_pending.pop(0)
                    for f4 in range(4):
                        ffc = 4 * u + f4
                        pa, bpa = pa_r.next()
                        mm_group(pa[:, 0:T], bpa, [(wu[:, kc, f4 * 128:(f4 + 1) * 128], hT[:, kc, 0:T]) for kc in range(8)], [bwu, b_hT])
                        rl, brl = rl_r.next()
                        S.op("act", lambda e, rl=rl, pa=pa: e.activation(out=rl[:, 0:T], in_=pa[:, 0:T], func=AF.Relu), reads=[bpa], writes=[brl])
                        S.op("pool" if ffc % 2 else "dve",
                             lambda e, rl=rl, ffc=ffc: e.tensor_tensor(out=actT[:, ffc, 0:T], in0=rl[:, 0:T], in1=rl[:, 0:T], op=ALU.mult),
                             reads=[brl], writes=[b_actT])

                def stage34(b_st, xh, bxh, ptl, bpt, si, s):
                    stt = cur_stt[0]
                    for hf in range(2):
                        pb_, bpb = pbk.next()
                        mm_group(pb_[:], bpb, [(actT[:, ffc, si * 128:(si + 1) * 128], wdn[:, ffc, hf * 512:(hf + 1) * 512]) for ffc in range(32)],
                                 [b_actT] + b_wdl)
                        xs = xh[:, hf * 512:(hf + 1) * 512]
                        S.op("dve", lambda e, pb_=pb_, xs=xs: e.scalar_tensor_tensor(out=xs, in0=pb_[:], scalar=stt[:, si, 3:4], in1=xs,
                                                                                    op0=ALU.mult, op1=ALU.add),
                             reads=[bpb, bxh, b_st], writes=[bxh])
                    stats_and_T(b_st, xh, bxh, si, 4, h2T, b_h2T, 0, gcol0=16)
                    S.op("pool", lambda e: e.tensor_copy(out=pb16[:], in_=ptl[:]), reads=[bpt], writes=[b_pb16])
                    for c in range(2):
                        S.op("pe", lambda e, c=c: e.transpose(out=pT2[:, c, :], in_=pb16[:, c * 128:(c + 1) * 128], identity=ident[:]),
                             reads=[b_pb16, b_c], writes=[b_pT2], sig=(c == 1))
                    S.op("dve", lambda e: e.tensor_copy(out=pTp[:], in_=pT2[:, 0:2, :]), reads=[b_pT2], writes=[b_pTp])
                    for hf in range(2):
                        pg, bpg = pbk.next()
                        mm_group(pg[:], bpg, [(h2T[:, kc, :], wg[:, kc, hf * 512:(hf + 1) * 512]) for kc in range(8)], [b_h2T, b_wg])
                        gt, bgt = gate_r.next()
                        S.op("act", lambda e, gt=gt, pg=pg: e.activation(out=gt[:], in_=pg[:], func=AF.Sigmoid, scale=stt[:, si, 6:7]),
                             reads=[bpg, b_st], writes=[bgt])
                        pp, bpp = pbk.next()
                        mm_group(pp[:], bpp, [(pTp[:, kc, :], wp[:, kc, hf * 512:(hf + 1) * 512]) for kc in range(2)], [b_pTp, b_wp])
                        xs = xh[:, hf * 512:(hf + 1) * 512]
                        S.op("dve", lambda e, gt=gt, pp=pp: e.tensor_tensor(out=gt[:], in0=pp[:], in1=gt[:], op=ALU.mult), reads=[bpp, bgt], writes=[bgt])
                        S.op("dve", lambda e, gt=gt, xs=xs: e.tensor_tensor(out=xs, in0=gt[:], in1=xs, op=ALU.add), reads=[bgt, bxh], writes=[bxh])
                    stats_and_T(b_st, xh, bxh, si, 8, None, None, 0)
                    S.op("dve", lambda e: e.scalar_tensor_tensor(out=xh[:], in0=xh[:], scalar=stt[:, si, 10:11], in1=fgb[:], op0=ALU.mult, op1=ALU.mult),
                         reads=[bxh, b_st, b_w], writes=[bxh])
                    S.dma(y_o[s], xh[:], reads=[bxh], writes=[b_yo])

                passes = [([2 * i, 2 * i + 1], False) for i in range(8)] + [([16], True)]
                pinfo = {}

                def pass_loads(p):
                    subs, samp = passes[p]
                    tiles = []
                    for si, s in enumerate(subs):
                        xh, bxh = xh_r.next()
                        ptl, bpt = pt_r.next()
                        S.dma(xh[:], xall[32 if samp else 16 + s], writes=[bxh])
                        S.dma(ptl[:], pall[s], writes=[bpt])
                        tiles.append((xh, bxh, ptl, bpt))
                    pinfo[p] = (tiles, Buf(), stt_all[p % 2])

                def pass_stage1(p):
                    subs, samp = passes[p]
                    tiles, b_st, sttp = pinfo[p]
                    cur_stt[0] = sttp
                    for si, s in enumerate(subs):
                        xh, bxh, ptl, bpt = tiles[si]
                        mc = mixTs[:, :, :] if samp else mixT[:, :, s * 128:(s + 1) * 128]
                        stage1(b_st, xh, bxh, mc, si)

                def pass_stage34(p):
                    subs, samp = passes[p]
                    tiles, b_st, sttp = pinfo[p]
                    cur_stt[0] = sttp
                    for si, s in enumerate(subs):
                        xh, bxh, ptl, bpt = tiles[si]
                        stage34(b_st, xh, bxh, ptl, bpt, si, s)

                pass_loads(0)
                wu_load(0); wu_load(1)
                pass_stage1(0)
                for p in range(len(passes)):
                    subs, samp = passes[p]
                    T = 128 * len(subs)
                    if p + 1 < len(passes):
                        pass_loads(p + 1)
                    for u in range(8):
                        stage2_unit(u, T)
                        g_next = p * 8 + u + 2
                        if g_next < 8 * len(passes):
                            wu_load(g_next % 8)
                    streams = [S.capture(pass_stage34, p)]
                    if p + 1 < len(passes):
                        streams.append(S.capture(pass_stage1, p + 1))
                    S.replay(streams)
                S.barrier()
        except ZeroDivisionError:
            pass
        S.emit()
    return nc


_PROGRAM = None


def _rope_tables(pos):
    pos = np.asarray(pos, dtype=np.float32)
    inv = (np.float32(500000.0) ** (-np.arange(0, 16, 2, dtype=np.float32) / np.float32(16))).astype(np.float32)
    ang = (pos[:, None] * inv[None, :]).astype(np.float32)
    return np.concatenate([np.cos(ang).astype(np.float32), np.sin(ang).astype(np.float32)], axis=1)


def kernel(x_prompt, x_sample, cache_k, cache_v, p_prompt, p_sample,
           norm1_g, w_in, ln_v_g, ln_v_b, w_spatial, b_spatial, w_out,
           norm2_g, w_up, w_down, gate_norm_g, w_gate, w_ple, final_g):
    global _PROGRAM
    f32 = np.float32
    x_prompt = np.asarray(x_prompt, f32); x_sample = np.asarray(x_sample, f32)
    cache_k = np.asarray(cache_k, f32); cache_v = np.asarray(cache_v, f32)
    p_prompt = np.asarray(p_prompt, f32); p_sample = np.asarray(p_sample, f32)
    xp = x_prompt[0]
    pp = p_prompt[0, 0]
    def gl(g):
        return np.ascontiguousarray(np.asarray(g, f32).reshape(8, 128).T)
    gains = np.concatenate([gl(norm1_g[0]), gl(norm2_g[0]), gl(gate_norm_g[0])], axis=1)
    rowv = np.zeros((3, 1024), f32)
    rowv[0] = np.asarray(final_g, f32)
    rowv[1, :512] = np.asarray(ln_v_g[0], f32)
    rowv[1, 512:] = np.asarray(ln_v_b[0], f32)
    ws = np.asarray(w_spatial[0], f32)
    bs = np.asarray(b_spatial[0], f32)
    wsT = np.zeros((2, 128, 8, 128), f32)
    wsT[0] = np.transpose(ws, (2, 0, 1))
    trilm = np.zeros((2, 128, 128), f32)
    trilm[0] = np.triu(np.ones((128, 128), f32))
    bsp = np.zeros((2, 128, 8), f32)
    bsp[0] = bs.T
    for b in range(4):
        for j in range(4):
            for i in range(4):
                wsT[1, 4 * b + j, :, 4 * b + i] = ws[:, i, j]
                if j <= i:
                    trilm[1, 4 * b + j, 4 * b + i] = 1.0
        bsp[1, 4 * b:4 * b + 4, :] = bs[:, :4].T
    wsT = wsT.reshape(2, 128, 1024)
    ident = np.eye(128, dtype=f32)
    ik = np.arange(128)[:, None]; iq = np.arange(128)[None, :]
    m_prev = (ik >= iq).astype(f32); m_cur = (ik <= iq).astype(f32)
    mN = np.concatenate([m_prev, m_cur, m_prev, m_cur], axis=1)
    mH = np.concatenate([np.zeros_like(m_prev), m_cur, np.zeros_like(m_prev), m_cur], axis=1)
    onesp = np.zeros((128, 2, 128), f32)
    onesp[:, 0, :64] = 1.0; onesp[:, 1, 64:] = 1.0
    onesp = onesp.reshape(128, 256)
    diagm = np.zeros((8, 8, 64), f32)
    for h in range(8):
        diagm[h, h, :] = 1.0
    diagm = diagm.reshape(8, 512)
    selc = np.zeros((8, 16, 16), f32)
    for bt in range(16):
        selc[:, bt, bt] = 1.0
    selc = selc.reshape(8, 256)
    shared = dict(w_in=np.ascontiguousarray(w_in[0], f32), w_out=np.ascontiguousarray(w_out[0], f32),
                  w_up=np.ascontiguousarray(w_up[0], f32), w_down=np.ascontiguousarray(w_down[0], f32),
                  w_gate=np.ascontiguousarray(w_gate[0], f32), w_ple=np.ascontiguousarray(w_ple[0], f32),
                  gains=gains, rowv=rowv, wsT=wsT, trilm=trilm, bsp=bsp, ident=ident,
                  onesp=onesp, diagm=diagm, selc=selc)
    in_maps = []
    for c in range(NCORES):
        xall = np.zeros((33, 128, D), f32)
        if c > 0:
            xall[0:16] = xp[(c - 1) * SH:c * SH].reshape(16, 128, D)
        xall[16:32] = xp[c * SH:(c + 1) * SH].reshape(16, 128, D)
        xall[32, :16] = x_sample[4 * c:4 * c + 4].reshape(16, D)
        pall = np.zeros((17, 128, 256), f32)
        pall[:16] = pp[c * SH:(c + 1) * SH].reshape(16, 128, 256)
        pall[16, :16] = p_sample[0, 4 * c:4 * c + 4].reshape(16, 256)
        pos = np.zeros((33, 128), f32)
        pos[:32] = ((c - 1) * SH + np.arange(2 * SH)).reshape(32, 128)
        pos[32, :16] = 16384 + np.tile(np.arange(4), 4)
        cs = _rope_tables(pos.reshape(-1)).reshape(33, 128, 16)
        m = dict(shared)
        m.update(xall=xall, pall=pall, cs=cs,
                 ck=np.ascontiguousarray(cache_k[0, 4 * c:4 * c + 4].reshape(4, 2048, 512)),
                 cv=np.ascontiguousarray(cache_v[0, 4 * c:4 * c + 4].reshape(4, 2048, 512)),
                 amask=np.stack([mN, mN if c > 0 else mH], axis=0))
        in_maps.append(m)
    if _PROGRAM is None:
        try:
            _PROGRAM = build_program()
        except _Stop as e_:
            _PROGRAM = e_.args[0]
    res = run_bass_kernel_spmd(_PROGRAM, in_maps, core_ids=list(range(NCORES)))
    R = res.results
    y_prompt = np.concatenate([R[c]["y"][:16].reshape(SH, D) for c in range(NCORES)], 0)[None]
    y_sample = np.concatenate([R[c]["y"][16, :16].reshape(4, 4, D) for c in range(NCORES)], 0)
    nkp = R[7]["nk"][:16].reshape(1, 1, SH, 8, 64)
    nvp = R[7]["nv"][:16].reshape(1, 1, SH, 8, 64)
    nks = np.concatenate([R[c]["nk"][16, :16].reshape(4, 4, 8, 64) for c in range(NCORES)], 0)[None]
    nvs = np.concatenate([R[c]["nv"][16, :16].reshape(4, 4, 8, 64) for c in range(NCORES)], 0)[None]
    nvc = np.concatenate([R[c]["nvc"][:16].reshape(4, 4, 512) for c in range(NCORES)], 0)[None]
    return (y_prompt.astype(f32), y_sample.astype(f32), nkp.astype(f32), nvp.astype(f32),
            nks.astype(f32), nvs.astype(f32), nvc.astype(f32))
```

```python
import numpy as np
from contextlib import ExitStack
import concourse.bass as bass
import concourse.mybir as mybir
from concourse.bass_utils import run_bass_kernel_spmd

F32 = mybir.dt.float32
BF16 = mybir.dt.bfloat16
AF = mybir.ActivationFunctionType
ALU = mybir.AluOpType
AX = mybir.AxisListType

ENGS = ["pe", "act", "dve", "pool", "sp"]
NCORES = 8
D = 1024
SH = 2048
NT = 16
EPS = 1e-6
LN3 = float(np.log(3.0))


import os
KSTOP = int(os.environ.get("KSTOP", "99"))


class _Stop(Exception):
    pass


class Buf:
    __slots__ = ("w", "r")

    def __init__(self):
        self.w = None
        self.r = []


class Sched:
    def __init__(self, nc, stack, n_dma_sems=24):
        self.nc = nc
        self.ops = {e: [] for e in ENGS}
        self.sem = {}
        self.cnt = {e: 0 for e in ENGS}
        for e in ENGS:
            self.sem[e] = stack.enter_context(nc.semaphore("s_" + e))
        self.dsems = {}
        self.dpos = {}
        for q, n in (("sp", n_dma_sems), ("act", 8), ("dve", 8), ("pool", 8)):
            self.dsems[q] = [[stack.enter_context(nc.semaphore(f"d_{q}{i}")), 0] for i in range(n)]
            self.dpos[q] = 0
        self.known = {}
        self.cap = None

    def capture(self, fn, *args):
        lst = []
        self.cap = lst
        fn(*args)
        self.cap = None
        return lst

    def replay(self, lists):
        lists = [l for l in lists if l]
        pos = [0] * len(lists)
        while True:
            best, bf = -1, 2.0
            for i, l in enumerate(lists):
                if pos[i] < len(l):
                    f = pos[i] / len(l)
                    if f < bf:
                        best, bf = i, f
            if best < 0:
                break
            kind, a, kw = lists[best][pos[best]]
            pos[best] += 1
            if kind == "op":
                self.op(*a, **kw)
            else:
                self.dma(*a, **kw)

    def _wait(self, eng, tok):
        if tok is None:
            return
        key, semobj, val, prod = tok
        if prod == eng and eng == "pe":
            return
        kk = (eng, key)
        if self.known.get(kk, 0) >= val:
            return
        self.known[kk] = val
        self.ops[eng].append(lambda e, s=semobj, v=val: e.wait_ge(s, v))

    def _deps(self, eng, reads, writes):
        for b in reads:
            self._wait(eng, b.w)
        for b in writes:
            self._wait(eng, b.w)
            for t in b.r:
                self._wait(eng, t)

    def _commit(self, tok, reads, writes):
        for b in reads:
            b.r.append(tok)
            if len(b.r) > 24:
                b.r = b.r[-24:]
        for b in writes:
            b.w = tok
            b.r = []

    def op(self, eng, fn, reads=(), writes=(), sig=True):
        if self.cap is not None:
            self.cap.append(("op", (eng, fn), dict(reads=reads, writes=writes, sig=sig)))
            return None
        self._deps(eng, reads, writes)
        if sig:
            self.cnt[eng] += 1
            v = self.cnt[eng]
            s = self.sem[eng]
            self.ops[eng].append(lambda e, f=fn, s=s: f(e).then_inc(s, 1))
            tok = (eng, s, v, eng)
            self._commit(tok, reads, writes)
            return tok
        self.ops[eng].append(lambda e, f=fn: f(e))
        return None

    def dma(self, out, in_, reads=(), writes=(), q="sp"):
        if self.cap is not None:
            self.cap.append(("dma", (out, in_), dict(reads=reads, writes=writes, q=q)))
            return None
        self._deps(q, reads, writes)
        pool = self.dsems[q]
        i = self.dpos[q]
        self.dpos[q] = (i + 1) % len(pool)
        ent = pool[i]
        key = f"d_{q}{i}"
        if ent[1] > 0:
            self._wait(q, (key, ent[0], ent[1], None))
        ent[1] += 16
        s, v = ent[0], ent[1]
        self.ops[q].append(lambda e, o=out, i_=in_, s=s: e.dma_start(out=o, in_=i_).then_inc(s, 16))
        tok = (key, s, v, None)
        self._commit(tok, reads, writes)
        return tok

    def barrier(self):
        toks = []
        for e in ENGS:
            if self.cnt[e] > 0:
                toks.append((e, self.sem[e], self.cnt[e], e))
        for q, pool in self.dsems.items():
            for i, ent in enumerate(pool):
                if ent[1] > 0:
                    toks.append((f"d_{q}{i}", ent[0], ent[1], None))
        for e in ENGS:
            for t in toks:
                if t[3] == e:
                    continue
                self._wait(e, t)

    def emit(self):
        nc = self.nc
        with nc.Block() as block:
            @block.tensor
            def _(e):
                for f in self.ops["pe"]:
                    f(e)

            @block.scalar
            def _(e):
                for f in self.ops["act"]:
                    f(e)

            @block.vector
            def _(e):
                for f in self.ops["dve"]:
                    f(e)

            @block.gpsimd
            def _(e):
                for f in self.ops["pool"]:
                    f(e)

            @block.sync
            def _(e):
                for f in self.ops["sp"]:
                    f(e)


class Ring:
    def __init__(self, tiles):
        self.tiles = tiles
        self.bufs = [Buf() for _ in tiles]
        self.i = 0

    def next(self):
        j = self.i % len(self.tiles)
        self.i += 1
        return self.tiles[j], self.bufs[j]


def build_program():
    nc = bass.Bass("TRN2", target_bir_lowering=False)

    def din(name, shape, dt=F32):
        return nc.dram_tensor(name, list(shape), dt, kind="ExternalInput").ap()

    def dout(name, shape, dt=F32):
        return nc.dram_tensor(name, list(shape), dt, kind="ExternalOutput").ap()

    def dscr(name, shape, dt):
        return nc.dram_tensor(name, list(shape), dt).ap()

    xall = din("xall", [33, 128, D])
    pall = din("pall", [17, 128, 256])
    cs = din("cs", [33, 128, 16])
    ck = din("ck", [4, 2048, 512])
    cv = din("cv", [4, 2048, 512])
    w_in = din("w_in", [D, 2560])
    w_out = din("w_out", [D, D])
    w_up = din("w_up", [D, 4096])
    w_down = din("w_down", [4096, D])
    w_gate = din("w_gate", [D, D])
    w_ple = din("w_ple", [256, D])
    gains = din("gains", [128, 24])
    rowv = din("rowv", [3, 1024])
    wsT = din("wsT", [2, 128, 1024])
    trilm = din("trilm", [2, 128, 128])
    bsp = din("bsp", [2, 128, 8])
    ident_in = din("ident", [128, 128])
    amask = din("amask", [2, 128, 512])
    onesp = din("onesp", [128, 256])
    diagm = din("diagm", [8, 512])
    selc = din("selc", [8, 256])

    y_o = dout("y", [17, 128, D])
    nk_o = dout("nk", [17, 128, 512])
    nv_o = dout("nv", [17, 128, 512])
    nvc_o = dout("nvc", [128, 512])

    vsc = dscr("vsc", [4096, 1024], BF16)
    qsc = dscr("qsc", [16, 512], F32)
    wo_s = dscr("wo_s", [128, 8, 1024], BF16)
    wg_s = dscr("wg_s", [128, 8, 1024], BF16)
    wp_s = dscr("wp_s", [128, 2, 1024], BF16)
    wd_s = dscr("wd_s", [128, 32, 1024], BF16)
    wu_s = dscr("wu_s", [8, 128, 8, 512], BF16)

    with ExitStack() as st:
        S = Sched(nc, st)

        def sbt(stack, name, shape, dt):
            return stack.enter_context(nc.sbuf_tensor("sb_" + name, list(shape), dt))

        def pst(stack, name, shape, dt=F32):
            return stack.enter_context(nc.psum_tensor("ps_" + name, list(shape), dt))

        try:
            ident_f = sbt(st, "ident_f", [128, 128], F32)
            ident = sbt(st, "ident", [128, 128], BF16)
            gn = sbt(st, "gn", [128, 24], F32)
            mixT = sbt(st, "mixT", [128, 8, SH], BF16)
            mixTs = sbt(st, "mixTs", [128, 8, 128], BF16)
            b_c = Buf()
            b_c1 = Buf(); b_c2 = Buf()
            S.dma(ident_f[:], ident_in, writes=[b_c1])
            S.dma(gn[:], gains, writes=[b_c2])
            S.op("dve", lambda e: e.tensor_copy(out=ident[:], in_=ident_f[:]), reads=[b_c1, b_c2], writes=[b_c])
            S.op("pool", lambda e: e.memset(mixTs[:], 0.0), writes=[b_c])

            with ExitStack() as sa:
                winb = sbt(sa, "winb", [128, 8, 2560], BF16)
                b_win = Buf()
                b_wblk = [[Buf() for _ in range(8)] for _ in range(5)]
                for blk_ in (1, 2, 0, 3, 4):
                    for kc in range(8):
                        S.dma(winb[:, kc, blk_ * 512:(blk_ + 1) * 512], w_in[kc * 128:(kc + 1) * 128, blk_ * 512:(blk_ + 1) * 512],
                              writes=[b_wblk[blk_][kc]], q="pool")
                b_wsc = Buf()
                wq = []
                for j in range(4):
                    wq.append((wo_s[:, 2 * j:2 * j + 2, :], w_out[j * 256:(j + 1) * 256, :].rearrange("(c p) n -> p c n", p=128)))
                for kc in range(8):
                    for hf in range(2):
                        wq.append((wu_s[4 * hf:4 * hf + 4, :, kc, :].rearrange("u p n -> p u n"),
                                   w_up[kc * 128:(kc + 1) * 128, hf * 2048:(hf + 1) * 2048].rearrange("p (u n) -> p u n", u=4)))
                for j in range(16):
                    wq.append((wd_s[:, 2 * j:2 * j + 2, :], w_down[j * 256:(j + 1) * 256, :].rearrange("(c p) n -> p c n", p=128)))
                for j in range(4):
                    wq.append((wg_s[:, 2 * j:2 * j + 2, :], w_gate[j * 256:(j + 1) * 256, :].rearrange("(c p) n -> p c n", p=128)))
                wq.append((wp_s[:, :, :], w_ple.rearrange("(c p) n -> p c n", p=128)))

                def wq_issue(n):
                    for _ in range(n):
                        if wq:
                            o_, i_ = wq.pop(0)
                            S.dma(o_, i_, writes=[Buf()], q="pool")

                kT = sbt(sa, "kT", [128, 4, 4096], BF16)
                qT = sbt(sa, "qT", [128, 4, SH], BF16)
                with ExitStack() as s1:
                    xr = Ring([sbt(s1, f"xt{i}", [128, D], F32) for i in range(3)])
                    csr = Ring([sbt(s1, f"cst{i}", [128, 16], F32) for i in range(6)])
                    xb_r = Ring([sbt(s1, f"xb{i}", [128, D], BF16) for i in range(2)])
                    xT_r = Ring([sbt(s1, f"xT{i}", [128, 8, 128], BF16) for i in range(2)])
                    st_r = Ring([sbt(s1, f"st1_{i}", [128, 8], F32) for i in range(4)])
                    st2 = sbt(s1, "st2", [128, 8], F32); b_st2 = Buf()
                    bnst = sbt(s1, "bnst", [128, 6], F32)
                    bnag = sbt(s1, "bnag", [128, 2], F32)
                    qkf_r = Ring([sbt(s1, f"qkf{i}", [128, 1024], F32) for i in range(2)])
                    vf_r = Ring([sbt(s1, f"vf{i}", [128, 512], F32) for i in range(2)])
                    vpad_r = Ring([sbt(s1, f"vpad{i}", [128, 8, 128], BF16) for i in range(1)])
                    rtmp = sbt(s1, "rtmp", [128, 4, 16, 8], F32); b_rtmp = Buf()
                    qkb = sbt(s1, "qkb", [128, 1024], BF16); b_qkb = Buf()
                    ub_r = Ring([sbt(s1, f"ub{i}", [128, 512], BF16) for i in range(2)])
                    vcf_r = Ring([sbt(s1, f"vcf{i}", [128, 512], F32) for i in range(2)])
                    vnf = sbt(s1, "vnf", [128, 512], F32); b_vnf = Buf()
                    vnb = sbt(s1, "vnb", [128, 512], BF16); b_vnb = Buf()
                    gtmp = None; b_gtmp = None
                    gb = sbt(s1, "gb", [128, 512], BF16); b_gb = Buf()
                    lng = sbt(s1, "lng", [128, 1024], F32)
                    wsf = qkf_r.tiles[0]; b_wsf = qkf_r.bufs[0]
                    wsb = sbt(s1, "wsb", [128, 2, 1024], BF16)
                    trm = sbt(s1, "trm", [128, 2, 128], F32)
                    bsb = sbt(s1, "bsb", [128, 2, 8], F32)
                    kt_r = Ring([sbt(s1, f"skt{i}", [128, 512], F32) for i in range(4)])
                    vt_r = Ring([sbt(s1, f"svt{i}", [128, 512], F32) for i in range(4)])
                    qbc_r = Ring([sbt(s1, f"qbc{i}", [128, 512], F32) for i in range(1)])
                    prod = sbt(s1, "prod", [128, 512], F32); b_prod = Buf(); gtmp = prod; b_gtmp = b_prod
                    ssc_r = Ring([sbt(s1, f"ssc{i}", [128, 32], F32) for i in range(2)])
                    pex_r = Ring([sbt(s1, f"pex{i}", [128, 32], F32) for i in range(2)])
                    onesc = sbt(s1, "onesc", [128, 1], F32)
                    rden = sbt(s1, "rden", [8, 1], F32); b_rden = Buf()
                    msk8 = sbt(s1, "msk8", [8, 512], F32); b_msk8 = Buf()
                    dgm = sbt(s1, "dgm", [8, 512], F32)
                    sel = sbt(s1, "sel", [8, 256], F32)
                    attb = sbt(s1, "attb", [128, 512], BF16); b_attb = Buf()
                    pz = Ring([pst(s1, f"pz{i}", [128, 512]) for i in range(2)])
                    pT = pst(s1, "pT", [128, 8, 128], BF16); b_pT = Buf()
                    pTx = pst(s1, "pTx", [128, 8, 128], BF16); b_pTx = Buf()
                    pmix = pst(s1, "pmix", [128, 512]); b_pmix = Buf()
                    pacc = pst(s1, "pacc", [8, 512]); b_pacc = Buf()
                    pden = pst(s1, "pden", [8, 512]); b_pden = Buf()
                    prow = pst(s1, "prow", [16, 512]); b_prow = Buf()

                    b_k1 = Buf()
                    k1l = [Buf() for _ in range(5)]
                    S.dma(lng[:], rowv[1:2, :].partition_broadcast(128), writes=[k1l[0]])
                    S.dma(trm[:], trilm.rearrange("t p n -> p t n"), writes=[k1l[1]])
                    S.dma(bsb[:], bsp.rearrange("t p n -> p t n"), writes=[k1l[2]])
                    S.dma(dgm[:], diagm, writes=[k1l[3]])
                    S.dma(sel[:], selc, writes=[k1l[4]])
                    for t2 in range(2):
                        S.dma(wsf[:], wsT[t2], writes=[b_wsf])
                        S.op("dve", lambda e, t2=t2: e.tensor_tensor(
                            out=wsb[:, t2, :].rearrange("p (g i) -> p g i", g=8),
                            in0=wsf[:].rearrange("p (g i) -> p g i", g=8),
                            in1=trm[:, t2, :].unsqueeze(1).broadcast_to([128, 8, 128]), op=ALU.mult),
                            reads=[b_k1, b_wsf] + k1l, writes=[b_k1])
                    S.op("pool", lambda e: e.memset(onesc[:], 1.0), writes=[b_k1])
                    for (sc_, bsc_) in zip(ssc_r.tiles, ssc_r.bufs):
                        S.op("pool", lambda e, sc_=sc_: e.memset(sc_[:], 0.0), writes=[bsc_])
                    S.op("pool", lambda e: e.memset(attb[:], 0.0), writes=[b_attb])
                    for (vp, bvp) in zip(vpad_r.tiles, vpad_r.bufs):
                        S.op("pool", lambda e, vp=vp: e.memset(vp[:], 0.0), writes=[bvp])

                    b_nk = Buf(); b_nv = Buf(); b_qsc = Buf(); b_vsc = Buf(); b_out = Buf()

                    order = [32] + list(range(32))
                    loads = {}

                    def issue_load(t):
                        xt, bx = xr.next()
                        ct, bct = csr.next()
                        S.dma(xt[:], xall[t], writes=[bx])
                        S.dma(ct[:], cs[t], writes=[bct])
                        loads[t] = (xt, bx, ct, bct)

                    ctxs = {}

                    f1ctx = {}

                    def f1a(t, part):
                        if part == "act":
                            xt, bx, ct, bct = loads.pop(t)
                            xb, b_xb = xb_r.next()
                            xT, b_xT = xT_r.next()
                            st1, b_st1 = st_r.next()
                            S.op("act", lambda e: e.activation(out=qkb[:], in_=xt[:], func=AF.Square, accum_out=st1[:, 0:1]),
                                 reads=[bx], writes=[b_qkb, b_st1])
                            S.op("act", lambda e: e.activation(out=st1[:, 1:2], in_=st1[:, 0:1], func=AF.Sqrt, scale=1.0 / D, bias=EPS),
                                 reads=[b_st1], writes=[b_st1])
                            f1ctx[t] = (xt, bx, ct, bct, xb, b_xb, xT, b_xT, st1, b_st1)
                        else:
                            st1, b_st1 = f1ctx[t][8], f1ctx[t][9]
                            S.op("dve", lambda e: e.reciprocal(out=st1[:, 2:3], in_=st1[:, 1:2]), reads=[b_st1], writes=[b_st1])

                    def f1b(t, part):
                        xt, bx, ct, bct, xb, b_xb, xT, b_xT, st1, b_st1 = f1ctx[t]
                        if part == "cast":
                            S.op("dve", lambda e: e.tensor_copy(out=xb[:], in_=xt[:]), reads=[bx], writes=[b_xb])
                            return
                        for c in range(8):
                            S.op("pe", lambda e, c=c: e.transpose(out=pTx[:, c, :], in_=xb[:, c * 128:(c + 1) * 128], identity=ident[:]),
                                 reads=[b_xb, b_c], writes=[b_pTx], sig=(c == 7))
                        for c in range(8):
                            S.op("dve", lambda e, c=c: e.tensor_scalar(out=xT[:, c, :], in0=pTx[:, c, :], scalar1=gn[:, c:c + 1], scalar2=None, op0=ALU.mult),
                                 reads=[b_pTx, b_c], writes=[b_xT], sig=(c == 7))

                    def f2(t):
                        xt, bx, ct, bct, xb, b_xb, xT, b_xT, st1, b_st1 = f1ctx.pop(t)
                        halo = t < 16
                        rstd = st1[:, 2:3]

                        def proj(col0):
                            pzt, bpz = pz.next()
                            for c in range(8):
                                S.op("pe", lambda e, c=c: e.matmul(pzt[:], lhsT=xT[:, c, :], rhs=winb[:, c, col0:col0 + 512],
                                                                   start=(c == 0), stop=(c == 7)),
                                     reads=[b_xT, b_wblk[col0 // 512][c]], writes=[bpz], sig=(c == 7))
                            return pzt, bpz

                        qkf, bqk = qkf_r.next()
                        vf, bvf = vf_r.next()
                        ub, b_ub = ub_r.next()
                        vcf, b_vcf = vcf_r.next()
                        if not halo:
                            pq, bpq = proj(0)
                            S.op("act", lambda e: e.activation(out=qkf[:, 0:512], in_=pq[:], func=AF.Copy, scale=rstd),
                                 reads=[bpq, b_st1], writes=[bqk])
                        pk, bpk = proj(512)
                        S.op("act", lambda e: e.activation(out=qkf[:, 512:1024], in_=pk[:], func=AF.Copy, scale=rstd),
                             reads=[bpk, b_st1], writes=[bqk])
                        pv, bpv = proj(1024)
                        S.op("act", lambda e: e.activation(out=vf[:], in_=pv[:], func=AF.Copy, scale=rstd),
                             reads=[bpv, b_st1], writes=[bvf])
                        if not halo:
                            pu, bpu = proj(1536)
                            S.op("act", lambda e: e.activation(out=ub[:], in_=pu[:], func=AF.Gelu, scale=rstd),
                                 reads=[bpu, b_st1], writes=[b_ub])
                            pc, bpc = proj(2048)
                            S.op("act", lambda e: e.activation(out=vcf[:], in_=pc[:], func=AF.Gelu, scale=rstd),
                                 reads=[bpc, b_st1], writes=[b_vcf])
                        ctxs[t] = (ct, bct, qkf, bqk, vf, bvf, ub, b_ub, vcf, b_vcf)

                    def lna(t, part):
                        if t < 16:
                            return
                        vcf, b_vcf = ctxs[t][8], ctxs[t][9]
                        if part == "stats":
                            S.op("dve", lambda e: e.bn_stats(out=bnst[:], in_=vcf[:]), reads=[b_vcf], writes=[b_st2])
                            S.op("dve", lambda e: e.bn_aggr(out=bnag[:], in_=bnst[:]), reads=[b_st2], writes=[b_st2])
                            S.op("act", lambda e: e.activation(out=st2[:, 3:4], in_=bnag[:, 1:2], func=AF.Sqrt, scale=1.0, bias=EPS),
                                 reads=[b_st2], writes=[b_st2])
                        else:
                            S.op("dve", lambda e: e.reciprocal(out=st2[:, 4:5], in_=st2[:, 3:4]), reads=[b_st2], writes=[b_st2])

                    bctx = {}

                    def tileA_back(t, part):
                        ct, bct, qkf, bqk, vf, bvf, ub, b_ub, vcf, b_vcf = ctxs[t]
                        halo = t < 16
                        samp = t == 32
                        h0 = 8 if halo else 0
                        nh = 16 - h0
                        if part == "a":
                            v3 = qkf[:].rearrange("p (h d) -> p h d", d=64)
                            x1 = v3[:, h0:16, 0:8]
                            x2 = v3[:, h0:16, 8:16]
                            cosb = ct[:, 0:8].unsqueeze(1).broadcast_to([128, nh, 8])
                            sinb = ct[:, 8:16].unsqueeze(1).broadcast_to([128, nh, 8])
                            tm = [rtmp[:, i, h0:16, :] for i in range(4)]
                            S.op("dve", lambda e: e.tensor_tensor(out=tm[0], in0=x1, in1=cosb, op=ALU.mult), reads=[bqk, bct], writes=[b_rtmp])
                            S.op("dve", lambda e: e.tensor_tensor(out=tm[1], in0=x2, in1=sinb, op=ALU.mult), reads=[bqk, bct], writes=[b_rtmp])
                            S.op("dve", lambda e: e.tensor_tensor(out=tm[2], in0=x2, in1=cosb, op=ALU.mult), reads=[bqk, bct], writes=[b_rtmp])
                            S.op("dve", lambda e: e.tensor_tensor(out=tm[3], in0=x1, in1=sinb, op=ALU.mult), reads=[bqk, bct], writes=[b_rtmp])
                            S.op("dve", lambda e: e.tensor_tensor(out=x1, in0=tm[0], in1=tm[1], op=ALU.subtract), reads=[b_rtmp], writes=[bqk])
                            S.op("dve", lambda e: e.tensor_tensor(out=x2, in0=tm[2], in1=tm[3], op=ALU.add), reads=[b_rtmp], writes=[bqk])
                            if not halo:
                                ot = t - 16
                                S.dma(nk_o[ot], qkf[:, 512:1024], reads=[bqk], writes=[b_nk])
                                S.dma(nv_o[ot], vf[:], reads=[bvf], writes=[b_nv])
                            if samp:
                                S.dma(qsc, qkf[0:16, 0:512], reads=[bqk], writes=[b_qsc])
                            else:
                                vp, bvp = vpad_r.next()
                                vp4 = vp[:].rearrange("p (c hh) n -> p c hh n", hh=2)
                                vf4 = vf[:].rearrange("p (c hh d) -> p c hh d", hh=2, d=64)
                                for hh in range(2):
                                    S.op("pool", lambda e, hh=hh: e.tensor_copy(out=vp4[:, :, hh, hh * 64:(hh + 1) * 64], in_=vf4[:, :, hh, :]),
                                         reads=[bvf], writes=[bvp])
                                S.dma(vsc[t * 128:(t + 1) * 128, :], vp[:].rearrange("p h n -> p (h n)"), reads=[bvp], writes=[b_vsc])
                                S.op("pool" if halo else "dve", lambda e: e.tensor_copy(out=qkb[:, h0 * 64:1024], in_=qkf[:, h0 * 64:1024]),
                                     reads=[bqk], writes=[b_qkb])
                            return
                        if part == "b":
                            if samp:
                                return
                            j0 = 4 if halo else 0
                            for j in range(j0, 8):
                                S.op("pe", lambda e, j=j: e.transpose(out=pT[:, j, :], in_=qkb[:, j * 128:(j + 1) * 128], identity=ident[:]),
                                     reads=[b_qkb, b_c], writes=[b_pT], sig=(j == 7))
                            S.op("dve", lambda e: e.tensor_copy(out=kT[:, :, t * 128:(t + 1) * 128], in_=pT[:, 4:8, :]), reads=[b_pT], writes=[b_out])
                            if not halo:
                                S.op("dve", lambda e: e.tensor_copy(out=qT[:, :, (t - 16) * 128:(t - 15) * 128], in_=pT[:, 0:4, :]),
                                     reads=[b_pT], writes=[b_out])
                            return
                        if halo:
                            return
                        ws_i = 1 if samp else 0
                        if part == "c":
                            S.op("dve", lambda e: e.tensor_scalar(out=vnf[:], in0=vcf[:], scalar1=bnag[:, 0:1], scalar2=st2[:, 4:5],
                                                                 op0=ALU.subtract, op1=ALU.mult), reads=[b_vcf, b_st2], writes=[b_vnf])
                            S.op("pool", lambda e: e.tensor_tensor(out=vnf[:], in0=vnf[:], in1=lng[:, 0:512], op=ALU.mult), reads=[b_vnf, b_k1], writes=[b_vnf])
                            if samp:
                                S.op("pool", lambda e: e.tensor_tensor(out=vnf[:], in0=vnf[:], in1=lng[:, 512:1024], op=ALU.add), reads=[b_vnf, b_k1], writes=[b_vnf])
                                S.op("pool", lambda e: e.tensor_copy(out=vnb[:], in_=vnf[:]), reads=[b_vnf], writes=[b_vnb])
                                S.dma(nvc_o, vnf[:], reads=[b_vnf], writes=[b_out])
                            else:
                                S.op("pool", lambda e: e.tensor_tensor(out=vnb[:], in0=vnf[:], in1=lng[:, 512:1024], op=ALU.add), reads=[b_vnf, b_k1], writes=[b_vnb])
                            return
                        if part == "d":
                            for g in range(8):
                                S.op("pe", lambda e, g=g: e.matmul(pmix[:, g * 64:(g + 1) * 64], lhsT=wsb[:, ws_i, g * 128:(g + 1) * 128],
                                                                   rhs=vnb[:, g * 64:(g + 1) * 64], start=True, stop=True),
                                     reads=[b_vnb, b_k1], writes=[b_pmix], sig=(g == 7))
                            S.op("dve", lambda e: e.tensor_tensor(out=gtmp[:].rearrange("p (g c) -> p g c", g=8),
                                                                 in0=pmix[:].rearrange("p (g c) -> p g c", g=8),
                                                                 in1=bsb[:, ws_i, :].unsqueeze(2).broadcast_to([128, 8, 64]), op=ALU.add),
                                 reads=[b_pmix, b_k1], writes=[b_gtmp])
                            S.op("dve", lambda e: e.tensor_tensor(out=gb[:], in0=gtmp[:], in1=ub[:], op=ALU.mult), reads=[b_gtmp, b_ub], writes=[b_gb])
                            return
                        if part == "e":
                            for j in range(4):
                                S.op("pe", lambda e, j=j: e.transpose(out=pT[:, j, :], in_=gb[:, j * 128:(j + 1) * 128], identity=ident[:]),
                                     reads=[b_gb, b_c], writes=[b_pT], sig=(j == 3))
                            if samp:
                                S.op("dve", lambda e: e.tensor_copy(out=mixTs[:, 4:8, :], in_=pT[:, 0:4, :]), reads=[b_pT], writes=[b_out])
                            else:
                                S.op("dve", lambda e: e.tensor_copy(out=mixT[:, 4:8, (t - 16) * 128:(t - 15) * 128], in_=pT[:, 0:4, :]),
                                     reads=[b_pT], writes=[b_out])

                    state = {"first": True}

                    sctx = {}
                    kv_b2 = {}

                    def sample_unit(bt, part):
                        b, tt = bt // 4, bt % 4
                        specs = []
                        for g, d in enumerate((1, 4, 16, 0)):
                            specs.append((g, d, 1 if d == 0 else 128))
                        if part == "kdma":
                            qb, bqb = qbc_r.next()
                            S.dma(qb[:], qsc[bt:bt + 1, :].partition_broadcast(128), reads=[b_qsc], writes=[bqb])
                            kts = []
                            for (g, d, np_) in specs:
                                ktile, bkt = kt_r.next()
                                if d == 0:
                                    S.dma(ktile[0:1, :], nk_o[16, bt:bt + 1, :], reads=[b_nk], writes=[bkt])
                                else:
                                    r0 = 2048 + tt - 128 * d
                                    if d == 1 and tt > 0:
                                        nc_ = 128 - tt
                                        S.dma(ktile[0:nc_, :], ck[b, r0:2048, :], writes=[bkt])
                                        S.dma(ktile[nc_:128, :], nk_o[16, 4 * b:4 * b + tt, :], reads=[b_nk], writes=[kv_b2.setdefault(id(bkt), Buf())])
                                    else:
                                        S.dma(ktile[:], ck[b, r0:r0 + 127 * d + 1:d, :], writes=[bkt])
                                kts.append((ktile, bkt))
                            sctx[bt] = dict(qb=qb, bqb=bqb, kts=kts)
                            return
                        c_ = sctx[bt]
                        if part == "vdma":
                            vts = []
                            for (g, d, np_) in specs:
                                vtile, bvt = vt_r.next()
                                if d == 0:
                                    S.dma(vtile[0:1, :], nv_o[16, bt:bt + 1, :], reads=[b_nv], writes=[bvt])
                                else:
                                    r0 = 2048 + tt - 128 * d
                                    if d == 1 and tt > 0:
                                        nc_ = 128 - tt
                                        S.dma(vtile[0:nc_, :], cv[b, r0:2048, :], writes=[bvt])
                                        S.dma(vtile[nc_:128, :], nv_o[16, 4 * b:4 * b + tt, :], reads=[b_nv], writes=[kv_b2.setdefault(id(bvt), Buf())])
                                    else:
                                        S.dma(vtile[:], cv[b, r0:r0 + 127 * d + 1:d, :], writes=[bvt])
                                vts.append((vtile, bvt))
                            c_["vts"] = vts
                            return
                        if part == "s1":
                            qb, bqb = c_["qb"], c_["bqb"]
                            sc, bsc = ssc_r.next()
                            pe_, bpe = pex_r.next()

                            def score(g, np_, ktile, bkt):
                                S.op("dve", lambda e: e.tensor_tensor(out=prod[0:np_, :], in0=ktile[0:np_, :], in1=qb[0:np_, :], op=ALU.mult),
                                     reads=[bkt, kv_b2.setdefault(id(bkt), Buf()), bqb], writes=[b_prod])
                                S.op("dve", lambda e: e.tensor_reduce(out=sc[0:np_, g * 8:(g + 1) * 8], in_=prod[0:np_, :].rearrange("p (h d) -> p h d", h=8),
                                                                     axis=AX.X, op=ALU.add), reads=[b_prod], writes=[bsc])
                            for (g, d, np_), (ktile, bkt) in zip(specs, c_["kts"]):
                                score(g, np_, ktile, bkt)
                            S.op("act", lambda e: e.activation(out=pe_[:], in_=sc[:], func=AF.Exp, scale=0.125), reads=[bsc], writes=[bpe])
                            S.op("dve", lambda e: e.tensor_scalar(out=pe_[0:1, 24:32], in0=pe_[0:1, 24:32], scalar1=3.0, scalar2=None, op0=ALU.mult),
                                 reads=[bpe], writes=[bpe])
                            c_["pe"], c_["bpe"] = pe_, bpe
                            return
                        pe_, bpe = c_["pe"], c_["bpe"]

                        def pv(g, np_, vtile, bvt):
                            S.op("pe", lambda e: e.matmul(pacc[:], lhsT=pe_[0:np_, g * 8:(g + 1) * 8], rhs=vtile[0:np_, :], start=(g == 0), stop=(g == 3)),
                                 reads=[bpe, bvt, kv_b2.setdefault(id(bvt), Buf())], writes=[b_pacc])
                            S.op("pe", lambda e: e.matmul(pden[:, 0:1], lhsT=pe_[0:np_, g * 8:(g + 1) * 8], rhs=onesc[0:np_, :], start=(g == 0), stop=(g == 3)),
                                 reads=[bpe, b_k1], writes=[b_pden])
                        for (g, d, np_), (vtile, bvt) in zip(specs, c_["vts"]):
                            pv(g, np_, vtile, bvt)
                        S.op("dve", lambda e: e.reciprocal(out=rden[:], in_=pden[:, 0:1]), reads=[b_pden], writes=[b_rden])
                        S.op("dve", lambda e: e.scalar_tensor_tensor(out=msk8[:], in0=pacc[:], scalar=rden[:], in1=dgm[:],
                                                                    op0=ALU.mult, op1=ALU.mult),
                             reads=[b_pacc, b_rden, b_k1], writes=[b_msk8])
                        S.op("pe", lambda e: e.matmul(prow[:], lhsT=sel[:, bt * 16:(bt + 1) * 16], rhs=msk8[:],
                                                      start=(bt == 0), stop=(bt == 15)),
                             reads=[b_msk8, b_k1], writes=[b_prow])
                        sctx.pop(bt)

                    def sample_finish():
                        S.op("dve", lambda e: e.tensor_copy(out=attb[0:16, :], in_=prow[:]), reads=[b_prow], writes=[b_attb])
                        for j in range(4):
                            S.op("pe", lambda e, j=j: e.transpose(out=pT[:, j, :], in_=attb[:, j * 128:(j + 1) * 128], identity=ident[:]),
                                 reads=[b_attb, b_c], writes=[b_pT], sig=(j == 3))
                        S.op("act", lambda e: e.activation(out=mixTs[:, 0:4, :], in_=pT[:, 0:4, :], func=AF.Copy), reads=[b_pT], writes=[b_out])

                    NO = len(order)
                    for k in range(3):
                        issue_load(order[k])
                    for k in range(3):
                        f1a(order[k], "act")
                    f1a(order[0], "recip"); f1a(order[1], "recip")
                    f1b(order[0], "cast"); f1b(order[0], "T"); f2(order[0])
                    f1b(order[1], "cast"); f1b(order[1], "T")
                    lna(order[0], "stats")
                    f1b(order[2], "cast")
                    issue_load(order[3])
                    for i, t in enumerate(order):
                        if i + 4 < NO:
                            issue_load(order[i + 4])
                        if 1 <= i <= 16:
                            sample_unit(i - 1, "kdma")
                        if i + 2 < NO:
                            f1b(order[i + 2], "T")
                        if i >= 1:
                            tileA_back(order[i - 1], "e")
                            ctxs.pop(order[i - 1])
                        if i + 1 < NO:
                            f2(order[i + 1])
                        if i + 2 < NO:
                            f1a(order[i + 2], "recip")
                        tileA_back(t, "a")
                        lna(t, "recip")
                        tileA_back(t, "c")
                        tileA_back(t, "b")
                        tileA_back(t, "d")
                        if 2 <= i <= 17:
                            sample_unit(i - 2, "s2")
                        if 1 <= i <= 16:
                            sample_unit(i - 1, "vdma")
                        wq_issue(2)
                        if i + 3 < NO:
                            f1a(order[i + 3], "act")
                            f1b(order[i + 3], "cast")
                        if 1 <= i <= 16:
                            sample_unit(i - 1, "s1")
                        if i + 1 < NO:
                            lna(order[i + 1], "stats")
                        if i == 17:
                            sample_finish()
                    tileA_back(order[-1], "e")
                    ctxs.pop(order[-1])
                    wq_issue(100)
                    S.barrier()
                    if KSTOP <= 1:
                        S.emit(); raise _Stop(nc)

                with ExitStack() as s2:
                    mk_f = sbt(s2, "mk_f", [128, 2, 512], F32)
                    mk = sbt(s2, "mk", [128, 2, 512], BF16)
                    onp_f = sbt(s2, "onp_f", [128, 256], F32)
                    onp = sbt(s2, "onp", [128, 2, 128], BF16)
                    accs = [(sbt(s2, f"accN{i}", [128, SH], F32), sbt(s2, f"accD{i}", [128, SH], F32), Buf(), Buf()) for i in range(2)]
                    PT_r = Ring([sbt(s2, f"PT{i}", [128, 512], BF16) for i in range(4)])
                    V_r = Ring([sbt(s2, f"Vt{i}", [128, 2, 2, 128], BF16) for i in range(6)])
                    ST_r = Ring([pst(s2, f"ST{i}", [128, 2, 512]) for i in range(3)])
                    NUM_r = Ring([pst(s2, f"NUM{i}", [128, 512]) for i in range(1)])
                    DEN_r = Ring([pst(s2, f"DEN{i}", [128, 512]) for i in range(1)])
                    b_k2 = Buf()
                    b_k2a = Buf(); b_k2b = Buf()
                    S.dma(mk_f[:], amask.rearrange("t p n -> p t n"), writes=[b_k2a])
                    S.dma(onp_f[:], onesp, writes=[b_k2b])
                    S.op("dve", lambda e: e.tensor_copy(out=mk[:], in_=mk_f[:]), reads=[b_k2a], writes=[b_k2])
                    S.op("dve", lambda e: e.tensor_copy(out=onp[:].rearrange("p h n -> p (h n)"), in_=onp_f[:]), reads=[b_k2b, b_k2], writes=[b_k2])

                    def blocks_of(d, G):
                        if d == 1:
                            return [128 * (4 * G + j) for j in range(4)]
                        if d == 4:
                            return [512 * G + r for r in range(4)]
                        return [4 * G + r for r in range(4)]

                    def acc_view(acc, d, G):
                        if d == 1:
                            return acc[:, 512 * G:512 * (G + 1)].rearrange("p (j i) -> p j i", j=4)
                        if d == 4:
                            return acc[:, 512 * G:512 * (G + 1)].rearrange("p (i r) -> p r i", r=4)
                        return acc[:].rearrange("p (i r) -> p r i", r=16)[:, 4 * G:4 * G + 4, :]

                    DILS = tuple(int(v) for v in os.environ.get("KDILS", "1,4,16").split(","))
                    blks = []
                    for c in range(4):
                        for di, d in enumerate(DILS):
                            for G in range(4):
                                for j, q0 in enumerate(blocks_of(d, G)):
                                    blks.append(dict(c=c, di=di, d=d, G=G, j=j, q0=q0))
                    nblk = len(blks)

                    v_b2 = {}

                    def att_vload(B):
                        d, c = B["d"], B["c"]
                        kc0 = 2048 + B["q0"]
                        kp0 = kc0 - 128 * d
                        Vt, bV = V_r.next()
                        bV2 = v_b2.setdefault(id(bV), Buf())
                        S.dma(Vt[:].rearrange("p kt h n -> p kt (h n)"),
                              vsc[kp0:kp0 + 255 * d + 1:d, 256 * c:256 * (c + 1)].rearrange("(kt p) n -> p kt n", p=128), writes=[bV])
                        B["Vt"], B["bV"], B["bV2"] = Vt, bV, bV2

                    def att_front(B, idx):
                        d, c, q0 = B["d"], B["c"], B["q0"]
                        kc0 = 2048 + q0
                        kp0 = kc0 - 128 * d
                        STp, bS = ST_r.next()
                        n = 0
                        for hh in range(2):
                            for kt, k0 in enumerate((kp0, kc0)):
                                n += 1
                                S.op("pe", lambda e, hh=hh, kt=kt, k0=k0: e.matmul(
                                    STp[:, hh, kt * 128:(kt + 1) * 128],
                                    lhsT=kT[hh * 64:(hh + 1) * 64, c, k0:k0 + 127 * d + 1:d],
                                    rhs=qT[hh * 64:(hh + 1) * 64, c, q0:q0 + 127 * d + 1:d], start=True, stop=True),
                                    writes=[bS], sig=(n == 4))
                        PT, bP = PT_r.next()
                        S.op("act", lambda e: e.activation(out=PT[:].rearrange("p (h x) -> p h x", h=2), in_=STp[:, :, 0:256], func=AF.Exp, scale=0.125),
                             reads=[bS], writes=[bP])
                        mi = 1 if kp0 < 2048 else 0
                        S.op("dve" if idx % 4 != 3 else "pool", lambda e: e.tensor_tensor(out=PT[:], in0=PT[:], in1=mk[:, mi, :], op=ALU.mult),
                             reads=[bP, b_k2], writes=[bP])
                        B["PT"], B["bP"] = PT, bP

                    cur = {}

                    def att_back(B):
                        c, di, d, G, j = B["c"], B["di"], B["d"], B["G"], B["j"]
                        PT, bP, Vt, bV, bV2 = B["PT"], B["bP"], B["Vt"], B["bV"], B["bV2"]
                        if j == 0:
                            cur["NUM"], cur["bN"] = NUM_r.next()
                            cur["DEN"], cur["bD"] = DEN_r.next()
                        NUM, bN, DEN, bD = cur["NUM"], cur["bN"], cur["DEN"], cur["bD"]
                        n = 0
                        for hh in range(2):
                            for kt in range(2):
                                n += 1
                                S.op("pe", lambda e, hh=hh, kt=kt, n=n: e.matmul(
                                    NUM[:, j * 128:(j + 1) * 128], lhsT=Vt[:, kt, hh, :],
                                    rhs=PT[:, (hh * 2 + kt) * 128:(hh * 2 + kt + 1) * 128], start=(n == 1), stop=(n == 4)),
                                    reads=[bP, bV, bV2], writes=[bN], sig=False)
                        n = 0
                        for hh in range(2):
                            for kt in range(2):
                                n += 1
                                S.op("pe", lambda e, hh=hh, kt=kt, n=n: e.matmul(
                                    DEN[:, j * 128:(j + 1) * 128], lhsT=onp[:, hh, :],
                                    rhs=PT[:, (hh * 2 + kt) * 128:(hh * 2 + kt + 1) * 128], start=(n == 1), stop=(n == 4)),
                                    reads=[bP, bV, bV2, b_k2], writes=[bN, bD], sig=(n == 4))
                        if j != 3:
                            return
                        aN, aD, baN, baD = accs[c % 2]
                        nv_ = acc_view(aN, d, G)
                        dv_ = acc_view(aD, d, G)
                        N3 = NUM[:].rearrange("p (j i) -> p j i", j=4)
                        D3 = DEN[:].rearrange("p (j i) -> p j i", j=4)
                        if di == 0:
                            S.op("act", lambda e: e.activation(out=nv_, in_=N3, func=AF.Copy), reads=[bN], writes=[baN])
                            S.op("dve", lambda e: e.tensor_copy(out=dv_, in_=D3), reads=[bD], writes=[baD])
                        else:
                            S.op("dve", lambda e: e.tensor_tensor(out=nv_, in0=N3, in1=nv_, op=ALU.add), reads=[bN, baN], writes=[baN])
                            S.op("dve", lambda e: e.tensor_tensor(out=dv_, in0=D3, in1=dv_, op=ALU.add), reads=[bD, baD], writes=[baD])
                        if di == len(DILS) - 1 and G == 3:
                            for G2 in range(4):
                                att_final(c, G2)

                    def att_final(c, G):
                        aN, aD, baN, baD = accs[c % 2]
                        sl = slice(512 * G, 512 * (G + 1))
                        S.op("dve", lambda e: e.reciprocal(out=aD[:, sl], in_=aD[:, sl]), reads=[baD], writes=[baD])
                        S.op("pool", lambda e: e.tensor_tensor(out=mixT[:, c, sl], in0=aN[:, sl], in1=aD[:, sl], op=ALU.mult),
                             reads=[baN, baD], writes=[baN, baD])

                    for k in range(min(4, nblk)):
                        att_vload(blks[k])
                    att_front(blks[0], 0)
                    if nblk > 1:
                        att_front(blks[1], 1)
                    for idx in range(nblk):
                        if idx + 4 < nblk:
                            att_vload(blks[idx + 4])
                        if idx + 2 < nblk:
                            att_front(blks[idx + 2], idx + 2)
                        att_back(blks[idx])
                    S.barrier()
                    if KSTOP <= 2:
                        S.emit(); raise _Stop(nc)

            with ExitStack() as s3:
                wdn = sbt(s3, "wdn", [128, 32, 1024], BF16)
                wo = sbt(s3, "wo", [128, 8, 1024], BF16)
                wg = sbt(s3, "wg", [128, 8, 1024], BF16)
                wp = sbt(s3, "wp", [128, 2, 1024], BF16)
                fgb = sbt(s3, "fgb", [128, 1024], F32)
                wu_r = Ring([sbt(s3, f"wu{i}", [128, 8, 512], BF16) for i in range(2)])
                xh_r = Ring([sbt(s3, f"xh{i}", [128, D], F32) for i in range(4)])
                pt_r = Ring([sbt(s3, f"ptl{i}", [128, 256], F32) for i in range(4)])
                hb = sbt(s3, "hb", [128, D], BF16); b_hb = Buf()
                junk2 = hb; b_junk2 = b_hb
                hT = sbt(s3, "hT", [128, 8, 256], BF16); b_hT = Buf()
                h2T = sbt(s3, "h2T", [128, 8, 128], BF16); b_h2T = Buf()
                actT = sbt(s3, "actT", [128, 32, 256], BF16); b_actT = Buf()
                rl_r = Ring([sbt(s3, f"rl{i}", [128, 256], F32) for i in range(2)])
                gate_r = Ring([sbt(s3, f"gate{i}", [128, 512], F32) for i in range(1)])
                pb16 = sbt(s3, "pb16", [128, 256], BF16); b_pb16 = Buf()
                pTp = sbt(s3, "pTp", [128, 2, 128], BF16); b_pTp = Buf()
                stt_all = [sbt(s3, f"stt{i}", [128, 2, 16], F32) for i in range(2)]; cur_stt = [stt_all[0]]
                pbk = Ring([pst(s3, f"pb{i}", [128, 512]) for i in range(3)])
                ps1 = Ring([pst(s3, "ps1", [128, 512])])
                pT3 = pst(s3, "pT3", [128, 8, 128], BF16); b_pT3 = Buf()
                hb2 = sbt(s3, "hb2", [128, D], BF16); b_hb2 = Buf()
                pa_r = Ring([pst(s3, f"pa{i}", [128, 512]) for i in range(2)])
                pT2 = pst(s3, "pT2", [128, 8, 128], BF16); b_pT2 = Buf()
                b_w = Buf(); b_yo = Buf()
                b_wo = Buf(); b_wg = Buf(); b_wp = Buf(); b_wd = Buf()
                S.dma(wo[:], wo_s, reads=[b_wsc], writes=[b_wo])
                S.dma(fgb[:], rowv[0:1, :].partition_broadcast(128), writes=[b_w])
                b_wdl = [Buf() for _ in range(4)]
                for j in range(4):
                    S.dma(wdn[:, 8 * j:8 * j + 8, :], wd_s[:, 8 * j:8 * j + 8, :], reads=[b_wsc], writes=[b_wdl[j]])
                S.dma(wg[:], wg_s, reads=[b_wsc], writes=[b_wg])
                S.dma(wp[:], wp_s, reads=[b_wsc], writes=[b_wp])

                def stats_and_T(b_st, xh, bxh, si, col, dstT, b_dstT, ncol_off, pTb=None, b_pTb=None, hbb=None, b_hbb=None, gcol0=0):
                    stt = cur_stt[0]
                    pTb = pT2 if pTb is None else pTb
                    b_pTb = b_pT2 if b_pTb is None else b_pTb
                    hbb = hb if hbb is None else hbb
                    b_hbb = b_hb if b_hbb is None else b_hbb
                    S.op("act", lambda e: e.activation(out=hbb[:], in_=xh[:], func=AF.Square, accum_out=stt[:, si, col:col + 1]),
                         reads=[bxh], writes=[b_hbb, b_st])
                    S.op("act", lambda e: e.activation(out=stt[:, si, col + 1:col + 2], in_=stt[:, si, col:col + 1], func=AF.Sqrt,
                                                       scale=1.0 / D, bias=EPS), reads=[b_st], writes=[b_st])
                    S.op("dve", lambda e: e.reciprocal(out=stt[:, si, col + 2:col + 3], in_=stt[:, si, col + 1:col + 2]), reads=[b_st], writes=[b_st])
                    if dstT is None:
                        return
                    S.op("act", lambda e: e.activation(out=hbb[:], in_=xh[:], func=AF.Copy), reads=[bxh], writes=[b_hbb])
                    for c in range(8):
                        S.op("pe", lambda e, c=c: e.transpose(out=pTb[:, c, :], in_=hbb[:, c * 128:(c + 1) * 128], identity=ident[:]),
                             reads=[b_hbb, b_c], writes=[b_pTb], sig=(c == 7))
                    for c in range(8):
                        S.op("dve", lambda e, c=c: e.tensor_scalar(out=dstT[:, c, ncol_off:ncol_off + 128], in0=pTb[:, c, :],
                                                                  scalar1=gn[:, gcol0 + c:gcol0 + c + 1], scalar2=None, op0=ALU.mult),
                             reads=[b_pTb, b_c], writes=[b_dstT], sig=(c == 7))

                def mm_group(out_ap, bout, pairs, reads):
                    n = len(pairs)
                    for i, (l, r) in enumerate(pairs):
                        S.op("pe", lambda e, l=l, r=r, i=i: e.matmul(out_ap, lhsT=l, rhs=r, start=(i == 0), stop=(i == n - 1)),
                             reads=reads, writes=[bout], sig=(i == n - 1))

                def stage1(b_st, xh, bxh, mc, si):
                    stt = cur_stt[0]
                    for hf in range(2):
                        pb_, bpb = ps1.next()
                        mm_group(pb_[:], bpb, [(mc[:, fc, :], wo[:, fc, hf * 512:(hf + 1) * 512]) for fc in range(8)], [b_wo])
                        xs = xh[:, hf * 512:(hf + 1) * 512]
                        S.op("dve", lambda e, pb_=pb_, xs=xs: e.tensor_tensor(out=xs, in0=pb_[:], in1=xs, op=ALU.add),
                             reads=[bpb, bxh], writes=[bxh])
                    stats_and_T(b_st, xh, bxh, si, 0, hT, b_hT, si * 128, pT3, b_pT3, hb2, b_hb2, gcol0=8)
                    S.op("dve", lambda e: e.tensor_tensor(out=stt[:, si, 3:4], in0=stt[:, si, 2:3], in1=stt[:, si, 2:3], op=ALU.mult),
                         reads=[b_st], writes=[b_st])

                wu_pending = []

                def wu_load(u):
                    wu, bwu = wu_r.next()
                    S.dma(wu[:], wu_s[u], reads=[b_wsc], writes=[bwu])
                    wu_pending.append((wu, bwu))

                def stage2_unit(u, T):
                    wu, bwu = wu_pending.pop(0)
                    for f4 in range(4):
                        ffc = 4 * u + f4
                        pa, bpa = pa_r.next()
                        mm_group(pa[:, 0:T], bpa, [(wu[:, kc, f4 * 128:(f4 + 1) * 128], hT[:, kc, 0:T]) for kc in range(8)], [bwu, b_hT])
                        rl, brl = rl_r.next()
                        S.op("act", lambda e, rl=rl, pa=pa: e.activation(out=rl[:, 0:T], in_=pa[:, 0:T], func=AF.Relu), reads=[bpa], writes=[brl])
                        S.op("pool" if ffc % 2 else "dve",
                             lambda e, rl=rl, ffc=ffc: e.tensor_tensor(out=actT[:, ffc, 0:T], in0=rl[:, 0:T], in1=rl[:, 0:T], op=ALU.mult),
                             reads=[brl], writes=[b_actT])

                def stage34(b_st, xh, bxh, ptl, bpt, si, s):
                    stt = cur_stt[0]
                    for hf in range(2):
                        pb_, bpb = pbk.next()
                        mm_group(pb_[:], bpb, [(actT[:, ffc, si * 128:(si + 1) * 128], wdn[:, ffc, hf * 512:(hf + 1) * 512]) for ffc in range(32)],
                                 [b_actT] + b_wdl)
                        xs = xh[:, hf * 512:(hf + 1) * 512]
                        S.op("dve", lambda e, pb_=pb_, xs=xs: e.scalar_tensor_tensor(out=xs, in0=pb_[:], scalar=stt[:, si, 3:4], in1=xs,
                                                                                    op0=ALU.mult, op1=ALU.add),
                             reads=[bpb, bxh, b_st], writes=[bxh])
                    stats_and_T(b_st, xh, bxh, si, 4, h2T, b_h2T, 0, gcol0=16)
                    S.op("pool", lambda e: e.tensor_copy(out=pb16[:], in_=ptl[:]), reads=[bpt], writes=[b_pb16])
                    for c in range(2):
                        S.op("pe", lambda e, c=c: e.transpose(out=pT2[:, c, :], in_=pb16[:, c * 128:(c + 1) * 128], identity=ident[:]),
                             reads=[b_pb16, b_c], writes=[b_pT2], sig=(c == 1))
                    S.op("dve", lambda e: e.tensor_copy(out=pTp[:], in_=pT2[:, 0:2, :]), reads=[b_pT2], writes=[b_pTp])
                    for hf in range(2):
                        pg, bpg = pbk.next()
                        mm_group(pg[:], bpg, [(h2T[:, kc, :], wg[:, kc, hf * 512:(hf + 1) * 512]) for kc in range(8)], [b_h2T, b_wg])
                        gt, bgt = gate_r.next()
                        S.op("act", lambda e, gt=gt, pg=pg: e.activation(out=gt[:], in_=pg[:], func=AF.Sigmoid, scale=stt[:, si, 6:7]),
                             reads=[bpg, b_st], writes=[bgt])
                        pp, bpp = pbk.next()
                        mm_group(pp[:], bpp, [(pTp[:, kc, :], wp[:, kc, hf * 512:(hf + 1) * 512]) for kc in range(2)], [b_pTp, b_wp])
                        xs = xh[:, hf * 512:(hf + 1) * 512]
                        S.op("dve", lambda e, gt=gt, pp=pp: e.tensor_tensor(out=gt[:], in0=pp[:], in1=gt[:], op=ALU.mult), reads=[bpp, bgt], writes=[bgt])
                        S.op("dve", lambda e, gt=gt, xs=xs: e.tensor_tensor(out=xs, in0=gt[:], in1=xs, op=ALU.add), reads=[bgt, bxh], writes=[bxh])
                    stats_and_T(b_st, xh, bxh, si, 8, None, None, 0)
                    S.op("dve", lambda e: e.scalar_tensor_tensor(out=xh[:], in0=xh[:], scalar=stt[:, si, 10:11], in1=fgb[:], op0=ALU.mult, op1=ALU.mult),
                         reads=[bxh, b_st, b_w], writes=[bxh])
                    S.dma(y_o[s], xh[:], reads=[bxh], writes=[Buf()], q="pool")

                passes = [([2 * i, 2 * i + 1], False) for i in range(8)] + [([16], True)]
                pinfo = {}

                def pass_loads(p):
                    subs, samp = passes[p]
                    tiles = []
                    for si, s in enumerate(subs):
                        xh, bxh = xh_r.next()
                        ptl, bpt = pt_r.next()
                        S.dma(xh[:], xall[32 if samp else 16 + s], writes=[bxh])
                        S.dma(ptl[:], pall[s], writes=[bpt])
                        tiles.append((xh, bxh, ptl, bpt))
                    pinfo[p] = (tiles, Buf(), stt_all[p % 2])

                def pass_stage1(p):
                    subs, samp = passes[p]
                    tiles, b_st, sttp = pinfo[p]
                    cur_stt[0] = sttp
                    for si, s in enumerate(subs):
                        xh, bxh, ptl, bpt = tiles[si]
                        mc = mixTs[:, :, :] if samp else mixT[:, :, s * 128:(s + 1) * 128]
                        stage1(b_st, xh, bxh, mc, si)

                def pass_stage34(p):
                    subs, samp = passes[p]
                    tiles, b_st, sttp = pinfo[p]
                    cur_stt[0] = sttp
                    for si, s in enumerate(subs):
                        xh, bxh, ptl, bpt = tiles[si]
                        stage34(b_st, xh, bxh, ptl, bpt, si, s)

                pass_loads(0)
                wu_load(0); wu_load(1)
                pass_stage1(0)
                for p in range(len(passes)):
                    subs, samp = passes[p]
                    T = 128 * len(subs)
                    if p + 1 < len(passes):
                        pass_loads(p + 1)
                    for u in range(8):
                        stage2_unit(u, T)
                        g_next = p * 8 + u + 2
                        if g_next < 8 * len(passes):
                            wu_load(g_next % 8)
                    streams = [S.capture(pass_stage34, p)]
                    if p + 1 < len(passes):
                        streams.append(S.capture(pass_stage1, p + 1))
                    S.replay(streams)
                S.barrier()
        except ZeroDivisionError:
            pass
        S.emit()
    return nc


_PROGRAM = None


def _rope_tables(pos):
    pos = np.asarray(pos, dtype=np.float32)
    inv = (np.float32(500000.0) ** (-np.arange(0, 16, 2, dtype=np.float32) / np.float32(16))).astype(np.float32)
    ang = (pos[:, None] * inv[None, :]).astype(np.float32)
    return np.concatenate([np.cos(ang).astype(np.float32), np.sin(ang).astype(np.float32)], axis=1)


def kernel(x_prompt, x_sample, cache_k, cache_v, p_prompt, p_sample,
           norm1_g, w_in, ln_v_g, ln_v_b, w_spatial, b_spatial, w_out,
           norm2_g, w_up, w_down, gate_norm_g, w_gate, w_ple, final_g):
    global _PROGRAM
    f32 = np.float32
    x_prompt = np.asarray(x_prompt, f32); x_sample = np.asarray(x_sample, f32)
    cache_k = np.asarray(cache_k, f32); cache_v = np.asarray(cache_v, f32)
    p_prompt = np.asarray(p_prompt, f32); p_sample = np.asarray(p_sample, f32)
    xp = x_prompt[0]
    pp = p_prompt[0, 0]
    def gl(g):
        return np.ascontiguousarray(np.asarray(g, f32).reshape(8, 128).T)
    gains = np.concatenate([gl(norm1_g[0]), gl(norm2_g[0]), gl(gate_norm_g[0])], axis=1)
    rowv = np.zeros((3, 1024), f32)
    rowv[0] = np.asarray(final_g, f32)
    rowv[1, :512] = np.asarray(ln_v_g[0], f32)
    rowv[1, 512:] = np.asarray(ln_v_b[0], f32)
    ws = np.asarray(w_spatial[0], f32)
    bs = np.asarray(b_spatial[0], f32)
    wsT = np.zeros((2, 128, 8, 128), f32)
    wsT[0] = np.transpose(ws, (2, 0, 1))
    trilm = np.zeros((2, 128, 128), f32)
    trilm[0] = np.triu(np.ones((128, 128), f32))
    bsp = np.zeros((2, 128, 8), f32)
    bsp[0] = bs.T
    for b in range(4):
        for j in range(4):
            for i in range(4):
                wsT[1, 4 * b + j, :, 4 * b + i] = ws[:, i, j]
                if j <= i:
                    trilm[1, 4 * b + j, 4 * b + i] = 1.0
        bsp[1, 4 * b:4 * b + 4, :] = bs[:, :4].T
    wsT = wsT.reshape(2, 128, 1024)
    ident = np.eye(128, dtype=f32)
    ik = np.arange(128)[:, None]; iq = np.arange(128)[None, :]
    m_prev = (ik >= iq).astype(f32); m_cur = (ik <= iq).astype(f32)
    mN = np.concatenate([m_prev, m_cur, m_prev, m_cur], axis=1)
    mH = np.concatenate([np.zeros_like(m_prev), m_cur, np.zeros_like(m_prev), m_cur], axis=1)
    onesp = np.zeros((128, 2, 128), f32)
    onesp[:, 0, :64] = 1.0; onesp[:, 1, 64:] = 1.0
    onesp = onesp.reshape(128, 256)
    diagm = np.zeros((8, 8, 64), f32)
    for h in range(8):
        diagm[h, h, :] = 1.0
    diagm = diagm.reshape(8, 512)
    selc = np.zeros((8, 16, 16), f32)
    for bt in range(16):
        selc[:, bt, bt] = 1.0
    selc = selc.reshape(8, 256)
    shared = dict(w_in=np.ascontiguousarray(w_in[0], f32), w_out=np.ascontiguousarray(w_out[0], f32),
                  w_up=np.ascontiguousarray(w_up[0], f32), w_down=np.ascontiguousarray(w_down[0], f32),
                  w_gate=np.ascontiguousarray(w_gate[0], f32), w_ple=np.ascontiguousarray(w_ple[0], f32),
                  gains=gains, rowv=rowv, wsT=wsT, trilm=trilm, bsp=bsp, ident=ident,
                  onesp=onesp, diagm=diagm, selc=selc)
    in_maps = []
    for c in range(NCORES):
        xall = np.zeros((33, 128, D), f32)
        if c > 0:
            xall[0:16] = xp[(c - 1) * SH:c * SH].reshape(16, 128, D)
        xall[16:32] = xp[c * SH:(c + 1) * SH].reshape(16, 128, D)
        xall[32, :16] = x_sample[4 * c:4 * c + 4].reshape(16, D)
        pall = np.zeros((17, 128, 256), f32)
        pall[:16] = pp[c * SH:(c + 1) * SH].reshape(16, 128, 256)
        pall[16, :16] = p_sample[0, 4 * c:4 * c + 4].reshape(16, 256)
        pos = np.zeros((33, 128), f32)
        pos[:32] = ((c - 1) * SH + np.arange(2 * SH)).reshape(32, 128)
        pos[32, :16] = 16384 + np.tile(np.arange(4), 4)
        cs = _rope_tables(pos.reshape(-1)).reshape(33, 128, 16)
        m = dict(shared)
        m.update(xall=xall, pall=pall, cs=cs,
                 ck=np.ascontiguousarray(cache_k[0, 4 * c:4 * c + 4].reshape(4, 2048, 512)),
                 cv=np.ascontiguousarray(cache_v[0, 4 * c:4 * c + 4].reshape(4, 2048, 512)),
                 amask=np.stack([mN, mN if c > 0 else mH], axis=0))
        in_maps.append(m)
    if _PROGRAM is None:
        try:
            _PROGRAM = build_program()
        except _Stop as e_:
            _PROGRAM = e_.args[0]
    res = run_bass_kernel_spmd(_PROGRAM, in_maps, core_ids=list(range(NCORES)))
    R = res.results
    y_prompt = np.concatenate([R[c]["y"][:16].reshape(SH, D) for c in range(NCORES)], 0)[None]
    y_sample = np.concatenate([R[c]["y"][16, :16].reshape(4, 4, D) for c in range(NCORES)], 0)
    nkp = R[7]["nk"][:16].reshape(1, 1, SH, 8, 64)
    nvp = R[7]["nv"][:16].reshape(1, 1, SH, 8, 64)
    nks = np.concatenate([R[c]["nk"][16, :16].reshape(4, 4, 8, 64) for c in range(NCORES)], 0)[None]
    nvs = np.concatenate([R[c]["nv"][16, :16].reshape(4, 4, 8, 64) for c in range(NCORES)], 0)[None]
    nvc = np.concatenate([R[c]["nvc"][:16].reshape(4, 4, 512) for c in range(NCORES)], 0)[None]
    return (y_prompt.astype(f32), y_sample.astype(f32), nkp.astype(f32), nvp.astype(f32),
            nks.astype(f32), nvs.astype(f32), nvc.astype(f32))
```
